# Optimizing a Trainium2 kernel written in Bass

```python
import jax, jax.numpy as jnp
from jax import lax
import numpy as np

D_MODEL = 2048
BATCH = 4
SEQ = 8192
DEPTH = 4

FOX_W = 3 * D_MODEL // 8
RWKV_W = 5 * D_MODEL // 16
RET_W = D_MODEL - FOX_W - RWKV_W
FOX_HEAD_DIM = 128
FOX_HEADS = FOX_W // FOX_HEAD_DIM
RWKV_HEAD_DIM = 64
RWKV_HEADS = RWKV_W // RWKV_HEAD_DIM
RET_HEAD_DIM = 128
RET_HEADS = RET_W // RET_HEAD_DIM
DECAY_LORA = 64
ICLR_LORA = 64
GATE_LORA = 128
D_FF = 4 * D_MODEL
BLOCK_Q = 128
RET_CHUNK = 128
ROPE_BASE = 10000.0
LN_EPS = 1e-5
RWKV_GN_EPS = 64e-5
RET_GN_EPS = 1e-5
ALPHA = (2 * DEPTH) ** 0.25
BETA = (8 * DEPTH) ** -0.25

FOX_SIZES = (FOX_W, FOX_W, FOX_W, FOX_HEADS)
RWKV_SIZES = (RWKV_W, RWKV_W, RWKV_W, DECAY_LORA, ICLR_LORA, GATE_LORA)
RET_SIZES = (RET_W, RET_W, RET_W, RET_W)
FOX_COLS = sum(FOX_SIZES)
RWKV_COLS = sum(RWKV_SIZES)
RET_COLS = sum(RET_SIZES)
P_IN = FOX_COLS + RWKV_COLS + RET_COLS

kernel_name = 'hybrid_fox_rwkv7_retention_deepnorm'


def _split(h, sizes):
    out, o = [], 0
    for s in sizes:
        out.append(h[..., o:o + s])
        o += s
    return out


def layer_norm(x, g, b):
    xf = x.astype(jnp.float32)
    mu = jnp.mean(xf, -1, keepdims=True)
    var = jnp.mean(jnp.square(xf - mu), -1, keepdims=True)
    return ((xf - mu) * lax.rsqrt(var + LN_EPS) * g + b).astype(x.dtype)


def head_norm(y, g, b, eps):
    mu = jnp.mean(y, -1, keepdims=True)
    var = jnp.mean(jnp.square(y - mu), -1, keepdims=True)
    hd = y.shape[-2:]
    return (y - mu) * lax.rsqrt(var + eps) * g.reshape(hd) + b.reshape(hd)


def rotary(x):
    half = x.shape[-1] // 2
    inv = 1.0 / (ROPE_BASE ** (jnp.arange(half, dtype=jnp.float32) / half))
    ang = jnp.arange(x.shape[1], dtype=jnp.float32)[:, None] * inv[None, :]
    cos = jnp.cos(ang)[None, :, None, :]
    sin = jnp.sin(ang)[None, :, None, :]
    x1, x2 = x[..., :half], x[..., half:]
    return jnp.concatenate([x1 * cos - x2 * sin, x1 * sin + x2 * cos], -1)


def forgetting_attention(q, k, v, f_logit):
    B, S, H, d = q.shape
    c = jnp.cumsum(jax.nn.log_sigmoid(f_logit), axis=1).transpose(0, 2, 1)
    qh = (q * d ** -0.5).transpose(0, 2, 1, 3)
    kh = k.transpose(0, 2, 1, 3)
    vh = v.transpose(0, 2, 1, 3)
    kpos = jnp.arange(S)

    def one_block(i):
        start = i * BLOCK_Q
        qb = lax.dynamic_slice_in_dim(qh, start, BLOCK_Q, axis=2)
        cb = lax.dynamic_slice_in_dim(c, start, BLOCK_Q, axis=2)
        s = jnp.einsum('bhqd,bhkd->bhqk', qb, kh) + cb[..., None] - c[:, :, None, :]
        qpos = start + jnp.arange(BLOCK_Q)
        s = jnp.where(kpos[None, :] <= qpos[:, None], s, -jnp.inf)
        p = jax.nn.softmax(s, axis=-1)
        return jnp.einsum('bhqk,bhkd->bhqd', p, vh)

    out = lax.map(one_block, jnp.arange(S // BLOCK_Q))
    return out.transpose(1, 0, 3, 2, 4).reshape(B, S, H * d)


def rwkv7_time_mix(h, mu, w0, w_up, a0, a_up, g_up, k_k, k_a, r_k, gn_w, gn_b):
    B, S, _ = h.shape
    H, N = RWKV_HEADS, RWKV_HEAD_DIM
    h_prev = jnp.pad(h, ((0, 0), (1, 0), (0, 0)))[:, :-1]
    h = h + (h_prev - h) * mu
    r, k, v, wd, ad, gd = _split(h, RWKV_SIZES)
    w = -jax.nn.softplus(-(w0 + jnp.tanh(wd) @ w_up)) - 0.5
    decay = jnp.exp(-jnp.exp(w))
    a = jax.nn.sigmoid(a0 + ad @ a_up)
    g = jax.nn.sigmoid(gd) @ g_up
    hs = lambda t: t.reshape(B, S, H, N)
    kk = hs(k * k_k)
    kk = kk / jnp.maximum(jnp.sqrt(jnp.sum(kk * kk, -1, keepdims=True)), 1e-12)
    k = k * (1.0 + (a - 1.0) * k_a)
    r, k, v, a, decay = hs(r), hs(k), hs(v), hs(a), hs(decay)
    a_vec = -kk
    b_vec = kk * a

    def step(state, inp):
        r_t, w_t, k_t, v_t, a_t, b_t = inp
        sa = jnp.einsum('bhvk,bhk->bhv', state, a_t)
        state = (state * w_t[:, :, None, :] + sa[..., None] * b_t[:, :, None, :]
                 + v_t[..., None] * k_t[:, :, None, :])
        return state, jnp.einsum('bhvk,bhk->bhv', state, r_t)

    tm = lambda t: jnp.moveaxis(t, 1, 0)
    state0 = jnp.zeros((B, H, N, N), r.dtype)
    _, y = lax.scan(step, state0, (tm(r), tm(decay), tm(k), tm(v), tm(a_vec), tm(b_vec)))
    y = jnp.moveaxis(y, 0, 1)
    y = head_norm(y, gn_w, gn_b, RWKV_GN_EPS)
    y = y + jnp.sum(r * k * r_k, -1, keepdims=True) * v
    return y.reshape(B, S, H * N) * g


def retention(q, k, v, g, gn_w, gn_b):
    B, S, H, d = q.shape
    C = RET_CHUNK
    n = S // C
    log_g = jnp.log(1.0 - 2.0 ** (-5.0 - jnp.arange(H, dtype=jnp.float32)))
    pos = jnp.arange(C, dtype=jnp.float32)
    rel = pos[:, None] - pos[None, :]
    inner_decay = jnp.where(rel >= 0, jnp.exp(log_g[:, None, None] * jnp.maximum(rel, 0.0)), 0.0)
    q_decay = jnp.exp(log_g[:, None] * (pos + 1.0))[..., None]
    k_decay = jnp.exp(log_g[:, None] * (C - 1.0 - pos))[..., None]
    chunk_decay = jnp.exp(log_g * C)[:, None, None]
    q = rotary(q)
    k = rotary(k) * d ** -0.5
    ch = lambda t: t.reshape(B, n, C, H, d).transpose(1, 0, 3, 2, 4)

    def step(R, inp):
        qc, kc, vc = inp
        inner = jnp.einsum('bhqd,bhkd->bhqk', qc, kc) * inner_decay
        o = (jnp.einsum('bhqk,bhkd->bhqd', inner, vc)
             + jnp.einsum('bhqd,bhde->bhqe', qc, R) * q_decay)
        R = chunk_decay * R + jnp.einsum('bhkd,bhke->bhde', kc * k_decay, vc)
        return R, o

    R0 = jnp.zeros((B, H, d, d), q.dtype)
    _, o = lax.scan(step, R0, (ch(q), ch(k), ch(v)))
    o = o.transpose(1, 0, 3, 2, 4).reshape(B, S, H, d)
    o = head_norm(o, gn_w, gn_b, RET_GN_EPS).reshape(B, S, H * d)
    return jax.nn.silu(g) * o


def setup_inputs(seed: int = 0) -> dict:
    key = jax.random.key(seed)
    ks = jax.random.split(key, 24)
    f32 = jnp.float32
    nrm = lambda k, shape, s: jax.random.normal(k, shape, f32) * s
    uni = lambda k, shape, lo, hi: jax.random.uniform(k, shape, f32, lo, hi)
    L = DEPTH
    return {
        'x': nrm(ks[0], (BATCH, SEQ, D_MODEL), 1.0),
        'w_in': nrm(ks[1], (L, D_MODEL, P_IN), D_MODEL ** -0.5),
        'fox_forget_bias': uni(ks[2], (L, FOX_HEADS), 1.0, 4.0),
        'rwkv_mu': uni(ks[3], (L, RWKV_COLS), 0.0, 1.0),
        'rwkv_w0': uni(ks[4], (L, RWKV_W), -5.0, 0.0),
        'rwkv_w_up': nrm(ks[5], (L, DECAY_LORA, RWKV_W), 0.1),
        'rwkv_a0': nrm(ks[6], (L, RWKV_W), 0.1),
        'rwkv_a_up': nrm(ks[7], (L, ICLR_LORA, RWKV_W), ICLR_LORA ** -0.5),
        'rwkv_g_up': nrm(ks[8], (L, GATE_LORA, RWKV_W), GATE_LORA ** -0.5),
        'rwkv_k_k': 0.85 + nrm(ks[9], (L, RWKV_W), 0.05),
        'rwkv_k_a': 1.0 + nrm(ks[10], (L, RWKV_W), 0.05),
        'rwkv_r_k': nrm(ks[11], (L, RWKV_HEADS, RWKV_HEAD_DIM), 0.1),
        'rwkv_gn_w': 1.0 + nrm(ks[12], (L, RWKV_W), 0.05),
        'rwkv_gn_b': nrm(ks[13], (L, RWKV_W), 0.02),
        'ret_gn_w': 1.0 + nrm(ks[14], (L, RET_W), 0.05),
        'ret_gn_b': nrm(ks[15], (L, RET_W), 0.02),
        'w_out': nrm(ks[16], (L, D_MODEL, D_MODEL), BETA * D_MODEL ** -0.5),
        'ln1_g': 1.0 + nrm(ks[17], (L, D_MODEL), 0.05),
        'ln1_b': nrm(ks[18], (L, D_MODEL), 0.02),
        'w_up': nrm(ks[19], (L, D_MODEL, D_FF), BETA * D_MODEL ** -0.5),
        'w_down': nrm(ks[20], (L, D_FF, D_MODEL), BETA * D_FF ** -0.5),
        'ln2_g': 1.0 + nrm(ks[21], (L, D_MODEL), 0.05),
        'ln2_b': nrm(ks[22], (L, D_MODEL), 0.02),
    }


def reference(x, w_in, fox_forget_bias, rwkv_mu, rwkv_w0, rwkv_w_up, rwkv_a0, rwkv_a_up,
              rwkv_g_up, rwkv_k_k, rwkv_k_a, rwkv_r_k, rwkv_gn_w, rwkv_gn_b, ret_gn_w, ret_gn_b,
              w_out, ln1_g, ln1_b, w_up, w_down, ln2_g, ln2_b):
    B, S, _ = x.shape
    for l in range(DEPTH):
        h = jnp.einsum('bsd,dp->bsp', x, w_in[l]).astype(jnp.float32)
        h_fox, h_rwkv, h_ret = _split(h, (FOX_COLS, RWKV_COLS, RET_COLS))

        fq, fk, fv, ff = _split(h_fox, FOX_SIZES)
        fh = lambda t: t.reshape(B, S, FOX_HEADS, FOX_HEAD_DIM)
        y_fox = forgetting_attention(fh(fq), fh(fk), fh(fv), ff + fox_forget_bias[l])

        y_rwkv = rwkv7_time_mix(h_rwkv, rwkv_mu[l], rwkv_w0[l], rwkv_w_up[l], rwkv_a0[l],
                                rwkv_a_up[l], rwkv_g_up[l], rwkv_k_k[l], rwkv_k_a[l],
                                rwkv_r_k[l], rwkv_gn_w[l], rwkv_gn_b[l])

        rq, rk, rv, rg = _split(h_ret, RET_SIZES)
        rh = lambda t: t.reshape(B, S, RET_HEADS, RET_HEAD_DIM)
        y_ret = retention(rh(rq), rh(rk), rh(rv), rg, ret_gn_w[l], ret_gn_b[l])

        mix = jnp.concatenate([y_fox, y_rwkv, y_ret], axis=-1).astype(x.dtype)
        x = layer_norm(ALPHA * x + mix @ w_out[l], ln1_g[l], ln1_b[l])

        hid = jnp.square(jax.nn.relu(x @ w_up[l]))
        x = layer_norm(ALPHA * x + hid @ w_down[l], ln2_g[l], ln2_b[l])
    return x
```

```python
import numpy as np
from contextlib import ExitStack
import concourse.bass as bass
import concourse.mybir as mybir
from concourse.bass_utils import run_bass_kernel_spmd

F32 = mybir.dt.float32
BF16 = mybir.dt.bfloat16
F32R = mybir.dt.float32r
AF = mybir.ActivationFunctionType
ALU = mybir.AluOpType
AX = mybir.AxisListType

D = 2048
DEPTH = 4
FOX_W, RWKV_W, RET_W = 768, 640, 640
FH, RH, TH = 6, 10, 5
P_IN = 7046
OFF_FQ, OFF_FK, OFF_FV, OFF_FF = 0, 768, 1536, 2304
OFF_RW = 2310
OFF_RT = 4486
RWKV_COLS = 2176
DFF = 8192
ALPHA = (2 * DEPTH) ** 0.25
LN_EPS = 1e-5
EXPM05 = float(np.exp(-0.5))

EPOCH = 30000
NDMA = 40


class Buf:
    __slots__ = ("w", "r", "name")

    def __init__(self, name=""):
        self.w = None
        self.r = {}
        self.name = name


class Sched:
    ENGS = ("pe", "act", "dve", "pool", "sp")

    def __init__(self, nc, stack, n_epochs=None):
        n_epochs = n_epochs or {"pe": 26, "act": 8, "dve": 8, "pool": 1}
        self.nc = nc
        self.ops = {e: [] for e in self.ENGS}
        self.cnt = {e: 0 for e in self.ENGS}
        self.sems = {e: [stack.enter_context(nc.semaphore(f"s_{e}_{i}")) for i in range(n_epochs[e])]
                     for e in ("pe", "act", "dve", "pool")}
        self.dsem = [stack.enter_context(nc.semaphore(f"s_dma_{i}")) for i in range(NDMA)]
        self.dval = [0] * NDMA
        self.dnext = 0
        self.waited = {e: {} for e in self.ENGS}
        self.ninstr = 0

    def _need_wait(self, eng, tok):
        if tok is None:
            return None
        if tok[0] == "e":
            _, f, n = tok
            if f == eng and eng == "pe":
                return None
            key = ("e", f)
            val = n
        else:
            _, k, val = tok
            key = ("d", k)
        if self.waited[eng].get(key, 0) >= val:
            return None
        self.waited[eng][key] = val
        return tok

    def _emit_wait(self, engine, tok):
        if tok[0] == "e":
            _, f, n = tok
            ep = (n - 1) // EPOCH
            engine.wait_ge(self.sems[f][ep], n - ep * EPOCH)
        else:
            _, k, val = tok
            engine.wait_ge(self.dsem[k], val)

    def _deps(self, eng, reads, writes):
        toks = []
        for b in reads:
            toks.append(b.w)
        for b in writes:
            toks.append(b.w)
            toks.extend(b.r.values())
        toks = [t for t in toks if t is not None]
        toks.sort(key=lambda t: -t[2])
        out = []
        for t in toks:
            w = self._need_wait(eng, t)
            if w is not None:
                out.append(w)
        return out

    def _mark(self, tok, key, reads, writes):
        for b in writes:
            b.w = tok
            b.r = {}
        for b in reads:
            if b.w is not tok:
                b.r[key] = tok

    def op(self, eng, fn, reads=(), writes=()):
        waits = self._deps(eng, reads, writes)
        self.cnt[eng] += 1
        n = self.cnt[eng]
        ep = (n - 1) // EPOCH
        sem = self.sems[eng][ep]
        tok = ("e", eng, n)

        def run(engine, waits=waits, fn=fn, sem=sem):
            for w in waits:
                self._emit_wait(engine, w)
            fn(engine).then_inc(sem, 1)
        self.ops[eng].append(run)
        self._mark(tok, ("e", eng), reads, writes)
        self.ninstr += 1 + len(waits)
        return tok

    def dma(self, q, out, in_, reads=(), writes=()):
        k = self.dnext
        self.dnext = (self.dnext + 1) % NDMA
        prev = self.dval[k]
        waits = self._deps(q, reads, writes)
        if prev > 0:
            w = self._need_wait(q, ("d", k, prev))
            if w is not None:
                waits.append(w)
        self.dval[k] = prev + 16
        tok = ("d", k, prev + 16)
        sem = self.dsem[k]

        def run(engine, waits=waits, sem=sem, out=out, in_=in_):
            for w in waits:
                self._emit_wait(engine, w)
            engine.dma_start(out=out, in_=in_).then_inc(sem, 16)
        self.ops[q].append(run)
        self._mark(tok, ("d", k), reads, writes)
        self.ninstr += 1 + len(waits)
        return tok

    def barrier(self):
        toks = [("e", e, self.cnt[e]) for e in ("pe", "act", "dve", "pool") if self.cnt[e] > 0]
        toks += [("d", k, self.dval[k]) for k in range(NDMA) if self.dval[k] > 0]
        for q in self.ENGS:
            waits = []
            for t in toks:
                if t[0] == "e" and t[1] == q:
                    continue
                w = self._need_wait(q, t)
                if w is not None:
                    waits.append(w)

            def run(engine, waits=waits):
                for w in waits:
                    self._emit_wait(engine, w)
            self.ops[q].append(run)
            self.ninstr += len(waits)

    def replay(self):
        nc = self.nc
        with nc.Block() as block:
            @block.sync
            def _(e):
                for f in self.ops["sp"]:
                    f(e)

            @block.scalar
            def _(e):
                for f in self.ops["act"]:
                    f(e)

            @block.vector
            def _(e):
                for f in self.ops["dve"]:
                    f(e)

            @block.gpsimd
            def _(e):
                for f in self.ops["pool"]:
                    f(e)

            @block.tensor
            def _(e):
                for f in self.ops["pe"]:
                    f(e)


class RR:
    def __init__(self, items):
        self.items = items
        self.i = 0

    def next(self):
        it = self.items[self.i % len(self.items)]
        self.i += 1
        return it


def host_consts():
    c = {}
    c["ident"] = np.eye(128, dtype=np.float32)
    p = np.arange(128)[:, None]
    f = np.arange(128)[None, :]
    c["m_le"] = (p <= f).astype(np.float32)
    p64 = np.arange(64)[:, None]
    f64 = np.arange(64)[None, :]
    rep = lambda m: np.ascontiguousarray(np.broadcast_to(m[:, None, :], (64, 10, 64))).reshape(64, 640).astype(np.float32)
    c["m64_lt"] = rep((p64 < f64).astype(np.float32))
    c["m64_le"] = rep((p64 <= f64).astype(np.float32))
    c["m64_gt"] = rep((p64 > f64).astype(np.float32))
    c["i64"] = rep(np.eye(64, dtype=np.float32))
    sel = np.zeros((128, 128), np.float32)
    sel[127, :] = 1.0
    c["sel_last"] = sel
    return c


def ret_consts(S):
    H, C, dh = TH, 128, 128
    log_g = np.log(1.0 - 2.0 ** (-5.0 - np.arange(H, dtype=np.float64)))
    pos = np.arange(C, dtype=np.float64)
    c = {}
    rel = pos[None, :] - pos[:, None]
    dec = np.where(rel[:, None, :] >= 0, np.exp(log_g[None, :, None] * np.maximum(rel[:, None, :], 0.0)), 0.0)
    c["ret_dec"] = dec.reshape(C, H * C).astype(np.float32)
    qd = np.exp(log_g[:, None] * (pos[None, :] + 1.0))
    c["ret_qd"] = np.ascontiguousarray(np.broadcast_to(qd[None], (128, H, C))).reshape(128, H * C).astype(np.float32)
    kd = np.exp(log_g[None, :] * (C - 1.0 - pos[:, None])) * dh ** -0.5
    c["ret_kd"] = kd.astype(np.float32)
    c["ret_cd"] = np.ascontiguousarray(np.broadcast_to(np.exp(log_g * C)[None, :], (128, H))).astype(np.float32)
    half = 64
    inv = 1.0 / (10000.0 ** (np.arange(half, dtype=np.float32) / half))
    ang = np.arange(S, dtype=np.float32)[:, None] * inv[None, :]
    c["ret_cos"] = np.cos(ang).astype(np.float32)
    c["ret_sin"] = np.sin(ang).astype(np.float32)
    return c


CONST_SHAPES = {
    "ident": [128, 128], "m_le": [128, 128], "m64_lt": [64, 640], "m64_le": [64, 640],
    "m64_gt": [64, 640], "i64": [64, 640], "sel_last": [128, 128],
    "ret_dec": [128, 640], "ret_qd": [128, 640], "ret_kd": [128, 5], "ret_cd": [128, 5],
}

PARAM_SHAPES = {
    "w_in": [D, P_IN], "fox_forget_bias": [FH, 1], "rwkv_mu": [1, RWKV_COLS], "rwkv_w0": [1, 640],
    "rwkv_w_up": [64, 640], "rwkv_a0": [1, 640], "rwkv_a_up": [64, 640], "rwkv_g_up": [128, 640],
    "rwkv_k_k": [1, 640], "rwkv_k_a": [1, 640], "rwkv_r_k": [1, 640], "rwkv_gn_w": [1, 640],
    "rwkv_gn_b": [1, 640], "ret_gn_w": [1, 640], "ret_gn_b": [1, 640], "w_out": [D, D],
    "ln1_g": [1, D], "ln1_b": [1, D], "w_up": [D, DFF], "w_down": [DFF, D], "ln2_g": [1, D], "ln2_b": [1, D],
}


def build_program(S, L, dbg=None, skip=()):
    NT = S // 128
    nc = bass.Bass("TRN2", target_bir_lowering=False)
    din = lambda name, shape, dt=F32: nc.dram_tensor(name, list(shape), dt, kind="ExternalInput").ap()
    x_in = din("x", [S, D])
    prm = {k: din(k, [L] + v) for k, v in PARAM_SHAPES.items()}
    cst = {k: din(k, v) for k, v in CONST_SHAPES.items()}
    cst["ret_cos"] = din("ret_cos", [S, 64])
    cst["ret_sin"] = din("ret_sin", [S, 64])
    y_out = nc.dram_tensor("y", [S, D], F32, kind="ExternalOutput").ap()
    dscr = lambda name, shape, dt: nc.dram_tensor(name, list(shape), dt).ap()
    xres = [dscr("xresA", [S, D], F32), dscr("xresB", [S, D], F32)]
    xT = dscr("xT", [D, S], BF16)
    x1 = dscr("x1", [S, D], F32)
    x1T = dscr("x1T", [D, S], BF16)
    qkT = dscr("qkT", [2 * FOX_W, S], BF16)
    fT = dscr("fT", [FH, S], F32)
    hproj = dscr("hproj", [S, P_IN - OFF_FV], F32)
    mix = dscr("mix", [S, D], F32)
    mixT = dscr("mixT", [D, S], BF16)
    ytmp = dscr("ytmp", [S, D], F32)
    hidT = dscr("hidT", [DFF, S], BF16)
    dbg_out = {}
    if dbg:
        for name, shape in (("hproj", [S, P_IN - OFF_FV]), ("mix", [S, D]), ("x1", [S, D]), ("fT", [FH, S])):
            dbg_out[name] = nc.dram_tensor("dbg_" + name, list(shape), F32, kind="ExternalOutput").ap()

    B = {n: Buf(n) for n in ["xresA", "xresB", "xT", "x1", "x1T", "qkT", "fT", "hproj", "mix", "mixT",
                              "ytmp", "hidT", "y"]}

    with ExitStack() as st:
        Sx = Sched(nc, st)
        _id = [0]

        def sb(shape, dt, name=None):
            _id[0] += 1
            t = st.enter_context(nc.sbuf_tensor(f"{name or 't'}_{_id[0]}", list(shape), dt))
            return t, Buf(name or "t")

        PS = []
        for i in range(4):
            t = st.enter_context(nc.psum_tensor(f"ps{i}", [128, 2, 512], F32))
            PS.append((t, [Buf(f"ps{i}a"), Buf(f"ps{i}b")]))
        banks = [(PS[i][0][:, a, :], PS[i][1][a]) for i in range(4) for a in range(2)]

        ident, b_ident = sb([128, 128], F32, "ident")
        Sx.dma("sp", ident[:], cst["ident"], writes=[b_ident])
        identb, b_identb = sb([128, 128], BF16, "identb")
        Sx.op("dve", lambda e: e.tensor_copy(identb[:], ident[:]), reads=[b_ident], writes=[b_identb])

        def transpose_store(src, b_src, ncols, dstT, b_dstT, row0, tok0, ntok, trbank, stg):
            for c0 in range(0, ncols, 512):
                cw = min(512, ncols - c0)
                pb, b_pb = trbank.next()
                nblk = (cw + 127) // 128
                for bi in range(nblk):
                    bw = min(128, cw - bi * 128)
                    Sx.op("pe", lambda e, pb=pb, bi=bi, bw=bw, c0=c0: e.transpose(
                        pb[:bw, bi * 128:bi * 128 + ntok], src[:ntok, c0 + bi * 128:c0 + bi * 128 + bw], ident[:ntok, :ntok]),
                        reads=[b_src, b_ident], writes=[b_pb])
                sg, b_sg = stg.next()
                if cw % 128 == 0:
                    Sx.op("act", lambda e, pb=pb, sg=sg, nblk=nblk: e.copy(
                        sg[:, :nblk, :ntok], pb[:, :nblk * 128].rearrange("p (b t) -> p b t", t=128)[:, :, :ntok]),
                        reads=[b_pb], writes=[b_sg])
                    Sx.dma("sp", dstT[row0 + c0:row0 + c0 + cw, tok0:tok0 + ntok].rearrange("(b p) t -> p b t", p=128),
                           sg[:, :nblk, :ntok], reads=[b_sg], writes=[b_dstT])
                else:
                    for bi in range(nblk):
                        bw = min(128, cw - bi * 128)
                        Sx.op("act", lambda e, pb=pb, sg=sg, bi=bi, bw=bw: e.copy(
                            sg[:bw, bi, :ntok], pb[:bw, bi * 128:bi * 128 + ntok]), reads=[b_pb], writes=[b_sg])
                        Sx.dma("sp", dstT[row0 + c0 + bi * 128:row0 + c0 + bi * 128 + bw, tok0:tok0 + ntok],
                               sg[:bw, bi, :ntok], reads=[b_sg], writes=[b_dstT])

        def gemm_tok(aT, b_aT, K, W, n0, n1, evac, cb=None):
            KC = K // 128
            cb = cb or (1024 if KC <= 16 else 512)
            with ExitStack() as ls:
                def lsb(shape, dt, name):
                    _id[0] += 1
                    return ls.enter_context(nc.sbuf_tensor(f"{name}_{_id[0]}", list(shape), dt)), Buf(name)
                tb = min(S, 512 if KC <= 16 else 256)
                wb = RR([lsb([128, KC, cb], BF16, "gw") for _ in range(2)])
                ab = RR([lsb([128, KC, tb], BF16, "ga") for _ in range(2)])
                pbk = RR(banks[0:4])
                Wv = W.rearrange("(kc p) n -> p kc n", p=128)
                aTv = aT.rearrange("(kc p) s -> p kc s", p=128)
                for c0 in range(n0, n1, cb):
                    cw = min(cb, n1 - c0)
                    wt, b_wt = wb.next()
                    kstep = max(1, min(KC, 2048 // max(1, (cw + 511) // 512) // 128))
                    for k0 in range(0, KC, kstep):
                        Sx.dma("pool", wt[:, k0:k0 + kstep, :cw], Wv[:, k0:k0 + kstep, c0:c0 + cw], writes=[b_wt])
                    for t0 in range(0, S, tb):
                        at, b_at = ab.next()
                        ksp = max(1, KC // 4)
                        for k0 in range(0, KC, ksp):
                            Sx.dma("sp", at[:, k0:k0 + ksp, :], aTv[:, k0:k0 + ksp, t0:t0 + tb], reads=[b_aT], writes=[b_at])
                        for ts_ in range(tb // 128):
                            ti = t0 // 128 + ts_
                            for s0 in range(0, cw, 512):
                                sw = min(512, cw - s0)
                                pb, b_pb = pbk.next()
                                for kc in range(KC):
                                    Sx.op("pe", lambda e, pb=pb, at=at, wt=wt, kc=kc, s0=s0, sw=sw, ts_=ts_: e.matmul(
                                        pb[:, :sw], at[:, kc, ts_ * 128:(ts_ + 1) * 128], wt[:, kc, s0:s0 + sw], start=(kc == 0), stop=(kc == KC - 1)),
                                        reads=[b_at, b_wt], writes=[b_pb])
                                evac(ti, c0 + s0, sw, pb, b_pb)
                Sx.barrier()

        def gemm_feat(aT, b_aT, K, W, n0, n1, evac, TB=4096):
            KC = K // 128
            TB = min(TB, S)
            with ExitStack() as ls:
                def lsb(shape, dt, name):
                    _id[0] += 1
                    return ls.enter_context(nc.sbuf_tensor(f"{name}_{_id[0]}", list(shape), dt)), Buf(name)
                ablk, b_ablk = lsb([128, KC, TB], BF16, "fa")
                wb = RR([lsb([128, KC, 512], BF16, "fw") for _ in range(2)])
                pbk = RR(banks[0:4])
                Wv = W.rearrange("(kc p) n -> p kc n", p=128)
                aTv = aT.rearrange("(kc p) s -> p kc s", p=128)
                for t0 in range(0, S, TB):
                    for k0 in range(0, KC, 2):
                        Sx.dma("sp", ablk[:, k0:k0 + 2, :], aTv[:, k0:k0 + 2, t0:t0 + TB], reads=[b_aT], writes=[b_ablk])
                    for g0 in range(n0, n1, 512):
                        gw = min(512, n1 - g0)
                        wt, b_wt = wb.next()
                        for k0 in range(0, KC, 4):
                            Sx.dma("pool", wt[:, k0:k0 + 4, :gw], Wv[:, k0:k0 + 4, g0:g0 + gw], writes=[b_wt])
                        for r0 in range(g0, g0 + gw, 128):
                            rw = min(128, g0 + gw - r0)
                            for t4 in range(0, TB, 512):
                                tw = min(512, TB - t4)
                                pb, b_pb = pbk.next()
                                for kc in range(KC):
                                    Sx.op("pe", lambda e, pb=pb, wt=wt, kc=kc, rw=rw, t4=t4, tw=tw, ro=r0 - g0: e.matmul(
                                        pb[:rw, :tw], wt[:, kc, ro:ro + rw], ablk[:, kc, t4:t4 + tw], start=(kc == 0), stop=(kc == KC - 1)),
                                        reads=[b_ablk, b_wt], writes=[b_pb])
                                evac(r0, rw, t0 + t4, tw, pb, b_pb)
                Sx.barrier()

        def ln_pass(xold, b_xold, yadd, b_yadd, gam, bet, xnew, b_xnew, xnewT, b_xnewT):
            with ExitStack() as ls:
                def lsb(shape, dt, name):
                    _id[0] += 1
                    return ls.enter_context(nc.sbuf_tensor(f"{name}_{_id[0]}", list(shape), dt)), Buf(name)
                g_b, b_g = lsb([128, D], F32, "lng")
                be_b, b_be = lsb([128, D], F32, "lnb")
                Sx.dma("sp", g_b[:], gam.broadcast_to([128, D]), writes=[b_g])
                Sx.dma("sp", be_b[:], bet.broadcast_to([128, D]), writes=[b_be])
                xo = RR([lsb([128, D], F32, "lx") for _ in range(2)])
                ya = RR([lsb([128, D], F32, "ly") for _ in range(2)])
                sq = RR([lsb([128, D], F32, "lsq") for _ in range(2)])
                stt = RR([lsb([128, 8], F32, "lst") for _ in range(2)])
                trb = RR(banks[4:8])
                stg = RR([lsb([128, 4, 128], BF16, "lstg") for _ in range(3)])
                for ti in range(NT):
                    r = slice(ti * 128, (ti + 1) * 128)
                    xt, b_xt = xo.next()
                    yt, b_yt = ya.next()
                    qt, b_qt = sq.next()
                    s_, b_s = stt.next()
                    Sx.dma("sp", xt[:], xold[r, :], reads=[b_xold], writes=[b_xt])
                    Sx.dma("sp", yt[:], yadd[r, :], reads=[b_yadd], writes=[b_yt])
                    Sx.op("dve", lambda e, xt=xt, yt=yt: e.scalar_tensor_tensor(yt[:], xt[:], ALPHA, yt[:], ALU.mult, ALU.add),
                          reads=[b_xt, b_yt], writes=[b_yt])
                    Sx.op("dve", lambda e, yt=yt, s_=s_: e.reduce_sum(s_[:, 0:1], yt[:], AX.X), reads=[b_yt], writes=[b_s])
                    Sx.op("dve", lambda e, s_=s_: e.tensor_scalar(s_[:, 1:2], s_[:, 0:1], -1.0 / D, None, ALU.mult),
                          reads=[b_s], writes=[b_s])
                    Sx.op("act", lambda e, yt=yt, qt=qt, s_=s_: e.activation(qt[:], yt[:], AF.Square, bias=s_[:, 1:2], scale=1.0),
                          reads=[b_yt, b_s], writes=[b_qt])
                    Sx.op("dve", lambda e, qt=qt, s_=s_: e.reduce_sum(s_[:, 2:3], qt[:], AX.X), reads=[b_qt], writes=[b_s])
                    Sx.op("dve", lambda e, s_=s_: e.tensor_scalar(s_[:, 3:4], s_[:, 2:3], 1.0 / D, LN_EPS, ALU.mult, ALU.add),
                          reads=[b_s], writes=[b_s])
                    Sx.op("act", lambda e, s_=s_: e.activation(s_[:, 4:5], s_[:, 3:4], AF.Sqrt), reads=[b_s], writes=[b_s])
                    Sx.op("dve", lambda e, s_=s_: e.reciprocal(s_[:, 5:6], s_[:, 4:5]), reads=[b_s], writes=[b_s])
                    Sx.op("dve", lambda e, yt=yt, s_=s_: e.tensor_scalar(yt[:], yt[:], s_[:, 1:2], s_[:, 5:6], ALU.add, ALU.mult),
                          reads=[b_yt, b_s], writes=[b_yt])
                    Sx.op("dve", lambda e, yt=yt: e.tensor_tensor(yt[:], yt[:], g_b[:], ALU.mult), reads=[b_yt, b_g], writes=[b_yt])
                    Sx.op("dve", lambda e, yt=yt: e.tensor_tensor(yt[:], yt[:], be_b[:], ALU.add), reads=[b_yt, b_be], writes=[b_yt])
                    Sx.dma("sp", xnew[r, :], yt[:], reads=[b_yt], writes=[b_xnew])
                    if xnewT is not None:
                        transpose_store(yt, b_yt, D, xnewT, b_xnewT, 0, ti * 128, 128, trb, stg)
                Sx.barrier()

        def input_transpose(xsrc, b_xsrc):
            with ExitStack() as ls:
                def lsb(shape, dt, name):
                    _id[0] += 1
                    return ls.enter_context(nc.sbuf_tensor(f"{name}_{_id[0]}", list(shape), dt)), Buf(name)
                xo = RR([lsb([128, D], F32, "ix") for _ in range(2)])
                trb = RR(banks[4:8])
                stg = RR([lsb([128, 4, 128], BF16, "istg") for _ in range(3)])
                for ti in range(NT):
                    xt, b_xt = xo.next()
                    Sx.dma("sp", xt[:], xsrc[ti * 128:(ti + 1) * 128, :], reads=[b_xsrc], writes=[b_xt])
                    transpose_store(xt, b_xt, D, xT, B["xT"], 0, ti * 128, 128, trb, stg)
                Sx.barrier()

        def in_proj(l):
            W = prm["w_in"][l]
            with ExitStack() as ls:
                def lsb(shape, dt, name):
                    _id[0] += 1
                    return ls.enter_context(nc.sbuf_tensor(f"{name}_{_id[0]}", list(shape), dt)), Buf(name)
                stg = RR([lsb([128, 512], BF16, "pstg") for _ in range(3)])
                stf = RR([lsb([128, 512], F32, "pstf") for _ in range(3)])
                fb, b_fb = lsb([FH, 1], F32, "fbias")
                Sx.dma("sp", fb[:], prm["fox_forget_bias"][l], writes=[b_fb])

                def evac_qk(r0, rw, t0, tw, pb, b_pb):
                    sg, b_sg = stg.next()
                    sc = 128 ** -0.5 if r0 < FOX_W else 1.0
                    Sx.op("act", lambda e: e.activation(sg[:rw, :tw], pb[:rw, :tw], AF.Copy, scale=sc),
                          reads=[b_pb], writes=[b_sg])
                    Sx.dma("sp", qkT[r0:r0 + rw, t0:t0 + tw], sg[:rw, :tw], reads=[b_sg], writes=[B["qkT"]])
                gemm_feat(xT, B["xT"], D, W, 0, 2 * FOX_W, evac_qk)

                def evac_f(r0, rw, t0, tw, pb, b_pb):
                    sg, b_sg = stf.next()
                    Sx.op("dve", lambda e: e.tensor_scalar(sg[:rw, :tw], pb[:rw, :tw], fb[:, 0:1], None, ALU.add),
                          reads=[b_pb, b_fb], writes=[b_sg])
                    Sx.dma("sp", fT[:, t0:t0 + tw], sg[:rw, :tw], reads=[b_sg], writes=[B["fT"]])
                gemm_feat(xT, B["xT"], D, W, OFF_FF, OFF_FF + FH, evac_f)

                def evac_tok(ti, c0, cw, pb, b_pb):
                    sg, b_sg = stf.next()
                    Sx.op("act", lambda e: e.copy(sg[:, :cw], pb[:, :cw]), reads=[b_pb], writes=[b_sg])
                    Sx.dma("sp", hproj[ti * 128:(ti + 1) * 128, c0 - OFF_FV:c0 - OFF_FV + cw], sg[:, :cw],
                           reads=[b_sg], writes=[B["hproj"]])
                gemm_tok(xT, B["xT"], D, W, OFF_FV, P_IN, evac_tok)

        def fox(l):
            with ExitStack() as ls:
                def lsb(shape, dt, name):
                    _id[0] += 1
                    return ls.enter_context(nc.sbuf_tensor(f"{name}_{_id[0]}", list(shape), dt)), Buf(name)
                ls2 = ExitStack()
                def lsb2(shape, dt, name):
                    _id[0] += 1
                    return ls2.enter_context(nc.sbuf_tensor(f"{name}_{_id[0]}", list(shape), dt)), Buf(name)
                ctok, b_ctok = lsb([128, NT, FH], F32, "ctok")
                crefb, b_crefb = lsb([128, NT, FH], F32, "crefb")
                sel, b_sel = lsb([128, 128], F32, "sel")
                mle, b_mle = lsb([128, 128], BF16, "mle")
                mlef, b_mlef = lsb([128, 128], F32, "mlef")
                fa, b_fa = lsb2([FH, S], F32, "fa")
                fbuf, b_fbuf = lsb2([FH, S], F32, "fb")
                Sx.dma("sp", fa[:], fT, reads=[B["fT"]], writes=[b_fa])
                Sx.op("act", lambda e: e.activation(fa[:], fa[:], AF.Sigmoid), reads=[b_fa], writes=[b_fa])
                Sx.op("act", lambda e: e.activation(fa[:], fa[:], AF.Ln), reads=[b_fa], writes=[b_fa])
                cur, b_cur, oth, b_oth = fa, b_fa, fbuf, b_fbuf
                sh = 1
                while sh < S:
                    Sx.op("dve", lambda e, cur=cur, oth=oth, sh=sh: e.tensor_copy(oth[:, 0:sh], cur[:, 0:sh]),
                          reads=[b_cur], writes=[b_oth])
                    Sx.op("dve", lambda e, cur=cur, oth=oth, sh=sh: e.tensor_tensor(oth[:, sh:S], cur[:, sh:S], cur[:, 0:S - sh], ALU.add),
                          reads=[b_cur], writes=[b_oth])
                    cur, b_cur, oth, b_oth = oth, b_oth, cur, b_cur
                    sh *= 2
                pb, b_pb = banks[4]
                for j in range(NT):
                    Sx.op("pe", lambda e, j=j, cur=cur: e.transpose(pb[:, j * FH:(j + 1) * FH], cur[:, j * 128:(j + 1) * 128], ident[:FH, :FH]),
                          reads=[b_cur, b_ident], writes=[b_pb])
                Sx.op("dve", lambda e: e.tensor_copy(ctok[:], pb[:, :NT * FH].rearrange("p (j h) -> p j h", h=FH)),
                      reads=[b_pb], writes=[b_ctok])
                Sx.dma("sp", sel[:], cst["sel_last"], writes=[b_sel])
                pb2, b_pb2 = banks[5]
                Sx.op("pe", lambda e: e.matmul(pb2[:, :NT * FH], sel[:], ctok[:].rearrange("p j h -> p (j h)"), start=True, stop=True),
                      reads=[b_sel, b_ctok], writes=[b_pb2])
                Sx.op("dve", lambda e: e.tensor_copy(crefb[:], pb2[:, :NT * FH].rearrange("p (j h) -> p j h", h=FH)),
                      reads=[b_pb2], writes=[b_crefb])
                Sx.dma("sp", mlef[:], cst["m_le"], writes=[b_mlef])
                Sx.op("dve", lambda e: e.tensor_copy(mle[:], mlef[:]), reads=[b_mlef], writes=[b_mle])
                Sx.barrier()
                ls2.close()

                qh = RR([lsb([128, S], BF16, "qh") for _ in range(2)])
                kh = RR([lsb([128, S], BF16, "kh") for _ in range(2)])
                vh = RR([lsb([128, NT, 129], BF16, "vh") for _ in range(2)])
                bias_t = RR([lsb([128, NT, NT], F32, "fbias") for _ in range(2)])
                pT = RR([lsb([128, 128], BF16, "pT") for _ in range(4)])
                ot = RR([lsb([128, 128], F32, "ot") for _ in range(3)])
                rc = RR([lsb([128, 1], F32, "rc") for _ in range(3)])
                sbk = RR(banks[0:4])
                obk = RR(banks[6:8])
                hv = hproj.rearrange("(j p) c -> p j c", p=128)
                for h in range(FH):
                    qt, b_qt = qh.next()
                    kt, b_kt = kh.next()
                    vt, b_vt = vh.next()
                    bt, b_bt = bias_t.next()
                    Sx.dma("sp", qt[:], qkT[h * 128:(h + 1) * 128, :], reads=[B["qkT"]], writes=[b_qt])
                    Sx.dma("sp", kt[:], qkT[FOX_W + h * 128:FOX_W + (h + 1) * 128, :], reads=[B["qkT"]], writes=[b_kt])
                    Sx.op("dve", lambda e, vt=vt: e.memset(vt[:, :, 128:129], 1.0), writes=[b_vt])
                    for j0 in range(0, NT, 16):
                        j1 = min(NT, j0 + 16)
                        Sx.dma("pool", vt[:, j0:j1, 0:128], hv[:, j0:j1, h * 128:(h + 1) * 128], reads=[B["hproj"]], writes=[b_vt])
                    for j in range(NT):
                        Sx.op("dve", lambda e, j=j, bt=bt, h=h: e.tensor_scalar(
                            bt[:, j, :], crefb[:, :, h], ctok[:, j, h:h + 1], None, ALU.subtract),
                            reads=[b_crefb, b_ctok], writes=[b_bt])
                    pairs = [(i, j) for i in range(NT) for j in range(i + 1)]
                    pend = []
                    state = {}

                    def emit_qk(i, j, kt=kt, qt=qt, b_kt=b_kt, b_qt=b_qt):
                        sp_, b_sp = sbk.next()
                        Sx.op("pe", lambda e, sp_=sp_, i=i, j=j: e.matmul(
                            sp_[:, :128], kt[:, j * 128:(j + 1) * 128], qt[:, i * 128:(i + 1) * 128], start=True, stop=True),
                            reads=[b_kt, b_qt], writes=[b_sp])
                        return (i, j, sp_, b_sp)

                    def emit_rest(i, j, sp_, b_sp, bt=bt, b_bt=b_bt, vt=vt, b_vt=b_vt, h=h):
                        if j == 0:
                            state["ob"] = obk.next()
                        ob, b_ob = state["ob"]
                        p_, b_p = pT.next()
                        Sx.op("act", lambda e: e.activation(p_[:], sp_[:, :128], AF.Exp, bias=bt[:, j, i:i + 1], scale=1.0),
                              reads=[b_sp, b_bt], writes=[b_p])
                        if j == i:
                            Sx.op("dve", lambda e: e.tensor_tensor(p_[:], p_[:], mle[:], ALU.mult),
                                  reads=[b_p, b_mle], writes=[b_p])
                        Sx.op("pe", lambda e: e.matmul(ob[:, :129], p_[:], vt[:, j, :], start=(j == 0), stop=(j == i)),
                              reads=[b_p, b_vt], writes=[b_ob])
                        if j == i:
                            r_, b_r = rc.next()
                            o_, b_o = ot.next()
                            Sx.op("dve", lambda e: e.reciprocal(r_[:], ob[:, 128:129]), reads=[b_ob], writes=[b_r])
                            Sx.op("dve", lambda e: e.tensor_scalar(o_[:], ob[:, 0:128], r_[:, 0:1], None, ALU.mult),
                                  reads=[b_ob, b_r], writes=[b_o])
                            Sx.dma("sp", mix[i * 128:(i + 1) * 128, h * 128:(h + 1) * 128], o_[:], reads=[b_o], writes=[B["mix"]])
                    for (i, j) in pairs:
                        pend.append(emit_qk(i, j))
                        if len(pend) > 3:
                            emit_rest(*pend.pop(0))
                    while pend:
                        emit_rest(*pend.pop(0))
                Sx.barrier()

        def retention(l):
            C = 128
            NCH = S // C
            oq, ok_, ov, og = (OFF_RT - OFF_FV, OFF_RT - OFF_FV + 640, OFF_RT - OFF_FV + 1280, OFF_RT - OFF_FV + 1920)
            with ExitStack() as ls:
                def lsb(shape, dt, name):
                    _id[0] += 1
                    return ls.enter_context(nc.sbuf_tensor(f"{name}_{_id[0]}", list(shape), dt)), Buf(name)

                def cload(shape, src, name):
                    t, b = lsb(shape, F32, name)
                    Sx.dma("sp", t[:], src, writes=[b])
                    return t, b
                dec, b_dec = cload([128, 640], cst["ret_dec"], "rdec")
                qd, b_qd = cload([128, 640], cst["ret_qd"], "rqd")
                kd, b_kd = cload([128, 5], cst["ret_kd"], "rkd")
                cd, b_cd = cload([128, 5], cst["ret_cd"], "rcd")
                gw, b_gw = cload([128, 640], prm["ret_gn_w"][l].broadcast_to([128, 640]), "rgw")
                gb, b_gb = cload([128, 640], prm["ret_gn_b"][l].broadcast_to([128, 640]), "rgb")
                R, b_R = lsb([128, TH, 128], F32, "R")
                Sx.op("dve", lambda e: e.memset(R[:], 0.0), writes=[b_R])
                hin = RR([lsb([128, 2560], F32, "rh") for _ in range(2)])
                cs = RR([lsb([128, 128], F32, "rcs") for _ in range(2)])
                qr = RR([lsb([128, 640], F32, "rqr") for _ in range(2)])
                kr = RR([lsb([128, 640], F32, "rkr") for _ in range(2)])
                ks = RR([lsb([128, 640], F32, "rks") for _ in range(2)])
                tmp = RR([lsb([128, 640], F32, "rtmp") for _ in range(2)])
                qT_ = RR([lsb([128, 640], F32, "rqT") for _ in range(2)])
                qTd = RR([lsb([128, 640], F32, "rqTd") for _ in range(2)])
                kT_ = RR([lsb([128, 640], F32, "rkT") for _ in range(2)])
                inn = RR([lsb([128, 640], F32, "rinn") for _ in range(2)])
                o_ = RR([lsb([128, 640], F32, "ro") for _ in range(2)])
                st_ = RR([lsb([128, 4, TH], F32, "rst") for _ in range(2)])
                gs = RR([lsb([128, 640], F32, "rgs") for _ in range(2)])
                P0, P1, P2, P3 = PS[0], PS[1], PS[2], PS[3]

                def v3(ap):
                    return ap.rearrange("p (h c) -> p h c", c=128)

                def pv(Pt):
                    return Pt[0]

                for ci in range(NCH):
                    r = slice(ci * C, (ci + 1) * C)
                    ht, b_ht = hin.next()
                    Sx.dma("sp", ht[:], hproj[r, oq:oq + 2560], reads=[B["hproj"]], writes=[b_ht])
                    ct, b_ct = cs.next()
                    Sx.dma("sp", ct[:, 0:64], cst["ret_cos"][r, :], writes=[b_ct])
                    Sx.dma("sp", ct[:, 64:128], cst["ret_sin"][r, :], writes=[b_ct])
                    q_, b_q = qr.next()
                    k_, b_k = kr.next()
                    t_, b_t = tmp.next()
                    cosb = ct[:, 0:64].unsqueeze(1).broadcast_to([128, TH, 64])
                    sinb = ct[:, 64:128].unsqueeze(1).broadcast_to([128, TH, 64])
                    for (src_off, dst, b_dst) in ((0, q_, b_q), (640, k_, b_k)):
                        s3 = v3(ht[:, src_off:src_off + 640])
                        d3 = v3(dst[:])
                        t3 = v3(t_[:])
                        x1_, x2_ = s3[:, :, 0:64], s3[:, :, 64:128]
                        Sx.op("dve", lambda e, d3=d3, x1_=x1_, cosb=cosb: e.tensor_tensor(d3[:, :, 0:64], x1_, cosb, ALU.mult), reads=[b_ht, b_ct], writes=[b_dst])
                        Sx.op("dve", lambda e, t3=t3, x2_=x2_, sinb=sinb: e.tensor_tensor(t3[:, :, 0:64], x2_, sinb, ALU.mult), reads=[b_ht, b_ct], writes=[b_t])
                        Sx.op("dve", lambda e, d3=d3, t3=t3: e.tensor_tensor(d3[:, :, 0:64], d3[:, :, 0:64], t3[:, :, 0:64], ALU.subtract), reads=[b_dst, b_t], writes=[b_dst])
                        Sx.op("dve", lambda e, d3=d3, x1_=x1_, sinb=sinb: e.tensor_tensor(d3[:, :, 64:128], x1_, sinb, ALU.mult), reads=[b_ht, b_ct], writes=[b_dst])
                        Sx.op("dve", lambda e, t3=t3, x2_=x2_, cosb=cosb: e.tensor_tensor(t3[:, :, 64:128], x2_, cosb, ALU.mult), reads=[b_ht, b_ct], writes=[b_t])
                        Sx.op("dve", lambda e, d3=d3, t3=t3: e.tensor_tensor(d3[:, :, 64:128], d3[:, :, 64:128], t3[:, :, 64:128], ALU.add), reads=[b_dst, b_t], writes=[b_dst])
                    ks_, b_ks = ks.next()
                    Sx.op("dve", lambda e, ks_=ks_, k_=k_: e.tensor_tensor(v3(ks_[:]), v3(k_[:]), kd[:].unsqueeze(2).broadcast_to([128, TH, 128]), ALU.mult),
                          reads=[b_k, b_kd], writes=[b_ks])
                    qTt, b_qT = qT_.next()
                    kTt, b_kT = kT_.next()
                    qTdt, b_qTd = qTd.next()
                    for (src, b_src, dstT, b_dstT, Pt, scale) in ((q_, b_q, qTt, b_qT, P0, 1.0), (k_, b_k, kTt, b_kT, P1, 128 ** -0.5)):
                        for hh in range(TH):
                            a, c5 = (0, hh) if hh < 4 else (1, 0)
                            Sx.op("pe", lambda e, Pt=Pt, a=a, c5=c5, src=src, hh=hh: e.transpose(
                                Pt[0][:, a, c5 * 128:(c5 + 1) * 128], src[:, hh * 128:(hh + 1) * 128], ident[:]),
                                reads=[b_src, b_ident], writes=[Pt[1][a]])
                        Sx.op("act", lambda e, Pt=Pt, dstT=dstT, scale=scale: e.activation(dstT[:, 0:512], Pt[0][:, 0, :], AF.Copy, scale=scale),
                              reads=[Pt[1][0]], writes=[b_dstT])
                        Sx.op("act", lambda e, Pt=Pt, dstT=dstT, scale=scale: e.activation(dstT[:, 512:640], Pt[0][:, 1, 0:128], AF.Copy, scale=scale),
                              reads=[Pt[1][1]], writes=[b_dstT])
                    Sx.op("dve", lambda e, qTdt=qTdt, qTt=qTt: e.tensor_tensor(qTdt[:], qTt[:], qd[:], ALU.mult),
                          reads=[b_qT, b_qd], writes=[b_qTd])
                    for hh in range(TH):
                        a, c5 = (0, hh) if hh < 4 else (1, 0)
                        Sx.op("pe", lambda e, a=a, c5=c5, hh=hh, kTt=kTt, qTt=qTt: e.matmul(
                            P2[0][:, a, c5 * 128:(c5 + 1) * 128], kTt[:, hh * 128:(hh + 1) * 128], qTt[:, hh * 128:(hh + 1) * 128],
                            start=True, stop=True), reads=[b_kT, b_qT], writes=[P2[1][a]])
                    in_, b_in = inn.next()
                    Sx.op("dve", lambda e, in_=in_: e.tensor_tensor(in_[:, 0:512], P2[0][:, 0, :], dec[:, 0:512], ALU.mult),
                          reads=[P2[1][0], b_dec], writes=[b_in])
                    Sx.op("dve", lambda e, in_=in_: e.tensor_tensor(in_[:, 512:640], P2[0][:, 1, 0:128], dec[:, 512:640], ALU.mult),
                          reads=[P2[1][1], b_dec], writes=[b_in])
                    for hh in range(TH):
                        a, c5 = (0, hh) if hh < 4 else (1, 0)
                        Sx.op("pe", lambda e, a=a, c5=c5, hh=hh, in_=in_, ht=ht: e.matmul(
                            P3[0][:, a, c5 * 128:(c5 + 1) * 128], in_[:, hh * 128:(hh + 1) * 128], ht[:, 1280 + hh * 128:1280 + (hh + 1) * 128],
                            start=True, stop=False), reads=[b_in, b_ht], writes=[P3[1][a]])
                        Sx.op("pe", lambda e, a=a, c5=c5, hh=hh, qTdt=qTdt: e.matmul(
                            P3[0][:, a, c5 * 128:(c5 + 1) * 128], qTdt[:, hh * 128:(hh + 1) * 128], R[:, hh, :],
                            start=False, stop=True), reads=[b_qTd, b_R], writes=[P3[1][a]])
                    ot, b_ot = o_.next()
                    Sx.op("act", lambda e, ot=ot: e.copy(ot[:, 0:512], P3[0][:, 0, :]), reads=[P3[1][0]], writes=[b_ot])
                    Sx.op("act", lambda e, ot=ot: e.copy(ot[:, 512:640], P3[0][:, 1, 0:128]), reads=[P3[1][1]], writes=[b_ot])
                    for hh in range(TH):
                        a, c5 = (0, hh) if hh < 4 else (1, 0)
                        Sx.op("pe", lambda e, a=a, c5=c5, hh=hh, ks_=ks_, ht=ht: e.matmul(
                            P0[0][:, a, c5 * 128:(c5 + 1) * 128], ks_[:, hh * 128:(hh + 1) * 128], ht[:, 1280 + hh * 128:1280 + (hh + 1) * 128],
                            start=True, stop=True), reads=[b_ks, b_ht, b_R], writes=[P0[1][a]])
                    Sx.op("dve", lambda e: e.tensor_tensor(R[:], R[:], cd[:].unsqueeze(2).broadcast_to([128, TH, 128]), ALU.mult),
                          reads=[b_R, b_cd], writes=[b_R])
                    Sx.op("dve", lambda e: e.tensor_tensor(R[:, 0:4, :], R[:, 0:4, :], P0[0][:, 0, :].rearrange("p (h c) -> p h c", c=128), ALU.add),
                          reads=[b_R, P0[1][0]], writes=[b_R])
                    Sx.op("dve", lambda e: e.tensor_tensor(R[:, 4, :], R[:, 4, :], P0[0][:, 1, 0:128], ALU.add),
                          reads=[b_R, P0[1][1]], writes=[b_R])
                    s4, b_s4 = st_.next()
                    o3 = v3(ot[:])
                    t3 = v3(t_[:])
                    Sx.op("dve", lambda e, s4=s4, o3=o3: e.reduce_sum(s4[:, 0, :], o3, AX.X), reads=[b_ot], writes=[b_s4])
                    Sx.op("dve", lambda e, s4=s4: e.tensor_scalar(s4[:, 0, :], s4[:, 0, :], 1.0 / 128, None, ALU.mult), reads=[b_s4], writes=[b_s4])
                    Sx.op("dve", lambda e, s4=s4, o3=o3: e.tensor_tensor(o3, o3, s4[:, 0, :].unsqueeze(2).broadcast_to([128, TH, 128]), ALU.subtract),
                          reads=[b_ot, b_s4], writes=[b_ot])
                    Sx.op("dve", lambda e, o3=o3, t3=t3: e.tensor_tensor(t3, o3, o3, ALU.mult), reads=[b_ot], writes=[b_t])
                    Sx.op("dve", lambda e, s4=s4, t3=t3: e.reduce_sum(s4[:, 1, :], t3, AX.X), reads=[b_t], writes=[b_s4])
                    Sx.op("dve", lambda e, s4=s4: e.tensor_scalar(s4[:, 1, :], s4[:, 1, :], 1.0 / 128, 1e-5, ALU.mult, ALU.add), reads=[b_s4], writes=[b_s4])
                    Sx.op("act", lambda e, s4=s4: e.activation(s4[:, 2, :], s4[:, 1, :], AF.Sqrt), reads=[b_s4], writes=[b_s4])
                    Sx.op("dve", lambda e, s4=s4: e.reciprocal(s4[:, 3, :], s4[:, 2, :]), reads=[b_s4], writes=[b_s4])
                    Sx.op("dve", lambda e, s4=s4, o3=o3: e.tensor_tensor(o3, o3, s4[:, 3, :].unsqueeze(2).broadcast_to([128, TH, 128]), ALU.mult),
                          reads=[b_ot, b_s4], writes=[b_ot])
                    Sx.op("dve", lambda e, ot=ot: e.tensor_tensor(ot[:], ot[:], gw[:], ALU.mult), reads=[b_ot, b_gw], writes=[b_ot])
                    Sx.op("dve", lambda e, ot=ot: e.tensor_tensor(ot[:], ot[:], gb[:], ALU.add), reads=[b_ot, b_gb], writes=[b_ot])
                    g_, b_g = gs.next()
                    Sx.op("act", lambda e, g_=g_, ht=ht: e.activation(g_[:], ht[:, 1920:2560], AF.Silu), reads=[b_ht], writes=[b_g])
                    Sx.op("dve", lambda e, ot=ot, g_=g_: e.tensor_tensor(ot[:], ot[:], g_[:], ALU.mult), reads=[b_ot, b_g], writes=[b_ot])
                    Sx.dma("sp", mix[r, FOX_W + RWKV_W:D], ot[:], reads=[b_ot], writes=[B["mix"]])
                Sx.barrier()

        def mix_transpose():
            with ExitStack() as ls:
                def lsb(shape, dt, name):
                    _id[0] += 1
                    return ls.enter_context(nc.sbuf_tensor(f"{name}_{_id[0]}", list(shape), dt)), Buf(name)
                xo = RR([lsb([128, D], F32, "mx") for _ in range(2)])
                trb = RR(banks[4:8])
                stg = RR([lsb([128, 4, 128], BF16, "mstg") for _ in range(3)])
                for ti in range(NT):
                    xt, b_xt = xo.next()
                    Sx.dma("sp", xt[:], mix[ti * 128:(ti + 1) * 128, :], reads=[B["mix"]], writes=[b_xt])
                    transpose_store(xt, b_xt, D, mixT, B["mixT"], 0, ti * 128, 128, trb, stg)
                Sx.barrier()

        def evac_to(dst, b_dst):
            def mk(ls_stf):
                def evac(ti, c0, cw, pb, b_pb):
                    sg, b_sg = ls_stf.next()
                    Sx.op("act", lambda e: e.copy(sg[:, :cw], pb[:, :cw]), reads=[b_pb], writes=[b_sg])
                    Sx.dma("sp", dst[ti * 128:(ti + 1) * 128, c0:c0 + cw], sg[:, :cw], reads=[b_sg], writes=[b_dst])
                return evac
            return mk

        def out_proj(l):
            with ExitStack() as ls:
                def lsb(shape, dt, name):
                    _id[0] += 1
                    return ls.enter_context(nc.sbuf_tensor(f"{name}_{_id[0]}", list(shape), dt)), Buf(name)
                stf = RR([lsb([128, 512], F32, "ostf") for _ in range(3)])
                gemm_tok(mixT, B["mixT"], D, prm["w_out"][l], 0, D, evac_to(ytmp, B["ytmp"])(stf))

        def ffn(l):
            with ExitStack() as ls:
                def lsb(shape, dt, name):
                    _id[0] += 1
                    return ls.enter_context(nc.sbuf_tensor(f"{name}_{_id[0]}", list(shape), dt)), Buf(name)
                stg = RR([lsb([128, 512], BF16, "fstg") for _ in range(3)])
                stf = RR([lsb([128, 512], F32, "fstf") for _ in range(3)])

                def evac_up(r0, rw, t0, tw, pb, b_pb):
                    sf, b_sf = stf.next()
                    sg, b_sg = stg.next()
                    Sx.op("act", lambda e: e.activation(sf[:rw, :tw], pb[:rw, :tw], AF.Relu), reads=[b_pb], writes=[b_sf])
                    Sx.op("dve", lambda e: e.tensor_tensor(sg[:rw, :tw], sf[:rw, :tw], sf[:rw, :tw], ALU.mult), reads=[b_sf], writes=[b_sg])
                    Sx.dma("sp", hidT[r0:r0 + rw, t0:t0 + tw], sg[:rw, :tw], reads=[b_sg], writes=[B["hidT"]])
                gemm_feat(x1T, B["x1T"], D, prm["w_up"][l], 0, DFF, evac_up)
                gemm_tok(hidT, B["hidT"], DFF, prm["w_down"][l], 0, D, evac_to(ytmp, B["ytmp"])(stf))

        rwkv = make_rwkv(nc, Sx, S, prm, cst, hproj, mix, B, PS, ident, b_ident, _id)

        cur = x_in
        b_cur = Buf("xin")
        input_transpose(cur, b_cur)
        for l in range(L):
            if "in_proj" not in skip:
                in_proj(l)
            if "fox" not in skip:
                fox(l)
            if "rwkv" not in skip:
                rwkv(l)
            if "ret" not in skip:
                retention(l)
            if "mixT" not in skip:
                mix_transpose()
            if "out_proj" not in skip:
                out_proj(l)
            if "ln" not in skip:
                ln_pass(cur, b_cur, ytmp, B["ytmp"], prm["ln1_g"][l], prm["ln1_b"][l], x1, B["x1"], x1T, B["x1T"])
            if "ffn" not in skip:
                ffn(l)
            last = (l == L - 1)
            nxt = y_out if last else xres[l % 2]
            b_nxt = B["y"] if last else B["xresA" if l % 2 == 0 else "xresB"]
            ln_pass(x1, B["x1"], ytmp, B["ytmp"], prm["ln2_g"][l], prm["ln2_b"][l], nxt, b_nxt,
                    None if last else xT, B["xT"])
            cur, b_cur = nxt, b_nxt
        Sx.barrier()
        if dbg:
            srcs = {"hproj": hproj, "mix": mix, "x1": x1, "fT": fT}
            for name, ap in dbg_out.items():
                Sx.dma("sp", ap, srcs[name])
            Sx.barrier()
        Sx.replay()
        print("instructions:", Sx.ninstr, {e: Sx.cnt[e] for e in Sx.cnt})
    return nc


def make_rwkv(nc, Sx, S, prm, cst, hproj, mix, B, PS, ident, b_ident, _id):
    C = 64
    NCH = S // C
    off = OFF_RW - OFF_FV

    def rwkv(l):
        with ExitStack() as ls:
            def lsb(shape, dt, name):
                _id[0] += 1
                return ls.enter_context(nc.sbuf_tensor(f"{name}_{_id[0]}", list(shape), dt)), Buf(name)

            def cload(shape, src, name):
                t, b = lsb(shape, F32, name)
                Sx.dma("sp", t[:], src, writes=[b])
                return t, b
            bc = lambda name, n: prm[name][l].broadcast_to([C, n])
            mu_b, b_mu = cload([C, RWKV_COLS], bc("rwkv_mu", RWKV_COLS), "wmu")
            w0_b, b_w0 = cload([C, 640], bc("rwkv_w0", 640), "ww0")
            a0_b, b_a0 = cload([C, 640], bc("rwkv_a0", 640), "wa0")
            kk_b, b_kkb = cload([C, 640], bc("rwkv_k_k", 640), "wkk")
            ka_b, b_ka = cload([C, 640], bc("rwkv_k_a", 640), "wka")
            rk_b, b_rk = cload([C, 640], bc("rwkv_r_k", 640), "wrk")
            gw_b, b_gw = cload([C, 640], bc("rwkv_gn_w", 640), "wgw")
            gb_b, b_gb = cload([C, 640], bc("rwkv_gn_b", 640), "wgb")
            wup, b_wup = cload([64, 640], prm["rwkv_w_up"][l], "wup")
            aup, b_aup = cload([64, 640], prm["rwkv_a_up"][l], "aup")
            gup, b_gup = cload([128, 640], prm["rwkv_g_up"][l], "gup")
            m_lt, b_mlt = cload([64, 640], cst["m64_lt"], "mlt")
            m_le, b_mle = cload([64, 640], cst["m64_le"], "mle")
            m_gt, b_mgt = cload([64, 640], cst["m64_gt"], "mgt")
            i64, b_i64 = cload([64, 640], cst["i64"], "i64")
            H, b_H = lsb([64, 640], F32, "H")
            Sx.op("dve", lambda e: e.tensor_scalar(H[:].bitcast(F32R), i64[:], 0.0, None, ALU.mult), reads=[b_i64], writes=[b_H])
            _pool = {}

            def T(name, shape=(64, 640), n=1):
                if name not in _pool:
                    _pool[name] = RR([lsb(list(shape), F32, name) for _ in range(n)])
                return _pool[name].next()
            prr = RR(PS)

            def s2(t):
                return t[:, :].rearrange("p (a c) -> p a c", a=2)

            def p2(P):
                return P[0][:64, :, 0:320]

            def ph(P, h):
                return P[0][:64, h // 5, (h % 5) * 64:(h % 5 + 1) * 64]

            def hs_(t, h):
                return t[:, h * 64:(h + 1) * 64]

            def v10(t):
                return t[:, :].rearrange("p (h c) -> p h c", c=64)

            def mm10(lhs_fn, rhs_fn, reads, acc=None):
                P = prr.next()
                for h in range(10):
                    ls_ = lhs_fn(h)
                    rs_ = rhs_fn(h)
                    if not isinstance(ls_, (list, tuple)):
                        ls_, rs_ = [ls_], [rs_]
                    n = len(ls_)
                    for i in range(n):
                        Sx.op("pe", lambda e, P=P, h=h, a=ls_[i], b=rs_[i], i=i, n=n: e.matmul(
                            ph(P, h), a.bitcast(F32R), b.bitcast(F32R), start=(i == 0), stop=(i == n - 1)), reads=reads, writes=[P[1][h // 5]])
                return P

            def evac(P, name, eng="act", mask=None, b_mask=None, add=None, b_add=None, r32=True):
                t, b_t = T(name)
                o = s2(t).bitcast(F32R) if r32 else s2(t)
                if mask is not None:
                    Sx.op("dve", lambda e: e.tensor_tensor(o, p2(P), s2(mask), ALU.mult), reads=[P[1][0], P[1][1], b_mask], writes=[b_t])
                elif add is not None:
                    Sx.op("dve", lambda e: e.tensor_tensor(o, p2(P), s2(add), ALU.add), reads=[P[1][0], P[1][1], b_add], writes=[b_t])
                elif eng == "act":
                    Sx.op("act", lambda e: e.copy(o, p2(P)), reads=[P[1][0], P[1][1]], writes=[b_t])
                else:
                    Sx.op("dve", lambda e: e.tensor_copy(o, p2(P)), reads=[P[1][0], P[1][1]], writes=[b_t])
                return t, b_t

            for ci in range(NCH):
                c0 = ci * C
                h_, b_h = T("h", (C, RWKV_COLS))
                hp, b_hp = T("hp", (C, RWKV_COLS))
                Sx.dma("sp", h_[:], hproj[c0:c0 + C, off:off + RWKV_COLS], reads=[B["hproj"]], writes=[b_h])
                if ci == 0:
                    Sx.op("dve", lambda e, hp=hp: e.memset(hp[:], 0.0), writes=[b_hp])
                    Sx.dma("sp", hp[1:C, :], hproj[0:C - 1, off:off + RWKV_COLS], reads=[B["hproj"]], writes=[b_hp])
                else:
                    Sx.dma("sp", hp[:], hproj[c0 - 1:c0 + C - 1, off:off + RWKV_COLS], reads=[B["hproj"]], writes=[b_hp])
                Sx.op("dve", lambda e, hp=hp, h_=h_: e.tensor_tensor(hp[:], hp[:], h_[:], ALU.subtract), reads=[b_hp, b_h], writes=[b_hp])
                Sx.op("dve", lambda e, hp=hp: e.tensor_tensor(hp[:], hp[:], mu_b[:], ALU.mult), reads=[b_hp, b_mu], writes=[b_hp])
                Sx.op("dve", lambda e, hp=hp, h_=h_: e.tensor_tensor(h_[:].bitcast(F32R), h_[:], hp[:], ALU.add), reads=[b_hp, b_h], writes=[b_h])
                r_ = h_[:, 0:640]
                k_ = h_[:, 640:1280]
                v_ = h_[:, 1280:1920]
                vh = lambda h, h_=h_: h_[:, 1280 + h * 64:1280 + (h + 1) * 64]
                lor, b_lor = T("lor", (C, 256))
                Sx.op("act", lambda e, lor=lor, h_=h_: e.activation(lor[:, 0:64], h_[:, 1920:1984], AF.Tanh), reads=[b_h], writes=[b_lor])
                Sx.op("act", lambda e, lor=lor, h_=h_: e.activation(lor[:, 128:256], h_[:, 2048:2176], AF.Sigmoid), reads=[b_h], writes=[b_lor])
                Pl = prr.next()
                Sx.op("pe", lambda e, Pl=Pl, lor=lor: e.transpose(Pl[0][:64, 0, 0:64], lor[:, 0:64], ident[:C, :C]), reads=[b_lor, b_ident], writes=[Pl[1][0]])
                Sx.op("pe", lambda e, Pl=Pl, h_=h_: e.transpose(Pl[0][:64, 0, 64:128], h_[:, 1984:2048], ident[:C, :C]), reads=[b_h, b_ident], writes=[Pl[1][0]])
                Sx.op("pe", lambda e, Pl=Pl, lor=lor: e.transpose(Pl[0][:, 0, 128:192], lor[:, 128:256], ident[:C, :C]), reads=[b_lor, b_ident], writes=[Pl[1][0]])
                lT, b_lT = T("lT", (128, 192))
                Sx.op("act", lambda e, lT=lT, Pl=Pl: e.copy(lT[:64, 0:128], Pl[0][:64, 0, 0:128]), reads=[Pl[1][0]], writes=[b_lT])
                Sx.op("act", lambda e, lT=lT, Pl=Pl: e.copy(lT[:, 128:192], Pl[0][:, 0, 128:192]), reads=[Pl[1][0]], writes=[b_lT])

                def lora_mm(lhsT, rhsW, b_w):
                    P = prr.next()
                    for a in range(2):
                        Sx.op("pe", lambda e, P=P, a=a: e.matmul(P[0][:64, a, 0:320], lhsT, rhsW[:, a * 320:(a + 1) * 320], start=True, stop=True),
                              reads=[b_lT, b_w], writes=[P[1][a]])
                    return P
                Pz = lora_mm(lT[:64, 0:64], wup, b_wup)
                sg, b_sg = evac(Pz, "sg", add=w0_b, b_add=b_w0, r32=False)
                Sx.op("act", lambda e, sg=sg: e.activation(sg[:], sg[:], AF.Sigmoid), reads=[b_sg], writes=[b_sg])
                Pa = lora_mm(lT[:64, 64:128], aup, b_aup)
                ai, b_ai = evac(Pa, "ai", add=a0_b, b_add=b_a0, r32=False)
                Sx.op("act", lambda e, ai=ai: e.activation(ai[:], ai[:], AF.Sigmoid), reads=[b_ai], writes=[b_ai])
                Pg = lora_mm(lT[:, 128:192], gup, b_gup)
                g_, b_g = evac(Pg, "g", r32=False)
                kk, b_kk = T("kk")
                sq, b_sq = T("sq")
                st, b_st = T("st", (C, 40))
                Sx.op("dve", lambda e, kk=kk, k_=k_: e.tensor_tensor(kk[:], k_, kk_b[:], ALU.mult), reads=[b_h, b_kkb], writes=[b_kk])
                Sx.op("dve", lambda e, kk=kk, sq=sq: e.tensor_tensor(sq[:], kk[:], kk[:], ALU.mult), reads=[b_kk], writes=[b_sq])
                Sx.op("dve", lambda e, st=st, sq=sq: e.reduce_sum(st[:, 0:10], v10(sq), AX.X), reads=[b_sq], writes=[b_st])
                Sx.op("act", lambda e, st=st: e.activation(st[:, 10:20], st[:, 0:10], AF.Sqrt), reads=[b_st], writes=[b_st])
                Sx.op("dve", lambda e, st=st: e.tensor_scalar(st[:, 10:20], st[:, 10:20], 1e-12, None, ALU.max), reads=[b_st], writes=[b_st])
                Sx.op("dve", lambda e, st=st: e.reciprocal(st[:, 20:30], st[:, 10:20]), reads=[b_st], writes=[b_st])
                Sx.op("dve", lambda e, kk=kk, st=st: e.tensor_tensor(v10(kk), v10(kk), st[:, 20:30].unsqueeze(2).broadcast_to([C, 10, 64]), ALU.mult),
                      reads=[b_kk, b_st], writes=[b_kk])
                k2, b_k2 = T("k2")
                Sx.op("dve", lambda e, k2=k2, ai=ai: e.scalar_tensor_tensor(k2[:], ai[:], -1.0, ka_b[:], ALU.add, ALU.mult), reads=[b_ai, b_ka], writes=[b_k2])
                Sx.op("dve", lambda e, k2=k2, k_=k_: e.scalar_tensor_tensor(k2[:], k2[:], 1.0, k_, ALU.add, ALU.mult), reads=[b_k2, b_h], writes=[b_k2])
                bv, b_bv = T("bv")
                Sx.op("dve", lambda e, bv=bv, kk=kk, ai=ai: e.tensor_tensor(bv[:], kk[:], ai[:], ALU.mult), reads=[b_kk, b_ai], writes=[b_bv])
                Pc = prr.next()
                for a in range(2):
                    Sx.op("pe", lambda e, Pc=Pc, a=a, sg=sg: e.matmul(Pc[0][:64, a, 0:320], m_le[:, 0:64], sg[:, a * 320:(a + 1) * 320], start=True, stop=True),
                          reads=[b_mle, b_sg], writes=[Pc[1][a]])
                E1, b_E1 = T("E1")
                E2, b_E2 = T("E2")
                E3, b_E3 = T("E3")
                Sx.op("act", lambda e, E1=E1, Pc=Pc: e.activation(s2(E1), p2(Pc), AF.Exp, scale=-EXPM05), reads=[Pc[1][0], Pc[1][1]], writes=[b_E1])
                Sx.op("act", lambda e, E2=E2, Pc=Pc: e.activation(s2(E2), p2(Pc), AF.Exp, scale=EXPM05), reads=[Pc[1][0], Pc[1][1]], writes=[b_E2])
                Sx.op("dve", lambda e, E3=E3, Pc=Pc, sg=sg: e.tensor_tensor(s2(E3), p2(Pc), s2(sg), ALU.subtract), reads=[Pc[1][0], Pc[1][1], b_sg], writes=[b_E3])
                Sx.op("act", lambda e, E3=E3: e.activation(E3[:], E3[:], AF.Exp, scale=-EXPM05), reads=[b_E3], writes=[b_E3])
                rt, b_rt = T("rt")
                at, b_at = T("at")
                bt, b_bt = T("bt")
                kt, b_kt = T("kt")
                Sx.op("dve", lambda e, rt=rt, r_=r_, E1=E1: e.tensor_tensor(rt[:], r_, E1[:], ALU.mult), reads=[b_h, b_E1], writes=[b_rt])
                Sx.op("dve", lambda e, at=at, kk=kk, E3=E3: e.scalar_tensor_tensor(at[:].bitcast(F32R), kk[:], -1.0, E3[:], ALU.mult, ALU.mult), reads=[b_kk, b_E3], writes=[b_at])
                Sx.op("dve", lambda e, bt=bt, bv=bv, E2=E2: e.tensor_tensor(bt[:].bitcast(F32R), bv[:], E2[:], ALU.mult), reads=[b_bv, b_E2], writes=[b_bt])
                Sx.op("dve", lambda e, kt=kt, k2=k2, E2=E2: e.tensor_tensor(kt[:].bitcast(F32R), k2[:], E2[:], ALU.mult), reads=[b_k2, b_E2], writes=[b_kt])

                def tr10(src, b_src, name):
                    P = prr.next()
                    for h in range(10):
                        Sx.op("pe", lambda e, P=P, h=h: e.transpose(ph(P, h), hs_(src, h), ident[:C, :C]),
                              reads=[b_src, b_ident], writes=[P[1][h // 5]])
                    return evac(P, name)
                atT, b_atT = tr10(at, b_at, "atT")
                btT, b_btT = tr10(bt, b_bt, "btT")
                ktT, b_ktT = tr10(kt, b_kt, "ktT")
                rtT, b_rtT = tr10(rt, b_rt, "rtT")
                E1T, b_E1T = tr10(E1, b_E1, "E1T")
                def score(lt, b_l, rt_, b_r, mask, b_mask, name):
                    P = mm10(lambda h: hs_(lt, h), lambda h: hs_(rt_, h), [b_l, b_r])
                    return evac(P, name, mask=mask, b_mask=b_mask)
                Mt, b_Mt = score(btT, b_btT, atT, b_atT, m_lt, b_mlt, "Nt")
                M, b_M = score(atT, b_atT, btT, b_btT, m_gt, b_mgt, "Nn")
                AakT, b_AakT = score(ktT, b_ktT, atT, b_atT, m_lt, b_mlt, "AakT")
                ArbT, b_ArbT = score(btT, b_btT, rtT, b_rtT, m_le, b_mle, "ArbT")
                ArkT, b_ArkT = score(ktT, b_ktT, rtT, b_rtT, m_le, b_mle, "ArkT")
                Tt, b_Tt = T("Tt")
                Sx.op("dve", lambda e, Tt=Tt, Mt=Mt: e.tensor_tensor(Tt[:].bitcast(F32R), Mt[:], i64[:], ALU.add), reads=[b_Mt, b_i64], writes=[b_Tt])
                for j in range(5):
                    P = mm10(lambda h: hs_(Mt, h), lambda h: hs_(M, h), [b_Mt, b_M])
                    M2, b_M2 = evac(P, "M2_%d" % (j % 2), eng="act")
                    if j < 4:
                        P = mm10(lambda h: hs_(M, h), lambda h: hs_(Mt, h), [b_Mt, b_M])
                        Mt2, b_Mt2 = evac(P, "Mt2_%d" % (j % 2), eng="dve")
                    P = mm10(lambda h: hs_(M2, h), lambda h: hs_(Tt, h), [b_M2, b_Tt])
                    Sx.op("dve", lambda e, Tt=Tt, P=P: e.tensor_tensor(s2(Tt).bitcast(F32R), p2(P), s2(Tt), ALU.add), reads=[P[1][0], P[1][1], b_Tt], writes=[b_Tt])
                    M, b_M = M2, b_M2
                    if j < 4:
                        Mt, b_Mt = Mt2, b_Mt2
                P = mm10(lambda h: hs_(AakT, h), vh, [b_AakT, b_h])
                AakV, b_AakV = evac(P, "AakV")
                P = mm10(lambda h: hs_(Tt, h), lambda h: hs_(AakV, h), [b_Tt, b_AakV])
                W2, b_W2 = evac(P, "W2", eng="dve", r32=False)
                P = mm10(lambda h: hs_(at, h), lambda h: hs_(Tt, h), [b_at, b_Tt])
                W1T, b_W1T = evac(P, "W1T")
                P = mm10(lambda h: hs_(W1T, h), lambda h: hs_(H, h), [b_W1T, b_H])
                U, b_U = evac(P, "U", add=W2, b_add=b_W2)
                Py = mm10(lambda h: [hs_(rtT, h), hs_(ArbT, h), hs_(ArkT, h)], lambda h: [hs_(H, h), hs_(U, h), vh(h)],
                          [b_rtT, b_ArbT, b_ArkT, b_H, b_U, b_h])
                y_, b_y = evac(Py, "y", r32=False)
                Ph = mm10(lambda h: [hs_(bt, h), hs_(kt, h)], lambda h: [hs_(U, h), vh(h)], [b_bt, b_kt, b_U, b_h, b_H])
                Sx.op("dve", lambda e, Ph=Ph: e.tensor_tensor(s2(H).bitcast(F32R), p2(Ph), s2(H), ALU.add), reads=[Ph[1][0], Ph[1][1], b_H], writes=[b_H])
                Sx.op("dve", lambda e, E1T=E1T: e.tensor_tensor(v10(H).bitcast(F32R), v10(H), v10(E1T)[:, :, 63:64].broadcast_to([64, 10, 64]), ALU.mult),
                      reads=[b_H, b_E1T], writes=[b_H])
                Sx.op("dve", lambda e, st=st, y_=y_: e.reduce_sum(st[:, 0:10], v10(y_), AX.X), reads=[b_y], writes=[b_st])
                Sx.op("dve", lambda e, st=st: e.tensor_scalar(st[:, 0:10], st[:, 0:10], 1.0 / 64, None, ALU.mult), reads=[b_st], writes=[b_st])
                Sx.op("dve", lambda e, st=st, y_=y_: e.tensor_tensor(v10(y_), v10(y_), st[:, 0:10].unsqueeze(2).broadcast_to([C, 10, 64]), ALU.subtract),
                      reads=[b_y, b_st], writes=[b_y])
                Sx.op("dve", lambda e, sq=sq, y_=y_: e.tensor_tensor(sq[:], y_[:], y_[:], ALU.mult), reads=[b_y], writes=[b_sq])
                Sx.op("dve", lambda e, st=st, sq=sq: e.reduce_sum(st[:, 10:20], v10(sq), AX.X), reads=[b_sq], writes=[b_st])
                Sx.op("dve", lambda e, st=st: e.tensor_scalar(st[:, 10:20], st[:, 10:20], 1.0 / 64, 64e-5, ALU.mult, ALU.add), reads=[b_st], writes=[b_st])
                Sx.op("act", lambda e, st=st: e.activation(st[:, 20:30], st[:, 10:20], AF.Sqrt), reads=[b_st], writes=[b_st])
                Sx.op("dve", lambda e, st=st: e.reciprocal(st[:, 30:40], st[:, 20:30]), reads=[b_st], writes=[b_st])
                Sx.op("dve", lambda e, st=st, y_=y_: e.tensor_tensor(v10(y_), v10(y_), st[:, 30:40].unsqueeze(2).broadcast_to([C, 10, 64]), ALU.mult),
                      reads=[b_y, b_st], writes=[b_y])
                Sx.op("dve", lambda e, y_=y_: e.tensor_tensor(y_[:], y_[:], gw_b[:], ALU.mult), reads=[b_y, b_gw], writes=[b_y])
                Sx.op("dve", lambda e, y_=y_: e.tensor_tensor(y_[:], y_[:], gb_b[:], ALU.add), reads=[b_y, b_gb], writes=[b_y])
                Sx.op("dve", lambda e, sq=sq, r_=r_, k2=k2: e.tensor_tensor(sq[:], r_, k2[:], ALU.mult), reads=[b_h, b_k2], writes=[b_sq])
                Sx.op("dve", lambda e, sq=sq: e.tensor_tensor(sq[:], sq[:], rk_b[:], ALU.mult), reads=[b_sq, b_rk], writes=[b_sq])
                Sx.op("dve", lambda e, st=st, sq=sq: e.reduce_sum(st[:, 0:10], v10(sq), AX.X), reads=[b_sq], writes=[b_st])
                Sx.op("dve", lambda e, sq=sq, st=st, v_=v_: e.tensor_tensor(v10(sq), v_.rearrange("p (h c) -> p h c", c=64), st[:, 0:10].unsqueeze(2).broadcast_to([C, 10, 64]), ALU.mult),
                      reads=[b_h, b_st], writes=[b_sq])
                Sx.op("dve", lambda e, y_=y_, sq=sq: e.tensor_tensor(y_[:], y_[:], sq[:], ALU.add), reads=[b_y, b_sq], writes=[b_y])
                Sx.op("dve", lambda e, y_=y_, g_=g_: e.tensor_tensor(y_[:], y_[:], g_[:], ALU.mult), reads=[b_y, b_g], writes=[b_y])
                Sx.dma("sp", mix[c0:c0 + C, FOX_W:FOX_W + RWKV_W], y_[:], reads=[b_y], writes=[B["mix"]])
            Sx.barrier()
    return rwkv


_CACHE = {}


def _prep_params(inputs, L):
    out = {}
    for k, shp in PARAM_SHAPES.items():
        a = np.asarray(inputs[k], dtype=np.float32)[:L]
        out[k] = np.ascontiguousarray(a.reshape([L] + shp))
    return out


def kernel(**inputs):
    x = np.asarray(inputs["x"], dtype=np.float32)
    Bsz, S, _ = x.shape
    L = np.asarray(inputs["w_in"]).shape[0]
    key = (S, L)
    if key not in _CACHE:
        _CACHE[key] = build_program(S, L)
    nc = _CACHE[key]
    shared = _prep_params(inputs, L)
    shared.update(host_consts())
    shared.update(ret_consts(S))
    n = 8
    in_maps = []
    for c in range(n):
        m = dict(shared)
        m["x"] = np.ascontiguousarray(x[c % Bsz])
        in_maps.append(m)
    res = run_bass_kernel_spmd(nc, in_maps, core_ids=list(range(n)))
    return np.stack([res.results[b]["y"] for b in range(Bsz)], axis=0).astype(np.float32)
```

```python
import numpy as np
from contextlib import ExitStack
import concourse.bass as bass
import concourse.mybir as mybir
from concourse.bass_utils import run_bass_kernel_spmd

F32 = mybir.dt.float32
BF16 = mybir.dt.bfloat16
F32R = mybir.dt.float32r
AF = mybir.ActivationFunctionType
ALU = mybir.AluOpType
AX = mybir.AxisListType

D = 2048
DEPTH = 4
FOX_W, RWKV_W, RET_W = 768, 640, 640
FH, RH, TH = 6, 10, 5
P_IN = 7046
OFF_FQ, OFF_FK, OFF_FV, OFF_FF = 0, 768, 1536, 2304
OFF_RW = 2310
OFF_RT = 4486
RWKV_COLS = 2176
DFF = 8192
ALPHA = (2 * DEPTH) ** 0.25
LN_EPS = 1e-5
EXPM05 = float(np.exp(-0.5))

EPOCH = 30000
NDMA = 40


class Buf:
    __slots__ = ("w", "r", "name")

    def __init__(self, name=""):
        self.w = None
        self.r = {}
        self.name = name


class Sched:
    ENGS = ("pe", "act", "dve", "pool", "sp")

    def __init__(self, nc, stack, n_epochs=None):
        n_epochs = n_epochs or {"pe": 26, "act": 8, "dve": 8, "pool": 1}
        self.nc = nc
        self.ops = {e: [] for e in self.ENGS}
        self.cnt = {e: 0 for e in self.ENGS}
        self.sems = {e: [stack.enter_context(nc.semaphore(f"s_{e}_{i}")) for i in range(n_epochs[e])]
                     for e in ("pe", "act", "dve", "pool")}
        self.dsem = [stack.enter_context(nc.semaphore(f"s_dma_{i}")) for i in range(NDMA)]
        self.dval = [0] * NDMA
        self.dnext = 0
        self.waited = {e: {} for e in self.ENGS}
        self.ninstr = 0

    def _need_wait(self, eng, tok):
        if tok is None:
            return None
        if tok[0] == "e":
            _, f, n = tok
            if f == eng and eng == "pe":
                return None
            key = ("e", f)
            val = n
        else:
            _, k, val = tok
            key = ("d", k)
        if self.waited[eng].get(key, 0) >= val:
            return None
        self.waited[eng][key] = val
        return tok

    def _emit_wait(self, engine, tok):
        if tok[0] == "e":
            _, f, n = tok
            ep = (n - 1) // EPOCH
            engine.wait_ge(self.sems[f][ep], n - ep * EPOCH)
        else:
            _, k, val = tok
            engine.wait_ge(self.dsem[k], val)

    def _deps(self, eng, reads, writes):
        toks = []
        for b in reads:
            toks.append(b.w)
        for b in writes:
            toks.append(b.w)
            toks.extend(b.r.values())
        toks = [t for t in toks if t is not None]
        toks.sort(key=lambda t: -t[2])
        out = []
        for t in toks:
            w = self._need_wait(eng, t)
            if w is not None:
                out.append(w)
        return out

    def _mark(self, tok, key, reads, writes):
        for b in writes:
            b.w = tok
            b.r = {}
        for b in reads:
            if b.w is not tok:
                b.r[key] = tok

    def op(self, eng, fn, reads=(), writes=()):
        waits = self._deps(eng, reads, writes)
        self.cnt[eng] += 1
        n = self.cnt[eng]
        ep = (n - 1) // EPOCH
        sem = self.sems[eng][ep]
        tok = ("e", eng, n)

        def run(engine, waits=waits, fn=fn, sem=sem):
            for w in waits:
                self._emit_wait(engine, w)
            fn(engine).then_inc(sem, 1)
        self.ops[eng].append(run)
        self._mark(tok, ("e", eng), reads, writes)
        self.ninstr += 1 + len(waits)
        return tok

    def dma(self, q, out, in_, reads=(), writes=()):
        k = self.dnext
        self.dnext = (self.dnext + 1) % NDMA
        prev = self.dval[k]
        waits = self._deps(q, reads, writes)
        if prev > 0:
            w = self._need_wait(q, ("d", k, prev))
            if w is not None:
                waits.append(w)
        self.dval[k] = prev + 16
        tok = ("d", k, prev + 16)
        sem = self.dsem[k]

        def run(engine, waits=waits, sem=sem, out=out, in_=in_):
            for w in waits:
                self._emit_wait(engine, w)
            engine.dma_start(out=out, in_=in_).then_inc(sem, 16)
        self.ops[q].append(run)
        self._mark(tok, ("d", k), reads, writes)
        self.ninstr += 1 + len(waits)
        return tok

    def barrier(self):
        toks = [("e", e, self.cnt[e]) for e in ("pe", "act", "dve", "pool") if self.cnt[e] > 0]
        toks += [("d", k, self.dval[k]) for k in range(NDMA) if self.dval[k] > 0]
        for q in self.ENGS:
            waits = []
            for t in toks:
                if t[0] == "e" and t[1] == q:
                    continue
                w = self._need_wait(q, t)
                if w is not None:
                    waits.append(w)

            def run(engine, waits=waits):
                for w in waits:
                    self._emit_wait(engine, w)
            self.ops[q].append(run)
            self.ninstr += len(waits)

    def replay(self):
        nc = self.nc
        with nc.Block() as block:
            @block.sync
            def _(e):
                for f in self.ops["sp"]:
                    f(e)

            @block.scalar
            def _(e):
                for f in self.ops["act"]:
                    f(e)

            @block.vector
            def _(e):
                for f in self.ops["dve"]:
                    f(e)

            @block.gpsimd
            def _(e):
                for f in self.ops["pool"]:
                    f(e)

            @block.tensor
            def _(e):
                for f in self.ops["pe"]:
                    f(e)


class RR:
    def __init__(self, items):
        self.items = items
        self.i = 0

    def next(self):
        it = self.items[self.i % len(self.items)]
        self.i += 1
        return it


def host_consts():
    c = {}
    c["ident"] = np.eye(128, dtype=np.float32)
    p = np.arange(128)[:, None]
    f = np.arange(128)[None, :]
    c["m_le"] = (p <= f).astype(np.float32)
    p64 = np.arange(64)[:, None]
    f64 = np.arange(64)[None, :]
    rep = lambda m: np.ascontiguousarray(np.broadcast_to(m[:, None, :], (64, 10, 64))).reshape(64, 640).astype(np.float32)
    c["m64_lt"] = rep((p64 < f64).astype(np.float32))
    c["m64_le"] = rep((p64 <= f64).astype(np.float32))
    c["m64_gt"] = rep((p64 > f64).astype(np.float32))
    c["i64"] = rep(np.eye(64, dtype=np.float32))
    sel = np.zeros((128, 128), np.float32)
    sel[127, :] = 1.0
    c["sel_last"] = sel
    return c


def ret_consts(S):
    H, C, dh = TH, 128, 128
    log_g = np.log(1.0 - 2.0 ** (-5.0 - np.arange(H, dtype=np.float64)))
    pos = np.arange(C, dtype=np.float64)
    c = {}
    rel = pos[None, :] - pos[:, None]
    dec = np.where(rel[:, None, :] >= 0, np.exp(log_g[None, :, None] * np.maximum(rel[:, None, :], 0.0)), 0.0)
    c["ret_dec"] = dec.reshape(C, H * C).astype(np.float32)
    qd = np.exp(log_g[:, None] * (pos[None, :] + 1.0))
    c["ret_qd"] = np.ascontiguousarray(np.broadcast_to(qd[None], (128, H, C))).reshape(128, H * C).astype(np.float32)
    kd = np.exp(log_g[None, :] * (C - 1.0 - pos[:, None])) * dh ** -0.5
    c["ret_kd"] = kd.astype(np.float32)
    c["ret_cd"] = np.ascontiguousarray(np.broadcast_to(np.exp(log_g * C)[None, :], (128, H))).astype(np.float32)
    half = 64
    inv = 1.0 / (10000.0 ** (np.arange(half, dtype=np.float32) / half))
    ang = np.arange(S, dtype=np.float32)[:, None] * inv[None, :]
    c["ret_cos"] = np.cos(ang).astype(np.float32)
    c["ret_sin"] = np.sin(ang).astype(np.float32)
    return c


CONST_SHAPES = {
    "ident": [128, 128], "m_le": [128, 128], "m64_lt": [64, 640], "m64_le": [64, 640],
    "m64_gt": [64, 640], "i64": [64, 640], "sel_last": [128, 128],
    "ret_dec": [128, 640], "ret_qd": [128, 640], "ret_kd": [128, 5], "ret_cd": [128, 5],
}

PARAM_SHAPES = {
    "w_in": [D, P_IN], "fox_forget_bias": [FH, 1], "rwkv_mu": [1, RWKV_COLS], "rwkv_w0": [1, 640],
    "rwkv_w_up": [64, 640], "rwkv_a0": [1, 640], "rwkv_a_up": [64, 640], "rwkv_g_up": [128, 640],
    "rwkv_k_k": [1, 640], "rwkv_k_a": [1, 640], "rwkv_r_k": [1, 640], "rwkv_gn_w": [1, 640],
    "rwkv_gn_b": [1, 640], "ret_gn_w": [1, 640], "ret_gn_b": [1, 640], "w_out": [D, D],
    "ln1_g": [1, D], "ln1_b": [1, D], "w_up": [D, DFF], "w_down": [DFF, D], "ln2_g": [1, D], "ln2_b": [1, D],
}


def build_program(S, L, dbg=None, skip=()):
    NT = S // 128
    nc = bass.Bass("TRN2", target_bir_lowering=False)
    din = lambda name, shape, dt=F32: nc.dram_tensor(name, list(shape), dt, kind="ExternalInput").ap()
    x_in = din("x", [S, D])
    prm = {k: din(k, [L] + v) for k, v in PARAM_SHAPES.items()}
    cst = {k: din(k, v) for k, v in CONST_SHAPES.items()}
    cst["ret_cos"] = din("ret_cos", [S, 64])
    cst["ret_sin"] = din("ret_sin", [S, 64])
    y_out = nc.dram_tensor("y", [S, D], F32, kind="ExternalOutput").ap()
    dscr = lambda name, shape, dt: nc.dram_tensor(name, list(shape), dt).ap()
    xres = [dscr("xresA", [S, D], F32), dscr("xresB", [S, D], F32)]
    xT = dscr("xT", [D, S], BF16)
    x1 = dscr("x1", [S, D], F32)
    x1T = dscr("x1T", [D, S], BF16)
    qkT = dscr("qkT", [2 * FOX_W, S], BF16)
    fT = dscr("fT", [FH, S], F32)
    hproj = dscr("hproj", [S, P_IN - OFF_FV], F32)
    mix = dscr("mix", [S, D], F32)
    mixT = dscr("mixT", [D, S], BF16)
    ytmp = dscr("ytmp", [S, D], F32)
    hidT = dscr("hidT", [DFF, S], BF16)
    dbg_out = {}
    if dbg:
        for name, shape in (("hproj", [S, P_IN - OFF_FV]), ("mix", [S, D]), ("x1", [S, D]), ("fT", [FH, S])):
            dbg_out[name] = nc.dram_tensor("dbg_" + name, list(shape), F32, kind="ExternalOutput").ap()

    B = {n: Buf(n) for n in ["xresA", "xresB", "xT", "x1", "x1T", "qkT", "fT", "hproj", "mix", "mixT",
                              "ytmp", "hidT", "y"]}

    with ExitStack() as st:
        Sx = Sched(nc, st)
        _id = [0]

        def sb(shape, dt, name=None):
            _id[0] += 1
            t = st.enter_context(nc.sbuf_tensor(f"{name or 't'}_{_id[0]}", list(shape), dt))
            return t, Buf(name or "t")

        PS = []
        for i in range(4):
            t = st.enter_context(nc.psum_tensor(f"ps{i}", [128, 2, 512], F32))
            PS.append((t, [Buf(f"ps{i}a"), Buf(f"ps{i}b")]))
        banks = [(PS[i][0][:, a, :], PS[i][1][a]) for i in range(4) for a in range(2)]

        ident, b_ident = sb([128, 128], F32, "ident")
        Sx.dma("sp", ident[:], cst["ident"], writes=[b_ident])
        identb, b_identb = sb([128, 128], BF16, "identb")
        Sx.op("dve", lambda e: e.tensor_copy(identb[:], ident[:]), reads=[b_ident], writes=[b_identb])

        def transpose_store(src, b_src, ncols, dstT, b_dstT, row0, tok0, ntok, trbank, stg):
            for c0 in range(0, ncols, 512):
                cw = min(512, ncols - c0)
                pb, b_pb = trbank.next()
                nblk = (cw + 127) // 128
                for bi in range(nblk):
                    bw = min(128, cw - bi * 128)
                    Sx.op("pe", lambda e, pb=pb, bi=bi, bw=bw, c0=c0: e.transpose(
                        pb[:bw, bi * 128:bi * 128 + ntok], src[:ntok, c0 + bi * 128:c0 + bi * 128 + bw], ident[:ntok, :ntok]),
                        reads=[b_src, b_ident], writes=[b_pb])
                sg, b_sg = stg.next()
                if cw % 128 == 0:
                    Sx.op("act", lambda e, pb=pb, sg=sg, nblk=nblk: e.copy(
                        sg[:, :nblk, :ntok], pb[:, :nblk * 128].rearrange("p (b t) -> p b t", t=128)[:, :, :ntok]),
                        reads=[b_pb], writes=[b_sg])
                    Sx.dma("sp", dstT[row0 + c0:row0 + c0 + cw, tok0:tok0 + ntok].rearrange("(b p) t -> p b t", p=128),
                           sg[:, :nblk, :ntok], reads=[b_sg], writes=[b_dstT])
                else:
                    for bi in range(nblk):
                        bw = min(128, cw - bi * 128)
                        Sx.op("act", lambda e, pb=pb, sg=sg, bi=bi, bw=bw: e.copy(
                            sg[:bw, bi, :ntok], pb[:bw, bi * 128:bi * 128 + ntok]), reads=[b_pb], writes=[b_sg])
                        Sx.dma("sp", dstT[row0 + c0 + bi * 128:row0 + c0 + bi * 128 + bw, tok0:tok0 + ntok],
                               sg[:bw, bi, :ntok], reads=[b_sg], writes=[b_dstT])

        def gemm_tok(aT, b_aT, K, W, n0, n1, evac, cb=None):
            KC = K // 128
            cb = cb or (1024 if KC <= 16 else 512)
            with ExitStack() as ls:
                def lsb(shape, dt, name):
                    _id[0] += 1
                    return ls.enter_context(nc.sbuf_tensor(f"{name}_{_id[0]}", list(shape), dt)), Buf(name)
                tb = min(S, 512 if KC <= 16 else 256)
                wb = RR([lsb([128, KC, cb], BF16, "gw") for _ in range(2)])
                ab = RR([lsb([128, KC, tb], BF16, "ga") for _ in range(2)])
                pbk = RR(banks[0:4])
                Wv = W.rearrange("(kc p) n -> p kc n", p=128)
                aTv = aT.rearrange("(kc p) s -> p kc s", p=128)
                for c0 in range(n0, n1, cb):
                    cw = min(cb, n1 - c0)
                    wt, b_wt = wb.next()
                    kstep = max(1, min(KC, 2048 // max(1, (cw + 511) // 512) // 128))
                    for k0 in range(0, KC, kstep):
                        Sx.dma("pool", wt[:, k0:k0 + kstep, :cw], Wv[:, k0:k0 + kstep, c0:c0 + cw], writes=[b_wt])
                    for t0 in range(0, S, tb):
                        at, b_at = ab.next()
                        ksp = max(1, KC // 4)
                        for k0 in range(0, KC, ksp):
                            Sx.dma("sp", at[:, k0:k0 + ksp, :], aTv[:, k0:k0 + ksp, t0:t0 + tb], reads=[b_aT], writes=[b_at])
                        for ts_ in range(tb // 128):
                            ti = t0 // 128 + ts_
                            for s0 in range(0, cw, 512):
                                sw = min(512, cw - s0)
                                pb, b_pb = pbk.next()
                                for kc in range(KC):
                                    Sx.op("pe", lambda e, pb=pb, at=at, wt=wt, kc=kc, s0=s0, sw=sw, ts_=ts_: e.matmul(
                                        pb[:, :sw], at[:, kc, ts_ * 128:(ts_ + 1) * 128], wt[:, kc, s0:s0 + sw], start=(kc == 0), stop=(kc == KC - 1)),
                                        reads=[b_at, b_wt], writes=[b_pb])
                                evac(ti, c0 + s0, sw, pb, b_pb)
                Sx.barrier()

        def gemm_feat(aT, b_aT, K, W, n0, n1, evac, TB=4096):
            KC = K // 128
            TB = min(TB, S)
            with ExitStack() as ls:
                def lsb(shape, dt, name):
                    _id[0] += 1
                    return ls.enter_context(nc.sbuf_tensor(f"{name}_{_id[0]}", list(shape), dt)), Buf(name)
                ablk, b_ablk = lsb([128, KC, TB], BF16, "fa")
                wb = RR([lsb([128, KC, 512], BF16, "fw") for _ in range(2)])
                pbk = RR(banks[0:4])
                Wv = W.rearrange("(kc p) n -> p kc n", p=128)
                aTv = aT.rearrange("(kc p) s -> p kc s", p=128)
                for t0 in range(0, S, TB):
                    for k0 in range(0, KC, 2):
                        Sx.dma("sp", ablk[:, k0:k0 + 2, :], aTv[:, k0:k0 + 2, t0:t0 + TB], reads=[b_aT], writes=[b_ablk])
                    for g0 in range(n0, n1, 512):
                        gw = min(512, n1 - g0)
                        wt, b_wt = wb.next()
                        for k0 in range(0, KC, 4):
                            Sx.dma("pool", wt[:, k0:k0 + 4, :gw], Wv[:, k0:k0 + 4, g0:g0 + gw], writes=[b_wt])
                        for r0 in range(g0, g0 + gw, 128):
                            rw = min(128, g0 + gw - r0)
                            for t4 in range(0, TB, 512):
                                tw = min(512, TB - t4)
                                pb, b_pb = pbk.next()
                                for kc in range(KC):
                                    Sx.op("pe", lambda e, pb=pb, wt=wt, kc=kc, rw=rw, t4=t4, tw=tw, ro=r0 - g0: e.matmul(
                                        pb[:rw, :tw], wt[:, kc, ro:ro + rw], ablk[:, kc, t4:t4 + tw], start=(kc == 0), stop=(kc == KC - 1)),
                                        reads=[b_ablk, b_wt], writes=[b_pb])
                                evac(r0, rw, t0 + t4, tw, pb, b_pb)
                Sx.barrier()

        def ln_pass(xold, b_xold, yadd, b_yadd, gam, bet, xnew, b_xnew, xnewT, b_xnewT):
            with ExitStack() as ls:
                def lsb(shape, dt, name):
                    _id[0] += 1
                    return ls.enter_context(nc.sbuf_tensor(f"{name}_{_id[0]}", list(shape), dt)), Buf(name)
                g_b, b_g = lsb([128, D], F32, "lng")
                be_b, b_be = lsb([128, D], F32, "lnb")
                Sx.dma("sp", g_b[:], gam.broadcast_to([128, D]), writes=[b_g])
                Sx.dma("sp", be_b[:], bet.broadcast_to([128, D]), writes=[b_be])
                xo = RR([lsb([128, D], F32, "lx") for _ in range(2)])
                ya = RR([lsb([128, D], F32, "ly") for _ in range(2)])
                sq = RR([lsb([128, D], F32, "lsq") for _ in range(2)])
                stt = RR([lsb([128, 8], F32, "lst") for _ in range(2)])
                trb = RR(banks[4:8])
                stg = RR([lsb([128, 4, 128], BF16, "lstg") for _ in range(3)])
                def tile_gen(ti):
                    r = slice(ti * 128, (ti + 1) * 128)
                    xt, b_xt = xo.next()
                    yt, b_yt = ya.next()
                    qt, b_qt = sq.next()
                    s_, b_s = stt.next()
                    Sx.dma("sp", xt[:], xold[r, :], reads=[b_xold], writes=[b_xt])
                    yield
                    Sx.dma("sp", yt[:], yadd[r, :], reads=[b_yadd], writes=[b_yt])
                    yield
                    Sx.op("dve", lambda e, s_=s_: e.memset(s_[:], 0.0), writes=[b_s])
                    yield
                    Sx.op("dve", lambda e, xt=xt, yt=yt: e.scalar_tensor_tensor(yt[:], xt[:], ALPHA, yt[:], ALU.mult, ALU.add),
                          reads=[b_xt, b_yt], writes=[b_yt])
                    yield
                    Sx.op("act", lambda e, yt=yt, qt=qt, s_=s_: e.activation(qt[:], yt[:], AF.Identity, accum_out=s_[:, 0:1]),
                          reads=[b_yt], writes=[b_qt, b_s])
                    yield
                    Sx.op("dve", lambda e, s_=s_: e.tensor_scalar(s_[:, 1:2], s_[:, 0:1], -1.0 / D, None, ALU.mult),
                          reads=[b_s], writes=[b_s])
                    Sx.op("act", lambda e, yt=yt, qt=qt, s_=s_: e.activation(qt[:], yt[:], AF.Square, bias=s_[:, 1:2], scale=1.0, accum_out=s_[:, 2:3]),
                          reads=[b_yt, b_s], writes=[b_qt, b_s])
                    yield
                    Sx.op("dve", lambda e, s_=s_: e.tensor_scalar(s_[:, 3:4], s_[:, 2:3], 1.0 / D, LN_EPS, ALU.mult, ALU.add),
                          reads=[b_s], writes=[b_s])
                    yield
                    Sx.op("act", lambda e, s_=s_: e.activation(s_[:, 4:5], s_[:, 3:4], AF.Sqrt), reads=[b_s], writes=[b_s])
                    yield
                    Sx.op("dve", lambda e, s_=s_: e.reciprocal(s_[:, 5:6], s_[:, 4:5]), reads=[b_s], writes=[b_s])
                    yield
                    Sx.op("dve", lambda e, s_=s_: e.tensor_tensor(s_[:, 6:7], s_[:, 1:2], s_[:, 5:6], ALU.mult), reads=[b_s], writes=[b_s])
                    yield
                    Sx.op("act", lambda e, yt=yt, s_=s_: e.activation(yt[:], yt[:], AF.Identity, bias=s_[:, 6:7], scale=s_[:, 5:6]),
                          reads=[b_yt, b_s], writes=[b_yt])
                    yield
                    Sx.op("dve", lambda e, yt=yt: e.tensor_tensor(yt[:], yt[:], g_b[:], ALU.mult), reads=[b_yt, b_g], writes=[b_yt])
                    yield
                    Sx.op("dve", lambda e, yt=yt: e.tensor_tensor(yt[:], yt[:], be_b[:], ALU.add), reads=[b_yt, b_be], writes=[b_yt])
                    yield
                    Sx.dma("sp", xnew[r, :], yt[:], reads=[b_yt], writes=[b_xnew])
                    yield
                    if xnewT is not None:
                        transpose_store(yt, b_yt, D, xnewT, b_xnewT, 0, ti * 128, 128, trb, stg)

                for t2 in range(0, NT, 2):
                    gens = [tile_gen(t2)] + ([tile_gen(t2 + 1)] if t2 + 1 < NT else [])
                    live = list(gens)
                    while live:
                        for g in list(live):
                            try:
                                next(g)
                            except StopIteration:
                                live.remove(g)
                Sx.barrier()

        def input_transpose(xsrc, b_xsrc):
            with ExitStack() as ls:
                def lsb(shape, dt, name):
                    _id[0] += 1
                    return ls.enter_context(nc.sbuf_tensor(f"{name}_{_id[0]}", list(shape), dt)), Buf(name)
                xo = RR([lsb([128, D], F32, "ix") for _ in range(2)])
                trb = RR(banks[4:8])
                stg = RR([lsb([128, 4, 128], BF16, "istg") for _ in range(3)])
                for ti in range(NT):
                    xt, b_xt = xo.next()
                    Sx.dma("sp", xt[:], xsrc[ti * 128:(ti + 1) * 128, :], reads=[b_xsrc], writes=[b_xt])
                    transpose_store(xt, b_xt, D, xT, B["xT"], 0, ti * 128, 128, trb, stg)
                Sx.barrier()

        def in_proj(l):
            W = prm["w_in"][l]
            with ExitStack() as ls:
                def lsb(shape, dt, name):
                    _id[0] += 1
                    return ls.enter_context(nc.sbuf_tensor(f"{name}_{_id[0]}", list(shape), dt)), Buf(name)
                stg = RR([lsb([128, 512], BF16, "pstg") for _ in range(3)])
                stf = RR([lsb([128, 512], F32, "pstf") for _ in range(3)])
                fb, b_fb = lsb([FH, 1], F32, "fbias")
                Sx.dma("sp", fb[:], prm["fox_forget_bias"][l], writes=[b_fb])

                def evac_qk(r0, rw, t0, tw, pb, b_pb):
                    sg, b_sg = stg.next()
                    sc = 128 ** -0.5 if r0 < FOX_W else 1.0
                    Sx.op("act", lambda e: e.activation(sg[:rw, :tw], pb[:rw, :tw], AF.Copy, scale=sc),
                          reads=[b_pb], writes=[b_sg])
                    Sx.dma("sp", qkT[r0:r0 + rw, t0:t0 + tw], sg[:rw, :tw], reads=[b_sg], writes=[B["qkT"]])
                gemm_feat(xT, B["xT"], D, W, 0, 2 * FOX_W, evac_qk)

                def evac_f(r0, rw, t0, tw, pb, b_pb):
                    sg, b_sg = stf.next()
                    Sx.op("dve", lambda e: e.tensor_scalar(sg[:rw, :tw], pb[:rw, :tw], fb[:, 0:1], None, ALU.add),
                          reads=[b_pb, b_fb], writes=[b_sg])
                    Sx.dma("sp", fT[:, t0:t0 + tw], sg[:rw, :tw], reads=[b_sg], writes=[B["fT"]])
                gemm_feat(xT, B["xT"], D, W, OFF_FF, OFF_FF + FH, evac_f)

                def evac_tok(ti, c0, cw, pb, b_pb):
                    sg, b_sg = stf.next()
                    Sx.op("act", lambda e: e.copy(sg[:, :cw], pb[:, :cw]), reads=[b_pb], writes=[b_sg])
                    Sx.dma("sp", hproj[ti * 128:(ti + 1) * 128, c0 - OFF_FV:c0 - OFF_FV + cw], sg[:, :cw],
                           reads=[b_sg], writes=[B["hproj"]])
                gemm_tok(xT, B["xT"], D, W, OFF_FV, P_IN, evac_tok)

        def fox(l):
            with ExitStack() as ls:
                def lsb(shape, dt, name):
                    _id[0] += 1
                    return ls.enter_context(nc.sbuf_tensor(f"{name}_{_id[0]}", list(shape), dt)), Buf(name)
                ls2 = ExitStack()
                def lsb2(shape, dt, name):
                    _id[0] += 1
                    return ls2.enter_context(nc.sbuf_tensor(f"{name}_{_id[0]}", list(shape), dt)), Buf(name)
                ctok, b_ctok = lsb([128, NT, FH], F32, "ctok")
                crefb, b_crefb = lsb([128, NT, FH], F32, "crefb")
                sel, b_sel = lsb([128, 128], F32, "sel")
                mle, b_mle = lsb([128, 128], BF16, "mle")
                mlef, b_mlef = lsb([128, 128], F32, "mlef")
                fa, b_fa = lsb2([FH, S], F32, "fa")
                fbuf, b_fbuf = lsb2([FH, S], F32, "fb")
                Sx.dma("sp", fa[:], fT, reads=[B["fT"]], writes=[b_fa])
                Sx.op("act", lambda e: e.activation(fa[:], fa[:], AF.Sigmoid), reads=[b_fa], writes=[b_fa])
                Sx.op("act", lambda e: e.activation(fa[:], fa[:], AF.Ln), reads=[b_fa], writes=[b_fa])
                cur, b_cur, oth, b_oth = fa, b_fa, fbuf, b_fbuf
                sh = 1
                while sh < S:
                    Sx.op("dve", lambda e, cur=cur, oth=oth, sh=sh: e.tensor_copy(oth[:, 0:sh], cur[:, 0:sh]),
                          reads=[b_cur], writes=[b_oth])
                    Sx.op("dve", lambda e, cur=cur, oth=oth, sh=sh: e.tensor_tensor(oth[:, sh:S], cur[:, sh:S], cur[:, 0:S - sh], ALU.add),
                          reads=[b_cur], writes=[b_oth])
                    cur, b_cur, oth, b_oth = oth, b_oth, cur, b_cur
                    sh *= 2
                pb, b_pb = banks[4]
                for j in range(NT):
                    Sx.op("pe", lambda e, j=j, cur=cur: e.transpose(pb[:, j * FH:(j + 1) * FH], cur[:, j * 128:(j + 1) * 128], ident[:FH, :FH]),
                          reads=[b_cur, b_ident], writes=[b_pb])
                Sx.op("dve", lambda e: e.tensor_copy(ctok[:], pb[:, :NT * FH].rearrange("p (j h) -> p j h", h=FH)),
                      reads=[b_pb], writes=[b_ctok])
                Sx.dma("sp", sel[:], cst["sel_last"], writes=[b_sel])
                pb2, b_pb2 = banks[5]
                Sx.op("pe", lambda e: e.matmul(pb2[:, :NT * FH], sel[:], ctok[:].rearrange("p j h -> p (j h)"), start=True, stop=True),
                      reads=[b_sel, b_ctok], writes=[b_pb2])
                Sx.op("dve", lambda e: e.tensor_copy(crefb[:], pb2[:, :NT * FH].rearrange("p (j h) -> p j h", h=FH)),
                      reads=[b_pb2], writes=[b_crefb])
                Sx.dma("sp", mlef[:], cst["m_le"], writes=[b_mlef])
                Sx.op("dve", lambda e: e.tensor_copy(mle[:], mlef[:]), reads=[b_mlef], writes=[b_mle])
                Sx.barrier()
                ls2.close()

                qh = RR([lsb([128, S], BF16, "qh") for _ in range(2)])
                kh = RR([lsb([128, S], BF16, "kh") for _ in range(2)])
                vh = RR([lsb([128, NT, 129], BF16, "vh") for _ in range(2)])
                bias_t = RR([lsb([128, NT, NT], F32, "fbias") for _ in range(2)])
                pT = RR([lsb([128, 128], BF16, "pT") for _ in range(4)])
                ot = RR([lsb([128, 128], F32, "ot") for _ in range(3)])
                rc = RR([lsb([128, 1], F32, "rc") for _ in range(3)])
                sbk = RR(banks[0:4])
                obk = RR(banks[6:8])
                hv = hproj.rearrange("(j p) c -> p j c", p=128)
                for h in range(FH):
                    qt, b_qt = qh.next()
                    kt, b_kt = kh.next()
                    vt, b_vt = vh.next()
                    bt, b_bt = bias_t.next()
                    Sx.dma("sp", qt[:], qkT[h * 128:(h + 1) * 128, :], reads=[B["qkT"]], writes=[b_qt])
                    Sx.dma("sp", kt[:], qkT[FOX_W + h * 128:FOX_W + (h + 1) * 128, :], reads=[B["qkT"]], writes=[b_kt])
                    Sx.op("dve", lambda e, vt=vt: e.memset(vt[:, :, 128:129], 1.0), writes=[b_vt])
                    for j0 in range(0, NT, 16):
                        j1 = min(NT, j0 + 16)
                        Sx.dma("pool", vt[:, j0:j1, 0:128], hv[:, j0:j1, h * 128:(h + 1) * 128], reads=[B["hproj"]], writes=[b_vt])
                    for j in range(NT):
                        Sx.op("dve", lambda e, j=j, bt=bt, h=h: e.tensor_scalar(
                            bt[:, j, :], crefb[:, :, h], ctok[:, j, h:h + 1], None, ALU.subtract),
                            reads=[b_crefb, b_ctok], writes=[b_bt])
                    pairs = [(i, j) for i in range(NT) for j in range(i + 1)]
                    pend = []
                    state = {}

                    def emit_qk(i, j, kt=kt, qt=qt, b_kt=b_kt, b_qt=b_qt):
                        sp_, b_sp = sbk.next()
                        Sx.op("pe", lambda e, sp_=sp_, i=i, j=j: e.matmul(
                            sp_[:, :128], kt[:, j * 128:(j + 1) * 128], qt[:, i * 128:(i + 1) * 128], start=True, stop=True),
                            reads=[b_kt, b_qt], writes=[b_sp])
                        return (i, j, sp_, b_sp)

                    def emit_rest(i, j, sp_, b_sp, bt=bt, b_bt=b_bt, vt=vt, b_vt=b_vt, h=h):
                        if j == 0:
                            state["ob"] = obk.next()
                        ob, b_ob = state["ob"]
                        p_, b_p = pT.next()
                        Sx.op("act", lambda e: e.activation(p_[:], sp_[:, :128], AF.Exp, bias=bt[:, j, i:i + 1], scale=1.0),
                              reads=[b_sp, b_bt], writes=[b_p])
                        if j == i:
                            Sx.op("dve", lambda e: e.tensor_tensor(p_[:], p_[:], mle[:], ALU.mult),
                                  reads=[b_p, b_mle], writes=[b_p])
                        Sx.op("pe", lambda e: e.matmul(ob[:, :129], p_[:], vt[:, j, :], start=(j == 0), stop=(j == i)),
                              reads=[b_p, b_vt], writes=[b_ob])
                        if j == i:
                            r_, b_r = rc.next()
                            o_, b_o = ot.next()
                            Sx.op("dve", lambda e: e.reciprocal(r_[:], ob[:, 128:129]), reads=[b_ob], writes=[b_r])
                            Sx.op("dve", lambda e: e.tensor_scalar(o_[:], ob[:, 0:128], r_[:, 0:1], None, ALU.mult),
                                  reads=[b_ob, b_r], writes=[b_o])
                            Sx.dma("sp", mix[i * 128:(i + 1) * 128, h * 128:(h + 1) * 128], o_[:], reads=[b_o], writes=[B["mix"]])
                    for (i, j) in pairs:
                        pend.append(emit_qk(i, j))
                        if len(pend) > 3:
                            emit_rest(*pend.pop(0))
                    while pend:
                        emit_rest(*pend.pop(0))
                Sx.barrier()

        def retention(l):
            C = 128
            NCH = S // C
            oq, ok_, ov, og = (OFF_RT - OFF_FV, OFF_RT - OFF_FV + 640, OFF_RT - OFF_FV + 1280, OFF_RT - OFF_FV + 1920)
            with ExitStack() as ls:
                def lsb(shape, dt, name):
                    _id[0] += 1
                    return ls.enter_context(nc.sbuf_tensor(f"{name}_{_id[0]}", list(shape), dt)), Buf(name)

                def cload(shape, src, name):
                    t, b = lsb(shape, F32, name)
                    Sx.dma("sp", t[:], src, writes=[b])
                    return t, b
                dec, b_dec = cload([128, 640], cst["ret_dec"], "rdec")
                qd, b_qd = cload([128, 640], cst["ret_qd"], "rqd")
                kd, b_kd = cload([128, 5], cst["ret_kd"], "rkd")
                cd, b_cd = cload([128, 5], cst["ret_cd"], "rcd")
                gw, b_gw = cload([128, 640], prm["ret_gn_w"][l].broadcast_to([128, 640]), "rgw")
                gb, b_gb = cload([128, 640], prm["ret_gn_b"][l].broadcast_to([128, 640]), "rgb")
                R, b_R = lsb([128, TH, 128], F32, "R")
                Sx.op("dve", lambda e: e.memset(R[:], 0.0), writes=[b_R])
                hin = RR([lsb([128, 2560], F32, "rh") for _ in range(2)])
                cs = RR([lsb([128, 128], F32, "rcs") for _ in range(2)])
                qr = RR([lsb([128, 640], F32, "rqr") for _ in range(2)])
                kr = RR([lsb([128, 640], F32, "rkr") for _ in range(2)])
                ks = RR([lsb([128, 640], F32, "rks") for _ in range(2)])
                tmp = RR([lsb([128, 640], F32, "rtmp") for _ in range(2)])
                qT_ = RR([lsb([128, 640], F32, "rqT") for _ in range(2)])
                qTd = RR([lsb([128, 640], F32, "rqTd") for _ in range(2)])
                kT_ = RR([lsb([128, 640], F32, "rkT") for _ in range(2)])
                inn = RR([lsb([128, 640], F32, "rinn") for _ in range(2)])
                o_ = RR([lsb([128, 640], F32, "ro") for _ in range(2)])
                st_ = RR([lsb([128, 4, TH], F32, "rst") for _ in range(2)])
                gs = RR([lsb([128, 640], F32, "rgs") for _ in range(2)])
                P0, P1, P2, P3 = PS[0], PS[1], PS[2], PS[3]

                def v3(ap):
                    return ap.rearrange("p (h c) -> p h c", c=128)

                def pv(Pt):
                    return Pt[0]

                for ci in range(NCH):
                    r = slice(ci * C, (ci + 1) * C)
                    ht, b_ht = hin.next()
                    Sx.dma("sp", ht[:], hproj[r, oq:oq + 2560], reads=[B["hproj"]], writes=[b_ht])
                    ct, b_ct = cs.next()
                    Sx.dma("sp", ct[:, 0:64], cst["ret_cos"][r, :], writes=[b_ct])
                    Sx.dma("sp", ct[:, 64:128], cst["ret_sin"][r, :], writes=[b_ct])
                    q_, b_q = qr.next()
                    k_, b_k = kr.next()
                    t_, b_t = tmp.next()
                    cosb = ct[:, 0:64].unsqueeze(1).broadcast_to([128, TH, 64])
                    sinb = ct[:, 64:128].unsqueeze(1).broadcast_to([128, TH, 64])
                    for (src_off, dst, b_dst) in ((0, q_, b_q), (640, k_, b_k)):
                        s3 = v3(ht[:, src_off:src_off + 640])
                        d3 = v3(dst[:])
                        t3 = v3(t_[:])
                        x1_, x2_ = s3[:, :, 0:64], s3[:, :, 64:128]
                        Sx.op("dve", lambda e, d3=d3, x1_=x1_, cosb=cosb: e.tensor_tensor(d3[:, :, 0:64], x1_, cosb, ALU.mult), reads=[b_ht, b_ct], writes=[b_dst])
                        Sx.op("dve", lambda e, t3=t3, x2_=x2_, sinb=sinb: e.tensor_tensor(t3[:, :, 0:64], x2_, sinb, ALU.mult), reads=[b_ht, b_ct], writes=[b_t])
                        Sx.op("dve", lambda e, d3=d3, t3=t3: e.tensor_tensor(d3[:, :, 0:64], d3[:, :, 0:64], t3[:, :, 0:64], ALU.subtract), reads=[b_dst, b_t], writes=[b_dst])
                        Sx.op("dve", lambda e, d3=d3, x1_=x1_, sinb=sinb: e.tensor_tensor(d3[:, :, 64:128], x1_, sinb, ALU.mult), reads=[b_ht, b_ct], writes=[b_dst])
                        Sx.op("dve", lambda e, t3=t3, x2_=x2_, cosb=cosb: e.tensor_tensor(t3[:, :, 64:128], x2_, cosb, ALU.mult), reads=[b_ht, b_ct], writes=[b_t])
                        Sx.op("dve", lambda e, d3=d3, t3=t3: e.tensor_tensor(d3[:, :, 64:128], d3[:, :, 64:128], t3[:, :, 64:128], ALU.add), reads=[b_dst, b_t], writes=[b_dst])
                    ks_, b_ks = ks.next()
                    Sx.op("dve", lambda e, ks_=ks_, k_=k_: e.tensor_tensor(v3(ks_[:]), v3(k_[:]), kd[:].unsqueeze(2).broadcast_to([128, TH, 128]), ALU.mult),
                          reads=[b_k, b_kd], writes=[b_ks])
                    qTt, b_qT = qT_.next()
                    kTt, b_kT = kT_.next()
                    qTdt, b_qTd = qTd.next()
                    for (src, b_src, dstT, b_dstT, Pt, scale) in ((q_, b_q, qTt, b_qT, P0, 1.0), (k_, b_k, kTt, b_kT, P1, 128 ** -0.5)):
                        for hh in range(TH):
                            a, c5 = (0, hh) if hh < 4 else (1, 0)
                            Sx.op("pe", lambda e, Pt=Pt, a=a, c5=c5, src=src, hh=hh: e.transpose(
                                Pt[0][:, a, c5 * 128:(c5 + 1) * 128], src[:, hh * 128:(hh + 1) * 128], ident[:]),
                                reads=[b_src, b_ident], writes=[Pt[1][a]])
                        Sx.op("act", lambda e, Pt=Pt, dstT=dstT, scale=scale: e.activation(dstT[:, 0:512], Pt[0][:, 0, :], AF.Copy, scale=scale),
                              reads=[Pt[1][0]], writes=[b_dstT])
                        Sx.op("act", lambda e, Pt=Pt, dstT=dstT, scale=scale: e.activation(dstT[:, 512:640], Pt[0][:, 1, 0:128], AF.Copy, scale=scale),
                              reads=[Pt[1][1]], writes=[b_dstT])
                    Sx.op("dve", lambda e, qTdt=qTdt, qTt=qTt: e.tensor_tensor(qTdt[:], qTt[:], qd[:], ALU.mult),
                          reads=[b_qT, b_qd], writes=[b_qTd])
                    for hh in range(TH):
                        a, c5 = (0, hh) if hh < 4 else (1, 0)
                        Sx.op("pe", lambda e, a=a, c5=c5, hh=hh, kTt=kTt, qTt=qTt: e.matmul(
                            P2[0][:, a, c5 * 128:(c5 + 1) * 128], kTt[:, hh * 128:(hh + 1) * 128], qTt[:, hh * 128:(hh + 1) * 128],
                            start=True, stop=True), reads=[b_kT, b_qT], writes=[P2[1][a]])
                    in_, b_in = inn.next()
                    Sx.op("dve", lambda e, in_=in_: e.tensor_tensor(in_[:, 0:512], P2[0][:, 0, :], dec[:, 0:512], ALU.mult),
                          reads=[P2[1][0], b_dec], writes=[b_in])
                    Sx.op("dve", lambda e, in_=in_: e.tensor_tensor(in_[:, 512:640], P2[0][:, 1, 0:128], dec[:, 512:640], ALU.mult),
                          reads=[P2[1][1], b_dec], writes=[b_in])
                    for hh in range(TH):
                        a, c5 = (0, hh) if hh < 4 else (1, 0)
                        Sx.op("pe", lambda e, a=a, c5=c5, hh=hh, in_=in_, ht=ht: e.matmul(
                            P3[0][:, a, c5 * 128:(c5 + 1) * 128], in_[:, hh * 128:(hh + 1) * 128], ht[:, 1280 + hh * 128:1280 + (hh + 1) * 128],
                            start=True, stop=False), reads=[b_in, b_ht], writes=[P3[1][a]])
                        Sx.op("pe", lambda e, a=a, c5=c5, hh=hh, qTdt=qTdt: e.matmul(
                            P3[0][:, a, c5 * 128:(c5 + 1) * 128], qTdt[:, hh * 128:(hh + 1) * 128], R[:, hh, :],
                            start=False, stop=True), reads=[b_qTd, b_R], writes=[P3[1][a]])
                    ot, b_ot = o_.next()
                    Sx.op("act", lambda e, ot=ot: e.copy(ot[:, 0:512], P3[0][:, 0, :]), reads=[P3[1][0]], writes=[b_ot])
                    Sx.op("act", lambda e, ot=ot: e.copy(ot[:, 512:640], P3[0][:, 1, 0:128]), reads=[P3[1][1]], writes=[b_ot])
                    for hh in range(TH):
                        a, c5 = (0, hh) if hh < 4 else (1, 0)
                        Sx.op("pe", lambda e, a=a, c5=c5, hh=hh, ks_=ks_, ht=ht: e.matmul(
                            P0[0][:, a, c5 * 128:(c5 + 1) * 128], ks_[:, hh * 128:(hh + 1) * 128], ht[:, 1280 + hh * 128:1280 + (hh + 1) * 128],
                            start=True, stop=True), reads=[b_ks, b_ht, b_R], writes=[P0[1][a]])
                    Sx.op("dve", lambda e: e.tensor_tensor(R[:], R[:], cd[:].unsqueeze(2).broadcast_to([128, TH, 128]), ALU.mult),
                          reads=[b_R, b_cd], writes=[b_R])
                    Sx.op("dve", lambda e: e.tensor_tensor(R[:, 0:4, :], R[:, 0:4, :], P0[0][:, 0, :].rearrange("p (h c) -> p h c", c=128), ALU.add),
                          reads=[b_R, P0[1][0]], writes=[b_R])
                    Sx.op("dve", lambda e: e.tensor_tensor(R[:, 4, :], R[:, 4, :], P0[0][:, 1, 0:128], ALU.add),
                          reads=[b_R, P0[1][1]], writes=[b_R])
                    s4, b_s4 = st_.next()
                    o3 = v3(ot[:])
                    t3 = v3(t_[:])
                    Sx.op("dve", lambda e, s4=s4, o3=o3: e.reduce_sum(s4[:, 0, :], o3, AX.X), reads=[b_ot], writes=[b_s4])
                    Sx.op("dve", lambda e, s4=s4: e.tensor_scalar(s4[:, 0, :], s4[:, 0, :], 1.0 / 128, None, ALU.mult), reads=[b_s4], writes=[b_s4])
                    Sx.op("dve", lambda e, s4=s4, o3=o3: e.tensor_tensor(o3, o3, s4[:, 0, :].unsqueeze(2).broadcast_to([128, TH, 128]), ALU.subtract),
                          reads=[b_ot, b_s4], writes=[b_ot])
                    Sx.op("dve", lambda e, o3=o3, t3=t3: e.tensor_tensor(t3, o3, o3, ALU.mult), reads=[b_ot], writes=[b_t])
                    Sx.op("dve", lambda e, s4=s4, t3=t3: e.reduce_sum(s4[:, 1, :], t3, AX.X), reads=[b_t], writes=[b_s4])
                    Sx.op("dve", lambda e, s4=s4: e.tensor_scalar(s4[:, 1, :], s4[:, 1, :], 1.0 / 128, 1e-5, ALU.mult, ALU.add), reads=[b_s4], writes=[b_s4])
                    Sx.op("act", lambda e, s4=s4: e.activation(s4[:, 2, :], s4[:, 1, :], AF.Sqrt), reads=[b_s4], writes=[b_s4])
                    Sx.op("dve", lambda e, s4=s4: e.reciprocal(s4[:, 3, :], s4[:, 2, :]), reads=[b_s4], writes=[b_s4])
                    Sx.op("dve", lambda e, s4=s4, o3=o3: e.tensor_tensor(o3, o3, s4[:, 3, :].unsqueeze(2).broadcast_to([128, TH, 128]), ALU.mult),
                          reads=[b_ot, b_s4], writes=[b_ot])
                    Sx.op("dve", lambda e, ot=ot: e.tensor_tensor(ot[:], ot[:], gw[:], ALU.mult), reads=[b_ot, b_gw], writes=[b_ot])
                    Sx.op("dve", lambda e, ot=ot: e.tensor_tensor(ot[:], ot[:], gb[:], ALU.add), reads=[b_ot, b_gb], writes=[b_ot])
                    g_, b_g = gs.next()
                    Sx.op("act", lambda e, g_=g_, ht=ht: e.activation(g_[:], ht[:, 1920:2560], AF.Silu), reads=[b_ht], writes=[b_g])
                    Sx.op("dve", lambda e, ot=ot, g_=g_: e.tensor_tensor(ot[:], ot[:], g_[:], ALU.mult), reads=[b_ot, b_g], writes=[b_ot])
                    Sx.dma("sp", mix[r, FOX_W + RWKV_W:D], ot[:], reads=[b_ot], writes=[B["mix"]])
                Sx.barrier()

        def mix_transpose():
            with ExitStack() as ls:
                def lsb(shape, dt, name):
                    _id[0] += 1
                    return ls.enter_context(nc.sbuf_tensor(f"{name}_{_id[0]}", list(shape), dt)), Buf(name)
                xo = RR([lsb([128, D], F32, "mx") for _ in range(2)])
                trb = RR(banks[4:8])
                stg = RR([lsb([128, 4, 128], BF16, "mstg") for _ in range(3)])
                for ti in range(NT):
                    xt, b_xt = xo.next()
                    Sx.dma("sp", xt[:], mix[ti * 128:(ti + 1) * 128, :], reads=[B["mix"]], writes=[b_xt])
                    transpose_store(xt, b_xt, D, mixT, B["mixT"], 0, ti * 128, 128, trb, stg)
                Sx.barrier()

        def evac_to(dst, b_dst):
            def mk(ls_stf):
                def evac(ti, c0, cw, pb, b_pb):
                    sg, b_sg = ls_stf.next()
                    Sx.op("act", lambda e: e.copy(sg[:, :cw], pb[:, :cw]), reads=[b_pb], writes=[b_sg])
                    Sx.dma("sp", dst[ti * 128:(ti + 1) * 128, c0:c0 + cw], sg[:, :cw], reads=[b_sg], writes=[b_dst])
                return evac
            return mk

        def out_proj(l):
            with ExitStack() as ls:
                def lsb(shape, dt, name):
                    _id[0] += 1
                    return ls.enter_context(nc.sbuf_tensor(f"{name}_{_id[0]}", list(shape), dt)), Buf(name)
                stf = RR([lsb([128, 512], F32, "ostf") for _ in range(3)])
                gemm_tok(mixT, B["mixT"], D, prm["w_out"][l], 0, D, evac_to(ytmp, B["ytmp"])(stf))

        def ffn(l):
            with ExitStack() as ls:
                def lsb(shape, dt, name):
                    _id[0] += 1
                    return ls.enter_context(nc.sbuf_tensor(f"{name}_{_id[0]}", list(shape), dt)), Buf(name)
                stg = RR([lsb([128, 512], BF16, "fstg") for _ in range(3)])
                stf = RR([lsb([128, 512], F32, "fstf") for _ in range(3)])

                def evac_up(r0, rw, t0, tw, pb, b_pb):
                    sf, b_sf = stf.next()
                    sg, b_sg = stg.next()
                    Sx.op("act", lambda e: e.activation(sf[:rw, :tw], pb[:rw, :tw], AF.Relu), reads=[b_pb], writes=[b_sf])
                    Sx.op("dve", lambda e: e.tensor_tensor(sg[:rw, :tw], sf[:rw, :tw], sf[:rw, :tw], ALU.mult), reads=[b_sf], writes=[b_sg])
                    Sx.dma("sp", hidT[r0:r0 + rw, t0:t0 + tw], sg[:rw, :tw], reads=[b_sg], writes=[B["hidT"]])
                gemm_feat(x1T, B["x1T"], D, prm["w_up"][l], 0, DFF, evac_up)
                gemm_tok(hidT, B["hidT"], DFF, prm["w_down"][l], 0, D, evac_to(ytmp, B["ytmp"])(stf))

        rwkv = make_rwkv(nc, Sx, S, prm, cst, hproj, mix, B, PS, ident, b_ident, _id)

        cur = x_in
        b_cur = Buf("xin")
        input_transpose(cur, b_cur)
        for l in range(L):
            if "in_proj" not in skip:
                in_proj(l)
            if "fox" not in skip:
                fox(l)
            if "rwkv" not in skip:
                rwkv(l)
            if "ret" not in skip:
                retention(l)
            if "mixT" not in skip:
                mix_transpose()
            if "out_proj" not in skip:
                out_proj(l)
            if "ln" not in skip:
                ln_pass(cur, b_cur, ytmp, B["ytmp"], prm["ln1_g"][l], prm["ln1_b"][l], x1, B["x1"], x1T, B["x1T"])
            if "ffn" not in skip:
                ffn(l)
            last = (l == L - 1)
            nxt = y_out if last else xres[l % 2]
            b_nxt = B["y"] if last else B["xresA" if l % 2 == 0 else "xresB"]
            ln_pass(x1, B["x1"], ytmp, B["ytmp"], prm["ln2_g"][l], prm["ln2_b"][l], nxt, b_nxt,
                    None if last else xT, B["xT"])
            cur, b_cur = nxt, b_nxt
        Sx.barrier()
        if dbg:
            srcs = {"hproj": hproj, "mix": mix, "x1": x1, "fT": fT}
            for name, ap in dbg_out.items():
                Sx.dma("sp", ap, srcs[name])
            Sx.barrier()
        Sx.replay()
        print("instructions:", Sx.ninstr, {e: Sx.cnt[e] for e in Sx.cnt})
    return nc


def make_rwkv(nc, Sx, S, prm, cst, hproj, mix, B, PS, ident, b_ident, _id):
    C = 64
    NCH = S // C
    off = OFF_RW - OFF_FV

    def rwkv(l):
        with ExitStack() as ls:
            def lsb(shape, dt, name):
                _id[0] += 1
                return ls.enter_context(nc.sbuf_tensor(f"{name}_{_id[0]}", list(shape), dt)), Buf(name)

            def cload(shape, src, name):
                t, b = lsb(shape, F32, name)
                Sx.dma("sp", t[:], src, writes=[b])
                return t, b
            bc = lambda name, n: prm[name][l].broadcast_to([C, n])
            mu_b, b_mu = cload([C, RWKV_COLS], bc("rwkv_mu", RWKV_COLS), "wmu")
            w0_b, b_w0 = cload([C, 640], bc("rwkv_w0", 640), "ww0")
            a0_b, b_a0 = cload([C, 640], bc("rwkv_a0", 640), "wa0")
            kk_b, b_kkb = cload([C, 640], bc("rwkv_k_k", 640), "wkk")
            ka_b, b_ka = cload([C, 640], bc("rwkv_k_a", 640), "wka")
            rk_b, b_rk = cload([C, 640], bc("rwkv_r_k", 640), "wrk")
            gw_b, b_gw = cload([C, 640], bc("rwkv_gn_w", 640), "wgw")
            gb_b, b_gb = cload([C, 640], bc("rwkv_gn_b", 640), "wgb")
            wup, b_wup = cload([64, 640], prm["rwkv_w_up"][l], "wup")
            aup, b_aup = cload([64, 640], prm["rwkv_a_up"][l], "aup")
            gup, b_gup = cload([128, 640], prm["rwkv_g_up"][l], "gup")
            m_lt, b_mlt = cload([64, 640], cst["m64_lt"], "mlt")
            m_le, b_mle = cload([64, 640], cst["m64_le"], "mle")
            m_gt, b_mgt = cload([64, 640], cst["m64_gt"], "mgt")
            i64, b_i64 = cload([64, 640], cst["i64"], "i64")
            H, b_H = lsb([64, 640], F32, "H")
            Sx.op("dve", lambda e: e.tensor_scalar(H[:].bitcast(F32R), i64[:], 0.0, None, ALU.mult), reads=[b_i64], writes=[b_H])
            _pool = {}

            N2 = {"g", "rtT", "E1T", "ArbT", "ArkT", "W1T", "W2"}

            def T(name, shape=(64, 640), n=1):
                if name in N2:
                    n = 2
                if name not in _pool:
                    _pool[name] = RR([lsb(list(shape), F32, name) for _ in range(n)])
                return _pool[name].next()
            prr = RR(PS)

            def s2(t):
                return t[:, :].rearrange("p (a c) -> p a c", a=2)

            def p2(P):
                return P[0][:64, :, 0:320]

            def ph(P, h):
                return P[0][:64, h // 5, (h % 5) * 64:(h % 5 + 1) * 64]

            def hs_(t, h):
                return t[:, h * 64:(h + 1) * 64]

            def v10(t):
                return t[:, :].rearrange("p (h c) -> p h c", c=64)

            def mm10(lhs_fn, rhs_fn, reads, acc=None):
                P = prr.next()
                for h in range(10):
                    ls_ = lhs_fn(h)
                    rs_ = rhs_fn(h)
                    if not isinstance(ls_, (list, tuple)):
                        ls_, rs_ = [ls_], [rs_]
                    n = len(ls_)
                    for i in range(n):
                        Sx.op("pe", lambda e, P=P, h=h, a=ls_[i], b=rs_[i], i=i, n=n: e.matmul(
                            ph(P, h), a.bitcast(F32R), b.bitcast(F32R), start=(i == 0), stop=(i == n - 1)), reads=reads, writes=[P[1][h // 5]])
                return P

            def evac(P, name, eng="act", mask=None, b_mask=None, add=None, b_add=None, r32=True):
                t, b_t = T(name)
                o = s2(t).bitcast(F32R) if r32 else s2(t)
                if mask is not None:
                    Sx.op("dve", lambda e: e.tensor_tensor(o, p2(P), s2(mask), ALU.mult), reads=[P[1][0], P[1][1], b_mask], writes=[b_t])
                elif add is not None:
                    Sx.op("dve", lambda e: e.tensor_tensor(o, p2(P), s2(add), ALU.add), reads=[P[1][0], P[1][1], b_add], writes=[b_t])
                elif eng == "act":
                    Sx.op("act", lambda e: e.copy(o, p2(P)), reads=[P[1][0], P[1][1]], writes=[b_t])
                else:
                    Sx.op("dve", lambda e: e.tensor_copy(o, p2(P)), reads=[P[1][0], P[1][1]], writes=[b_t])
                return t, b_t

            def chunk(ci):
                c0 = ci * C
                h_, b_h = T("h", (C, RWKV_COLS), n=2)
                hp, b_hp = T("hp", (C, RWKV_COLS))
                Sx.dma("sp", h_[:], hproj[c0:c0 + C, off:off + RWKV_COLS], reads=[B["hproj"]], writes=[b_h])
                if ci == 0:
                    Sx.op("dve", lambda e, hp=hp: e.memset(hp[:], 0.0), writes=[b_hp])
                    Sx.dma("sp", hp[1:C, :], hproj[0:C - 1, off:off + RWKV_COLS], reads=[B["hproj"]], writes=[b_hp])
                else:
                    Sx.dma("sp", hp[:], hproj[c0 - 1:c0 + C - 1, off:off + RWKV_COLS], reads=[B["hproj"]], writes=[b_hp])
                Sx.op("dve", lambda e, hp=hp, h_=h_: e.tensor_tensor(hp[:], hp[:], h_[:], ALU.subtract), reads=[b_hp, b_h], writes=[b_hp])
                Sx.op("dve", lambda e, hp=hp: e.tensor_tensor(hp[:], hp[:], mu_b[:], ALU.mult), reads=[b_hp, b_mu], writes=[b_hp])
                Sx.op("dve", lambda e, hp=hp, h_=h_: e.tensor_tensor(h_[:].bitcast(F32R), h_[:], hp[:], ALU.add), reads=[b_hp, b_h], writes=[b_h])
                r_ = h_[:, 0:640]
                yield
                k_ = h_[:, 640:1280]
                v_ = h_[:, 1280:1920]
                vh = lambda h, h_=h_: h_[:, 1280 + h * 64:1280 + (h + 1) * 64]
                lor, b_lor = T("lor", (C, 256))
                Sx.op("act", lambda e, lor=lor, h_=h_: e.activation(lor[:, 0:64], h_[:, 1920:1984], AF.Tanh), reads=[b_h], writes=[b_lor])
                Sx.op("act", lambda e, lor=lor, h_=h_: e.activation(lor[:, 128:256], h_[:, 2048:2176], AF.Sigmoid), reads=[b_h], writes=[b_lor])
                Pl = prr.next()
                Sx.op("pe", lambda e, Pl=Pl, lor=lor: e.transpose(Pl[0][:64, 0, 0:64], lor[:, 0:64], ident[:C, :C]), reads=[b_lor, b_ident], writes=[Pl[1][0]])
                Sx.op("pe", lambda e, Pl=Pl, h_=h_: e.transpose(Pl[0][:64, 0, 64:128], h_[:, 1984:2048], ident[:C, :C]), reads=[b_h, b_ident], writes=[Pl[1][0]])
                Sx.op("pe", lambda e, Pl=Pl, lor=lor: e.transpose(Pl[0][:, 0, 128:192], lor[:, 128:256], ident[:C, :C]), reads=[b_lor, b_ident], writes=[Pl[1][0]])
                lT, b_lT = T("lT", (128, 192))
                Sx.op("act", lambda e, lT=lT, Pl=Pl: e.copy(lT[:64, 0:128], Pl[0][:64, 0, 0:128]), reads=[Pl[1][0]], writes=[b_lT])
                Sx.op("act", lambda e, lT=lT, Pl=Pl: e.copy(lT[:, 128:192], Pl[0][:, 0, 128:192]), reads=[Pl[1][0]], writes=[b_lT])

                def lora_mm(lhsT, rhsW, b_w):
                    P = prr.next()
                    for a in range(2):
                        Sx.op("pe", lambda e, P=P, a=a: e.matmul(P[0][:64, a, 0:320], lhsT, rhsW[:, a * 320:(a + 1) * 320], start=True, stop=True),
                              reads=[b_lT, b_w], writes=[P[1][a]])
                    return P
                Pz = lora_mm(lT[:64, 0:64], wup, b_wup)
                sg, b_sg = evac(Pz, "sg", add=w0_b, b_add=b_w0, r32=False)
                yield
                Sx.op("act", lambda e, sg=sg: e.activation(sg[:], sg[:], AF.Sigmoid), reads=[b_sg], writes=[b_sg])
                Pa = lora_mm(lT[:64, 64:128], aup, b_aup)
                ai, b_ai = evac(Pa, "ai", add=a0_b, b_add=b_a0, r32=False)
                yield
                Sx.op("act", lambda e, ai=ai: e.activation(ai[:], ai[:], AF.Sigmoid), reads=[b_ai], writes=[b_ai])
                Pg = lora_mm(lT[:, 128:192], gup, b_gup)
                g_, b_g = evac(Pg, "g", r32=False)
                yield
                kk, b_kk = T("kk")
                sq, b_sq = T("sq")
                st, b_st = T("st", (C, 40))
                Sx.op("dve", lambda e, kk=kk, k_=k_: e.tensor_tensor(kk[:], k_, kk_b[:], ALU.mult), reads=[b_h, b_kkb], writes=[b_kk])
                Sx.op("dve", lambda e, kk=kk, sq=sq: e.tensor_tensor(sq[:], kk[:], kk[:], ALU.mult), reads=[b_kk], writes=[b_sq])
                Sx.op("dve", lambda e, st=st, sq=sq: e.reduce_sum(st[:, 0:10], v10(sq), AX.X), reads=[b_sq], writes=[b_st])
                Sx.op("act", lambda e, st=st: e.activation(st[:, 10:20], st[:, 0:10], AF.Sqrt), reads=[b_st], writes=[b_st])
                Sx.op("dve", lambda e, st=st: e.tensor_scalar(st[:, 10:20], st[:, 10:20], 1e-12, None, ALU.max), reads=[b_st], writes=[b_st])
                Sx.op("dve", lambda e, st=st: e.reciprocal(st[:, 20:30], st[:, 10:20]), reads=[b_st], writes=[b_st])
                Sx.op("dve", lambda e, kk=kk, st=st: e.tensor_tensor(v10(kk), v10(kk), st[:, 20:30].unsqueeze(2).broadcast_to([C, 10, 64]), ALU.mult),
                      reads=[b_kk, b_st], writes=[b_kk])
                k2, b_k2 = T("k2", n=2)
                Sx.op("dve", lambda e, k2=k2, ai=ai: e.scalar_tensor_tensor(k2[:], ai[:], -1.0, ka_b[:], ALU.add, ALU.mult), reads=[b_ai, b_ka], writes=[b_k2])
                Sx.op("dve", lambda e, k2=k2, k_=k_: e.scalar_tensor_tensor(k2[:], k2[:], 1.0, k_, ALU.add, ALU.mult), reads=[b_k2, b_h], writes=[b_k2])
                bv, b_bv = T("bv")
                Sx.op("dve", lambda e, bv=bv, kk=kk, ai=ai: e.tensor_tensor(bv[:], kk[:], ai[:], ALU.mult), reads=[b_kk, b_ai], writes=[b_bv])
                yield
                Pc = prr.next()
                for a in range(2):
                    Sx.op("pe", lambda e, Pc=Pc, a=a, sg=sg: e.matmul(Pc[0][:64, a, 0:320], m_le[:, 0:64], sg[:, a * 320:(a + 1) * 320], start=True, stop=True),
                          reads=[b_mle, b_sg], writes=[Pc[1][a]])
                E1, b_E1 = T("E1")
                E2, b_E2 = T("E2")
                E3, b_E3 = T("E3")
                Sx.op("act", lambda e, E1=E1, Pc=Pc: e.activation(s2(E1), p2(Pc), AF.Exp, scale=-EXPM05), reads=[Pc[1][0], Pc[1][1]], writes=[b_E1])
                Sx.op("act", lambda e, E2=E2, Pc=Pc: e.activation(s2(E2), p2(Pc), AF.Exp, scale=EXPM05), reads=[Pc[1][0], Pc[1][1]], writes=[b_E2])
                Sx.op("dve", lambda e, E3=E3, Pc=Pc, sg=sg: e.tensor_tensor(s2(E3), p2(Pc), s2(sg), ALU.subtract), reads=[Pc[1][0], Pc[1][1], b_sg], writes=[b_E3])
                Sx.op("act", lambda e, E3=E3: e.activation(E3[:], E3[:], AF.Exp, scale=-EXPM05), reads=[b_E3], writes=[b_E3])
                yield
                rt, b_rt = T("rt")
                at, b_at = T("at", n=2)
                bt, b_bt = T("bt", n=2)
                kt, b_kt = T("kt", n=2)
                Sx.op("dve", lambda e, rt=rt, r_=r_, E1=E1: e.tensor_tensor(rt[:], r_, E1[:], ALU.mult), reads=[b_h, b_E1], writes=[b_rt])
                Sx.op("dve", lambda e, at=at, kk=kk, E3=E3: e.scalar_tensor_tensor(at[:].bitcast(F32R), kk[:], -1.0, E3[:], ALU.mult, ALU.mult), reads=[b_kk, b_E3], writes=[b_at])
                Sx.op("dve", lambda e, bt=bt, bv=bv, E2=E2: e.tensor_tensor(bt[:].bitcast(F32R), bv[:], E2[:], ALU.mult), reads=[b_bv, b_E2], writes=[b_bt])
                Sx.op("dve", lambda e, kt=kt, k2=k2, E2=E2: e.tensor_tensor(kt[:].bitcast(F32R), k2[:], E2[:], ALU.mult), reads=[b_k2, b_E2], writes=[b_kt])

                def tr10(src, b_src, name):
                    P = prr.next()
                    for h in range(10):
                        Sx.op("pe", lambda e, P=P, h=h: e.transpose(ph(P, h), hs_(src, h), ident[:C, :C]),
                              reads=[b_src, b_ident], writes=[P[1][h // 5]])
                    return evac(P, name)
                atT, b_atT = tr10(at, b_at, "atT")
                yield
                btT, b_btT = tr10(bt, b_bt, "btT")
                yield
                ktT, b_ktT = tr10(kt, b_kt, "ktT")
                yield
                rtT, b_rtT = tr10(rt, b_rt, "rtT")
                yield
                E1T, b_E1T = tr10(E1, b_E1, "E1T")
                yield
                def score(lt, b_l, rt_, b_r, mask, b_mask, name):
                    P = mm10(lambda h: hs_(lt, h), lambda h: hs_(rt_, h), [b_l, b_r])
                    return evac(P, name, mask=mask, b_mask=b_mask)
                Mt, b_Mt = score(btT, b_btT, atT, b_atT, m_lt, b_mlt, "Nt")
                yield
                M, b_M = score(atT, b_atT, btT, b_btT, m_gt, b_mgt, "Nn")
                yield
                AakT, b_AakT = score(ktT, b_ktT, atT, b_atT, m_lt, b_mlt, "AakT")
                yield
                ArbT, b_ArbT = score(btT, b_btT, rtT, b_rtT, m_le, b_mle, "ArbT")
                yield
                ArkT, b_ArkT = score(ktT, b_ktT, rtT, b_rtT, m_le, b_mle, "ArkT")
                yield
                Tt, b_Tt = T("Tt")
                Sx.op("dve", lambda e, Tt=Tt, Mt=Mt: e.tensor_tensor(Tt[:].bitcast(F32R), Mt[:], i64[:], ALU.add), reads=[b_Mt, b_i64], writes=[b_Tt])
                for j in range(5):
                    P = mm10(lambda h: hs_(Mt, h), lambda h: hs_(M, h), [b_Mt, b_M])
                    M2, b_M2 = evac(P, "M2_%d" % (j % 2), eng="act")
                    yield
                    if j < 4:
                        P = mm10(lambda h: hs_(M, h), lambda h: hs_(Mt, h), [b_Mt, b_M])
                        Mt2, b_Mt2 = evac(P, "Mt2_%d" % (j % 2), eng="dve")
                    P = mm10(lambda h: hs_(M2, h), lambda h: hs_(Tt, h), [b_M2, b_Tt])
                    Sx.op("dve", lambda e, Tt=Tt, P=P: e.tensor_tensor(s2(Tt).bitcast(F32R), p2(P), s2(Tt), ALU.add), reads=[P[1][0], P[1][1], b_Tt], writes=[b_Tt])
                    yield
                    M, b_M = M2, b_M2
                    if j < 4:
                        Mt, b_Mt = Mt2, b_Mt2
                P = mm10(lambda h: hs_(AakT, h), vh, [b_AakT, b_h])
                AakV, b_AakV = evac(P, "AakV")
                yield
                P = mm10(lambda h: hs_(Tt, h), lambda h: hs_(AakV, h), [b_Tt, b_AakV])
                W2, b_W2 = evac(P, "W2", eng="dve", r32=False)
                yield
                P = mm10(lambda h: hs_(at, h), lambda h: hs_(Tt, h), [b_at, b_Tt])
                W1T, b_W1T = evac(P, "W1T")
                yield
                yield "TAIL"
                sq, b_sq = T("sq_t")
                st, b_st = T("st_t", (C, 40))
                P = mm10(lambda h: hs_(W1T, h), lambda h: hs_(H, h), [b_W1T, b_H])
                U, b_U = evac(P, "U", add=W2, b_add=b_W2)
                yield
                Py = mm10(lambda h: [hs_(rtT, h), hs_(ArbT, h), hs_(ArkT, h)], lambda h: [hs_(H, h), hs_(U, h), vh(h)],
                          [b_rtT, b_ArbT, b_ArkT, b_H, b_U, b_h])
                y_, b_y = evac(Py, "y", r32=False)
                yield
                Ph = mm10(lambda h: [hs_(bt, h), hs_(kt, h)], lambda h: [hs_(U, h), vh(h)], [b_bt, b_kt, b_U, b_h, b_H])
                Sx.op("dve", lambda e, Ph=Ph: e.tensor_tensor(s2(H).bitcast(F32R), p2(Ph), s2(H), ALU.add), reads=[Ph[1][0], Ph[1][1], b_H], writes=[b_H])
                Sx.op("dve", lambda e, E1T=E1T: e.tensor_tensor(v10(H).bitcast(F32R), v10(H), v10(E1T)[:, :, 63:64].broadcast_to([64, 10, 64]), ALU.mult),
                      reads=[b_H, b_E1T], writes=[b_H])
                yield
                Sx.op("dve", lambda e, st=st, y_=y_: e.reduce_sum(st[:, 0:10], v10(y_), AX.X), reads=[b_y], writes=[b_st])
                Sx.op("dve", lambda e, st=st: e.tensor_scalar(st[:, 0:10], st[:, 0:10], 1.0 / 64, None, ALU.mult), reads=[b_st], writes=[b_st])
                Sx.op("dve", lambda e, st=st, y_=y_: e.tensor_tensor(v10(y_), v10(y_), st[:, 0:10].unsqueeze(2).broadcast_to([C, 10, 64]), ALU.subtract),
                      reads=[b_y, b_st], writes=[b_y])
                Sx.op("dve", lambda e, sq=sq, y_=y_: e.tensor_tensor(sq[:], y_[:], y_[:], ALU.mult), reads=[b_y], writes=[b_sq])
                Sx.op("dve", lambda e, st=st, sq=sq: e.reduce_sum(st[:, 10:20], v10(sq), AX.X), reads=[b_sq], writes=[b_st])
                Sx.op("dve", lambda e, st=st: e.tensor_scalar(st[:, 10:20], st[:, 10:20], 1.0 / 64, 64e-5, ALU.mult, ALU.add), reads=[b_st], writes=[b_st])
                Sx.op("act", lambda e, st=st: e.activation(st[:, 20:30], st[:, 10:20], AF.Sqrt), reads=[b_st], writes=[b_st])
                Sx.op("dve", lambda e, st=st: e.reciprocal(st[:, 30:40], st[:, 20:30]), reads=[b_st], writes=[b_st])
                yield
                Sx.op("dve", lambda e, st=st, y_=y_: e.tensor_tensor(v10(y_), v10(y_), st[:, 30:40].unsqueeze(2).broadcast_to([C, 10, 64]), ALU.mult),
                      reads=[b_y, b_st], writes=[b_y])
                Sx.op("dve", lambda e, y_=y_: e.tensor_tensor(y_[:], y_[:], gw_b[:], ALU.mult), reads=[b_y, b_gw], writes=[b_y])
                Sx.op("dve", lambda e, y_=y_: e.tensor_tensor(y_[:], y_[:], gb_b[:], ALU.add), reads=[b_y, b_gb], writes=[b_y])
                yield
                Sx.op("dve", lambda e, sq=sq, r_=r_, k2=k2: e.tensor_tensor(sq[:], r_, k2[:], ALU.mult), reads=[b_h, b_k2], writes=[b_sq])
                Sx.op("dve", lambda e, sq=sq: e.tensor_tensor(sq[:], sq[:], rk_b[:], ALU.mult), reads=[b_sq, b_rk], writes=[b_sq])
                Sx.op("dve", lambda e, st=st, sq=sq: e.reduce_sum(st[:, 0:10], v10(sq), AX.X), reads=[b_sq], writes=[b_st])
                Sx.op("dve", lambda e, sq=sq, st=st, v_=v_: e.tensor_tensor(v10(sq), v_.rearrange("p (h c) -> p h c", c=64), st[:, 0:10].unsqueeze(2).broadcast_to([C, 10, 64]), ALU.mult),
                      reads=[b_h, b_st], writes=[b_sq])
                Sx.op("dve", lambda e, y_=y_, sq=sq: e.tensor_tensor(y_[:], y_[:], sq[:], ALU.add), reads=[b_y, b_sq], writes=[b_y])
                Sx.op("dve", lambda e, y_=y_, g_=g_: e.tensor_tensor(y_[:], y_[:], g_[:], ALU.mult), reads=[b_y, b_g], writes=[b_y])
                Sx.dma("sp", mix[c0:c0 + C, FOX_W:FOX_W + RWKV_W], y_[:], reads=[b_y], writes=[B["mix"]])

            def adv(g):
                try:
                    return next(g)
                except StopIteration:
                    return "END"
            cur_g = chunk(0)
            while adv(cur_g) != "TAIL":
                pass
            for ci in range(NCH):
                nxt = chunk(ci + 1) if ci + 1 < NCH else None
                nxt_done = nxt is None
                tail_done = False
                while not (tail_done and nxt_done):
                    for _ in range(4):
                        if not nxt_done and adv(nxt) == "TAIL":
                            nxt_done = True
                    if not tail_done and adv(cur_g) == "END":
                        tail_done = True
                cur_g = nxt
            Sx.barrier()
    return rwkv


_CACHE = {}


def _prep_params(inputs, L):
    out = {}
    for k, shp in PARAM_SHAPES.items():
        a = np.asarray(inputs[k], dtype=np.float32)[:L]
        out[k] = np.ascontiguousarray(a.reshape([L] + shp))
    return out


def kernel(**inputs):
    x = np.asarray(inputs["x"], dtype=np.float32)
    Bsz, S, _ = x.shape
    L = np.asarray(inputs["w_in"]).shape[0]
    key = (S, L)
    if key not in _CACHE:
        _CACHE[key] = build_program(S, L)
    nc = _CACHE[key]
    shared = _prep_params(inputs, L)
    shared.update(host_consts())
    shared.update(ret_consts(S))
    n = 8
    in_maps = []
    for c in range(n):
        m = dict(shared)
        m["x"] = np.ascontiguousarray(x[c % Bsz])
        in_maps.append(m)
    res = run_bass_kernel_spmd(nc, in_maps, core_ids=list(range(n)))
    return np.stack([res.results[b]["y"] for b in range(Bsz)], axis=0).astype(np.float32)
```

```python
import numpy as np
from contextlib import ExitStack
import concourse.bass as bass
import concourse.mybir as mybir
from concourse.bass_utils import run_bass_kernel_spmd

F32 = mybir.dt.float32
BF16 = mybir.dt.bfloat16
F32R = mybir.dt.float32r
AF = mybir.ActivationFunctionType
ALU = mybir.AluOpType
AX = mybir.AxisListType

D = 2048
DEPTH = 4
FOX_W, RWKV_W, RET_W = 768, 640, 640
FH, RH, TH = 6, 10, 5
P_IN = 7046
OFF_FQ, OFF_FK, OFF_FV, OFF_FF = 0, 768, 1536, 2304
OFF_RW = 2310
OFF_RT = 4486
RWKV_COLS = 2176
DFF = 8192
ALPHA = (2 * DEPTH) ** 0.25
LN_EPS = 1e-5
EXPM05 = float(np.exp(-0.5))

EPOCH = 30000
NDMA = 40


class Buf:
    __slots__ = ("w", "r", "name")

    def __init__(self, name=""):
        self.w = None
        self.r = {}
        self.name = name


class Sched:
    ENGS = ("pe", "act", "dve", "pool", "sp")

    def __init__(self, nc, stack, n_epochs=None):
        n_epochs = n_epochs or {"pe": 26, "act": 8, "dve": 8, "pool": 1}
        self.nc = nc
        self.ops = {e: [] for e in self.ENGS}
        self.cnt = {e: 0 for e in self.ENGS}
        self.sems = {e: [stack.enter_context(nc.semaphore(f"s_{e}_{i}")) for i in range(n_epochs[e])]
                     for e in ("pe", "act", "dve", "pool")}
        self.dsem = [stack.enter_context(nc.semaphore(f"s_dma_{i}")) for i in range(NDMA)]
        self.dval = [0] * NDMA
        self.dnext = 0
        self.waited = {e: {} for e in self.ENGS}
        self.ninstr = 0

    def _need_wait(self, eng, tok):
        if tok is None:
            return None
        if tok[0] == "e":
            _, f, n = tok
            if f == eng and eng == "pe":
                return None
            key = ("e", f)
            val = n
        else:
            _, k, val = tok
            key = ("d", k)
        if self.waited[eng].get(key, 0) >= val:
            return None
        self.waited[eng][key] = val
        return tok

    def _emit_wait(self, engine, tok):
        if tok[0] == "e":
            _, f, n = tok
            ep = (n - 1) // EPOCH
            engine.wait_ge(self.sems[f][ep], n - ep * EPOCH)
        else:
            _, k, val = tok
            engine.wait_ge(self.dsem[k], val)

    def _deps(self, eng, reads, writes):
        toks = []
        for b in reads:
            toks.append(b.w)
        for b in writes:
            toks.append(b.w)
            toks.extend(b.r.values())
        toks = [t for t in toks if t is not None]
        toks.sort(key=lambda t: -t[2])
        out = []
        for t in toks:
            w = self._need_wait(eng, t)
            if w is not None:
                out.append(w)
        return out

    def _mark(self, tok, key, reads, writes):
        for b in writes:
            b.w = tok
            b.r = {}
        for b in reads:
            if b.w is not tok:
                b.r[key] = tok

    def op(self, eng, fn, reads=(), writes=()):
        waits = self._deps(eng, reads, writes)
        self.cnt[eng] += 1
        n = self.cnt[eng]
        ep = (n - 1) // EPOCH
        sem = self.sems[eng][ep]
        tok = ("e", eng, n)

        def run(engine, waits=waits, fn=fn, sem=sem):
            for w in waits:
                self._emit_wait(engine, w)
            fn(engine).then_inc(sem, 1)
        self.ops[eng].append(run)
        self._mark(tok, ("e", eng), reads, writes)
        self.ninstr += 1 + len(waits)
        return tok

    def dma(self, q, out, in_, reads=(), writes=()):
        k = self.dnext
        self.dnext = (self.dnext + 1) % NDMA
        prev = self.dval[k]
        waits = self._deps(q, reads, writes)
        if prev > 0:
            w = self._need_wait(q, ("d", k, prev))
            if w is not None:
                waits.append(w)
        self.dval[k] = prev + 16
        tok = ("d", k, prev + 16)
        sem = self.dsem[k]

        def run(engine, waits=waits, sem=sem, out=out, in_=in_):
            for w in waits:
                self._emit_wait(engine, w)
            engine.dma_start(out=out, in_=in_).then_inc(sem, 16)
        self.ops[q].append(run)
        self._mark(tok, ("d", k), reads, writes)
        self.ninstr += 1 + len(waits)
        return tok

    def barrier(self):
        toks = [("e", e, self.cnt[e]) for e in ("pe", "act", "dve", "pool") if self.cnt[e] > 0]
        toks += [("d", k, self.dval[k]) for k in range(NDMA) if self.dval[k] > 0]
        for q in self.ENGS:
            waits = []
            for t in toks:
                if t[0] == "e" and t[1] == q:
                    continue
                w = self._need_wait(q, t)
                if w is not None:
                    waits.append(w)

            def run(engine, waits=waits):
                for w in waits:
                    self._emit_wait(engine, w)
            self.ops[q].append(run)
            self.ninstr += len(waits)

    def replay(self):
        nc = self.nc
        with nc.Block() as block:
            @block.sync
            def _(e):
                for f in self.ops["sp"]:
                    f(e)

            @block.scalar
            def _(e):
                for f in self.ops["act"]:
                    f(e)

            @block.vector
            def _(e):
                for f in self.ops["dve"]:
                    f(e)

            @block.gpsimd
            def _(e):
                for f in self.ops["pool"]:
                    f(e)

            @block.tensor
            def _(e):
                for f in self.ops["pe"]:
                    f(e)


class RR:
    def __init__(self, items):
        self.items = items
        self.i = 0

    def next(self):
        it = self.items[self.i % len(self.items)]
        self.i += 1
        return it


def host_consts():
    c = {}
    c["ident"] = np.eye(128, dtype=np.float32)
    p = np.arange(128)[:, None]
    f = np.arange(128)[None, :]
    c["m_le"] = (p <= f).astype(np.float32)
    p64 = np.arange(64)[:, None]
    f64 = np.arange(64)[None, :]
    rep = lambda m: np.ascontiguousarray(np.broadcast_to(m[:, None, :], (64, 10, 64))).reshape(64, 640).astype(np.float32)
    c["m64_lt"] = rep((p64 < f64).astype(np.float32))
    c["m64_le"] = rep((p64 <= f64).astype(np.float32))
    c["m64_gt"] = rep((p64 > f64).astype(np.float32))
    c["i64"] = rep(np.eye(64, dtype=np.float32))
    sel = np.zeros((128, 128), np.float32)
    sel[127, :] = 1.0
    c["sel_last"] = sel
    return c


def ret_consts(S):
    H, C, dh = TH, 128, 128
    log_g = np.log(1.0 - 2.0 ** (-5.0 - np.arange(H, dtype=np.float64)))
    pos = np.arange(C, dtype=np.float64)
    c = {}
    rel = pos[None, :] - pos[:, None]
    dec = np.where(rel[:, None, :] >= 0, np.exp(log_g[None, :, None] * np.maximum(rel[:, None, :], 0.0)), 0.0)
    c["ret_dec"] = dec.reshape(C, H * C).astype(np.float32)
    qd = np.exp(log_g[:, None] * (pos[None, :] + 1.0))
    c["ret_qd"] = np.ascontiguousarray(np.broadcast_to(qd[None], (128, H, C))).reshape(128, H * C).astype(np.float32)
    kd = np.exp(log_g[None, :] * (C - 1.0 - pos[:, None])) * dh ** -0.5
    c["ret_kd"] = kd.astype(np.float32)
    c["ret_cd"] = np.ascontiguousarray(np.broadcast_to(np.exp(log_g * C)[None, :], (128, H))).astype(np.float32)
    half = 64
    inv = 1.0 / (10000.0 ** (np.arange(half, dtype=np.float32) / half))
    ang = np.arange(S, dtype=np.float32)[:, None] * inv[None, :]
    c["ret_cos"] = np.cos(ang).astype(np.float32)
    c["ret_sin"] = np.sin(ang).astype(np.float32)
    return c


CONST_SHAPES = {
    "ident": [128, 128], "m_le": [128, 128], "m64_lt": [64, 640], "m64_le": [64, 640],
    "m64_gt": [64, 640], "i64": [64, 640], "sel_last": [128, 128],
    "ret_dec": [128, 640], "ret_qd": [128, 640], "ret_kd": [128, 5], "ret_cd": [128, 5],
}

PARAM_SHAPES = {
    "w_in": [D, P_IN], "fox_forget_bias": [FH, 1], "rwkv_mu": [1, RWKV_COLS], "rwkv_w0": [1, 640],
    "rwkv_w_up": [64, 640], "rwkv_a0": [1, 640], "rwkv_a_up": [64, 640], "rwkv_g_up": [128, 640],
    "rwkv_k_k": [1, 640], "rwkv_k_a": [1, 640], "rwkv_r_k": [1, 640], "rwkv_gn_w": [1, 640],
    "rwkv_gn_b": [1, 640], "ret_gn_w": [1, 640], "ret_gn_b": [1, 640], "w_out": [D, D],
    "ln1_g": [1, D], "ln1_b": [1, D], "w_up": [D, DFF], "w_down": [DFF, D], "ln2_g": [1, D], "ln2_b": [1, D],
}


def build_program(S, L, dbg=None, skip=()):
    NT = S // 128
    nc = bass.Bass("TRN2", target_bir_lowering=False)
    din = lambda name, shape, dt=F32: nc.dram_tensor(name, list(shape), dt, kind="ExternalInput").ap()
    x_in = din("x", [S, D])
    prm = {k: din(k, [L] + v) for k, v in PARAM_SHAPES.items()}
    cst = {k: din(k, v) for k, v in CONST_SHAPES.items()}
    cst["ret_cos"] = din("ret_cos", [S, 64])
    cst["ret_sin"] = din("ret_sin", [S, 64])
    y_out = nc.dram_tensor("y", [S, D], F32, kind="ExternalOutput").ap()
    dscr = lambda name, shape, dt: nc.dram_tensor(name, list(shape), dt).ap()
    xres = [dscr("xresA", [S, D], F32), dscr("xresB", [S, D], F32)]
    xT = dscr("xT", [D, S], BF16)
    x1 = dscr("x1", [S, D], F32)
    x1T = dscr("x1T", [D, S], BF16)
    qkT = dscr("qkT", [2 * FOX_W, S], BF16)
    fT = dscr("fT", [FH, S], F32)
    hproj = dscr("hproj", [S, P_IN - OFF_FV], F32)
    mix = dscr("mix", [S, D], F32)
    mixT = dscr("mixT", [D, S], BF16)
    ytmp = dscr("ytmp", [S, D], F32)
    hidT = dscr("hidT", [DFF, S], BF16)
    dbg_out = {}
    if dbg:
        for name, shape in (("hproj", [S, P_IN - OFF_FV]), ("mix", [S, D]), ("x1", [S, D]), ("fT", [FH, S])):
            dbg_out[name] = nc.dram_tensor("dbg_" + name, list(shape), F32, kind="ExternalOutput").ap()

    B = {n: Buf(n) for n in ["xresA", "xresB", "xT", "x1", "x1T", "qkT", "fT", "hproj", "mix", "mixT",
                              "ytmp", "hidT", "y"]}

    with ExitStack() as st:
        Sx = Sched(nc, st)
        _id = [0]

        def sb(shape, dt, name=None):
            _id[0] += 1
            t = st.enter_context(nc.sbuf_tensor(f"{name or 't'}_{_id[0]}", list(shape), dt))
            return t, Buf(name or "t")

        PS = []
        for i in range(4):
            t = st.enter_context(nc.psum_tensor(f"ps{i}", [128, 2, 512], F32))
            PS.append((t, [Buf(f"ps{i}a"), Buf(f"ps{i}b")]))
        banks = [(PS[i][0][:, a, :], PS[i][1][a]) for i in range(4) for a in range(2)]

        ident, b_ident = sb([128, 128], F32, "ident")
        Sx.dma("sp", ident[:], cst["ident"], writes=[b_ident])
        identb, b_identb = sb([128, 128], BF16, "identb")
        Sx.op("dve", lambda e: e.tensor_copy(identb[:], ident[:]), reads=[b_ident], writes=[b_identb])

        def transpose_store(src, b_src, ncols, dstT, b_dstT, row0, tok0, ntok, trbank, stg):
            for c0 in range(0, ncols, 512):
                cw = min(512, ncols - c0)
                pb, b_pb = trbank.next()
                nblk = (cw + 127) // 128
                for bi in range(nblk):
                    bw = min(128, cw - bi * 128)
                    Sx.op("pe", lambda e, pb=pb, bi=bi, bw=bw, c0=c0: e.transpose(
                        pb[:bw, bi * 128:bi * 128 + ntok], src[:ntok, c0 + bi * 128:c0 + bi * 128 + bw], ident[:ntok, :ntok]),
                        reads=[b_src, b_ident], writes=[b_pb])
                sg, b_sg = stg.next()
                if cw % 128 == 0:
                    Sx.op("act", lambda e, pb=pb, sg=sg, nblk=nblk: e.copy(
                        sg[:, :nblk, :ntok], pb[:, :nblk * 128].rearrange("p (b t) -> p b t", t=128)[:, :, :ntok]),
                        reads=[b_pb], writes=[b_sg])
                    Sx.dma("sp", dstT[row0 + c0:row0 + c0 + cw, tok0:tok0 + ntok].rearrange("(b p) t -> p b t", p=128),
                           sg[:, :nblk, :ntok], reads=[b_sg], writes=[b_dstT])
                else:
                    for bi in range(nblk):
                        bw = min(128, cw - bi * 128)
                        Sx.op("act", lambda e, pb=pb, sg=sg, bi=bi, bw=bw: e.copy(
                            sg[:bw, bi, :ntok], pb[:bw, bi * 128:bi * 128 + ntok]), reads=[b_pb], writes=[b_sg])
                        Sx.dma("sp", dstT[row0 + c0 + bi * 128:row0 + c0 + bi * 128 + bw, tok0:tok0 + ntok],
                               sg[:bw, bi, :ntok], reads=[b_sg], writes=[b_dstT])

        def gemm_tok(aT, b_aT, K, W, n0, n1, evac, cb=None):
            KC = K // 128
            cb = cb or (1024 if KC <= 16 else 512)
            with ExitStack() as ls:
                def lsb(shape, dt, name):
                    _id[0] += 1
                    return ls.enter_context(nc.sbuf_tensor(f"{name}_{_id[0]}", list(shape), dt)), Buf(name)
                tb = min(S, 512 if KC <= 16 else 256)
                wb = RR([lsb([128, KC, cb], BF16, "gw") for _ in range(2)])
                ab = RR([lsb([128, KC, tb], BF16, "ga") for _ in range(2)])
                pbk = RR(banks[0:4])
                Wv = W.rearrange("(kc p) n -> p kc n", p=128)
                aTv = aT.rearrange("(kc p) s -> p kc s", p=128)
                for c0 in range(n0, n1, cb):
                    cw = min(cb, n1 - c0)
                    wt, b_wt = wb.next()
                    kstep = max(1, min(KC, 2048 // max(1, (cw + 511) // 512) // 128))
                    for k0 in range(0, KC, kstep):
                        Sx.dma("pool", wt[:, k0:k0 + kstep, :cw], Wv[:, k0:k0 + kstep, c0:c0 + cw], writes=[b_wt])
                    for t0 in range(0, S, tb):
                        at, b_at = ab.next()
                        ksp = max(1, KC // 4)
                        for k0 in range(0, KC, ksp):
                            Sx.dma("sp", at[:, k0:k0 + ksp, :], aTv[:, k0:k0 + ksp, t0:t0 + tb], reads=[b_aT], writes=[b_at])
                        for ts_ in range(tb // 128):
                            ti = t0 // 128 + ts_
                            for s0 in range(0, cw, 512):
                                sw = min(512, cw - s0)
                                pb, b_pb = pbk.next()
                                for kc in range(KC):
                                    Sx.op("pe", lambda e, pb=pb, at=at, wt=wt, kc=kc, s0=s0, sw=sw, ts_=ts_: e.matmul(
                                        pb[:, :sw], at[:, kc, ts_ * 128:(ts_ + 1) * 128], wt[:, kc, s0:s0 + sw], start=(kc == 0), stop=(kc == KC - 1)),
                                        reads=[b_at, b_wt], writes=[b_pb])
                                evac(ti, c0 + s0, sw, pb, b_pb)
                Sx.barrier()

        def gemm_feat(aT, b_aT, K, W, n0, n1, evac, TB=4096):
            KC = K // 128
            TB = min(TB, S)
            with ExitStack() as ls:
                def lsb(shape, dt, name):
                    _id[0] += 1
                    return ls.enter_context(nc.sbuf_tensor(f"{name}_{_id[0]}", list(shape), dt)), Buf(name)
                ablk, b_ablk = lsb([128, KC, TB], BF16, "fa")
                wb = RR([lsb([128, KC, 512], BF16, "fw") for _ in range(2)])
                pbk = RR(banks[0:4])
                Wv = W.rearrange("(kc p) n -> p kc n", p=128)
                aTv = aT.rearrange("(kc p) s -> p kc s", p=128)
                for t0 in range(0, S, TB):
                    for k0 in range(0, KC, 2):
                        Sx.dma("sp", ablk[:, k0:k0 + 2, :], aTv[:, k0:k0 + 2, t0:t0 + TB], reads=[b_aT], writes=[b_ablk])
                    for g0 in range(n0, n1, 512):
                        gw = min(512, n1 - g0)
                        wt, b_wt = wb.next()
                        for k0 in range(0, KC, 4):
                            Sx.dma("pool", wt[:, k0:k0 + 4, :gw], Wv[:, k0:k0 + 4, g0:g0 + gw], writes=[b_wt])
                        for r0 in range(g0, g0 + gw, 128):
                            rw = min(128, g0 + gw - r0)
                            for t4 in range(0, TB, 512):
                                tw = min(512, TB - t4)
                                pb, b_pb = pbk.next()
                                for kc in range(KC):
                                    Sx.op("pe", lambda e, pb=pb, wt=wt, kc=kc, rw=rw, t4=t4, tw=tw, ro=r0 - g0: e.matmul(
                                        pb[:rw, :tw], wt[:, kc, ro:ro + rw], ablk[:, kc, t4:t4 + tw], start=(kc == 0), stop=(kc == KC - 1)),
                                        reads=[b_ablk, b_wt], writes=[b_pb])
                                evac(r0, rw, t0 + t4, tw, pb, b_pb)
                Sx.barrier()

        def ln_pass(xold, b_xold, yadd, b_yadd, gam, bet, xnew, b_xnew, xnewT, b_xnewT):
            with ExitStack() as ls:
                def lsb(shape, dt, name):
                    _id[0] += 1
                    return ls.enter_context(nc.sbuf_tensor(f"{name}_{_id[0]}", list(shape), dt)), Buf(name)
                g_b, b_g = lsb([128, D], F32, "lng")
                be_b, b_be = lsb([128, D], F32, "lnb")
                Sx.dma("sp", g_b[:], gam.broadcast_to([128, D]), writes=[b_g])
                Sx.dma("sp", be_b[:], bet.broadcast_to([128, D]), writes=[b_be])
                LK = 4 if NT % 4 == 0 else 2
                xo = RR([lsb([128, D], F32, "lx") for _ in range(LK)])
                ya = RR([lsb([128, D], F32, "ly") for _ in range(LK)])
                sq = RR([lsb([128, D], F32, "lsq") for _ in range(LK)])
                stt = RR([lsb([128, 8], F32, "lst") for _ in range(LK)])
                trb = RR(banks[0:8])
                stg = RR([lsb([128, 4, 128], BF16, "lstg") for _ in range(4)])
                def tile_gen(ti):
                    r = slice(ti * 128, (ti + 1) * 128)
                    xt, b_xt = xo.next()
                    yt, b_yt = ya.next()
                    qt, b_qt = sq.next()
                    s_, b_s = stt.next()
                    Sx.dma("sp", xt[:], xold[r, :], reads=[b_xold], writes=[b_xt])
                    yield
                    Sx.dma("sp", yt[:], yadd[r, :], reads=[b_yadd], writes=[b_yt])
                    yield
                    Sx.op("dve", lambda e, s_=s_: e.memset(s_[:], 0.0), writes=[b_s])
                    yield
                    Sx.op("dve", lambda e, xt=xt, yt=yt: e.scalar_tensor_tensor(yt[:], xt[:], ALPHA, yt[:], ALU.mult, ALU.add),
                          reads=[b_xt, b_yt], writes=[b_yt])
                    yield
                    Sx.op("act", lambda e, yt=yt, qt=qt, s_=s_: e.activation(qt[:], yt[:], AF.Identity, accum_out=s_[:, 0:1]),
                          reads=[b_yt], writes=[b_qt, b_s])
                    yield
                    Sx.op("dve", lambda e, s_=s_: e.tensor_scalar(s_[:, 1:2], s_[:, 0:1], -1.0 / D, None, ALU.mult),
                          reads=[b_s], writes=[b_s])
                    Sx.op("act", lambda e, yt=yt, qt=qt, s_=s_: e.activation(qt[:], yt[:], AF.Square, bias=s_[:, 1:2], scale=1.0, accum_out=s_[:, 2:3]),
                          reads=[b_yt, b_s], writes=[b_qt, b_s])
                    yield
                    Sx.op("dve", lambda e, s_=s_: e.tensor_scalar(s_[:, 3:4], s_[:, 2:3], 1.0 / D, LN_EPS, ALU.mult, ALU.add),
                          reads=[b_s], writes=[b_s])
                    yield
                    Sx.op("act", lambda e, s_=s_: e.activation(s_[:, 4:5], s_[:, 3:4], AF.Sqrt), reads=[b_s], writes=[b_s])
                    yield
                    Sx.op("dve", lambda e, s_=s_: e.reciprocal(s_[:, 5:6], s_[:, 4:5]), reads=[b_s], writes=[b_s])
                    yield
                    Sx.op("dve", lambda e, s_=s_: e.tensor_tensor(s_[:, 6:7], s_[:, 1:2], s_[:, 5:6], ALU.mult), reads=[b_s], writes=[b_s])
                    yield
                    Sx.op("act", lambda e, yt=yt, s_=s_: e.activation(yt[:], yt[:], AF.Identity, bias=s_[:, 6:7], scale=s_[:, 5:6]),
                          reads=[b_yt, b_s], writes=[b_yt])
                    yield
                    Sx.op("dve", lambda e, yt=yt: e.tensor_tensor(yt[:], yt[:], g_b[:], ALU.mult), reads=[b_yt, b_g], writes=[b_yt])
                    yield
                    Sx.op("dve", lambda e, yt=yt: e.tensor_tensor(yt[:], yt[:], be_b[:], ALU.add), reads=[b_yt, b_be], writes=[b_yt])
                    yield
                    Sx.dma("sp", xnew[r, :], yt[:], reads=[b_yt], writes=[b_xnew])
                    yield
                    if xnewT is not None:
                        transpose_store(yt, b_yt, D, xnewT, b_xnewT, 0, ti * 128, 128, trb, stg)

                for t2 in range(0, NT, LK):
                    gens = [tile_gen(t2 + q_) for q_ in range(LK) if t2 + q_ < NT]
                    live = list(gens)
                    while live:
                        for g in list(live):
                            try:
                                next(g)
                            except StopIteration:
                                live.remove(g)
                Sx.barrier()

        def input_transpose(xsrc, b_xsrc):
            with ExitStack() as ls:
                def lsb(shape, dt, name):
                    _id[0] += 1
                    return ls.enter_context(nc.sbuf_tensor(f"{name}_{_id[0]}", list(shape), dt)), Buf(name)
                xo = RR([lsb([128, D], F32, "ix") for _ in range(4)])
                trb = RR(banks[0:8])
                stg = RR([lsb([128, 4, 128], BF16, "istg") for _ in range(6)])
                for ti in range(NT):
                    xt, b_xt = xo.next()
                    Sx.dma("sp", xt[:], xsrc[ti * 128:(ti + 1) * 128, :], reads=[b_xsrc], writes=[b_xt])
                    transpose_store(xt, b_xt, D, xT, B["xT"], 0, ti * 128, 128, trb, stg)
                Sx.barrier()

        def in_proj(l):
            W = prm["w_in"][l]
            with ExitStack() as ls:
                def lsb(shape, dt, name):
                    _id[0] += 1
                    return ls.enter_context(nc.sbuf_tensor(f"{name}_{_id[0]}", list(shape), dt)), Buf(name)
                stg = RR([lsb([128, 512], BF16, "pstg") for _ in range(3)])
                stf = RR([lsb([128, 512], F32, "pstf") for _ in range(3)])
                fb, b_fb = lsb([FH, 1], F32, "fbias")
                Sx.dma("sp", fb[:], prm["fox_forget_bias"][l], writes=[b_fb])

                def evac_qk(r0, rw, t0, tw, pb, b_pb):
                    sg, b_sg = stg.next()
                    sc = 128 ** -0.5 if r0 < FOX_W else 1.0
                    Sx.op("act", lambda e: e.activation(sg[:rw, :tw], pb[:rw, :tw], AF.Copy, scale=sc),
                          reads=[b_pb], writes=[b_sg])
                    Sx.dma("sp", qkT[r0:r0 + rw, t0:t0 + tw], sg[:rw, :tw], reads=[b_sg], writes=[B["qkT"]])
                gemm_feat(xT, B["xT"], D, W, 0, 2 * FOX_W, evac_qk)

                def evac_f(r0, rw, t0, tw, pb, b_pb):
                    sg, b_sg = stf.next()
                    Sx.op("dve", lambda e: e.tensor_scalar(sg[:rw, :tw], pb[:rw, :tw], fb[:, 0:1], None, ALU.add),
                          reads=[b_pb, b_fb], writes=[b_sg])
                    Sx.dma("sp", fT[:, t0:t0 + tw], sg[:rw, :tw], reads=[b_sg], writes=[B["fT"]])
                gemm_feat(xT, B["xT"], D, W, OFF_FF, OFF_FF + FH, evac_f)

                def evac_tok(ti, c0, cw, pb, b_pb):
                    sg, b_sg = stf.next()
                    Sx.op("act", lambda e: e.copy(sg[:, :cw], pb[:, :cw]), reads=[b_pb], writes=[b_sg])
                    Sx.dma("sp", hproj[ti * 128:(ti + 1) * 128, c0 - OFF_FV:c0 - OFF_FV + cw], sg[:, :cw],
                           reads=[b_sg], writes=[B["hproj"]])
                gemm_tok(xT, B["xT"], D, W, OFF_FV, P_IN, evac_tok)

        def fox(l):
            with ExitStack() as ls:
                def lsb(shape, dt, name):
                    _id[0] += 1
                    return ls.enter_context(nc.sbuf_tensor(f"{name}_{_id[0]}", list(shape), dt)), Buf(name)
                ls2 = ExitStack()
                def lsb2(shape, dt, name):
                    _id[0] += 1
                    return ls2.enter_context(nc.sbuf_tensor(f"{name}_{_id[0]}", list(shape), dt)), Buf(name)
                ctok, b_ctok = lsb([128, NT, FH], F32, "ctok")
                crefb, b_crefb = lsb([128, NT, FH], F32, "crefb")
                sel, b_sel = lsb([128, 128], F32, "sel")
                mle, b_mle = lsb([128, 128], BF16, "mle")
                mlef, b_mlef = lsb([128, 128], F32, "mlef")
                fa, b_fa = lsb2([FH, S], F32, "fa")
                fbuf, b_fbuf = lsb2([FH, S], F32, "fb")
                Sx.dma("sp", fa[:], fT, reads=[B["fT"]], writes=[b_fa])
                Sx.op("act", lambda e: e.activation(fa[:], fa[:], AF.Sigmoid), reads=[b_fa], writes=[b_fa])
                Sx.op("act", lambda e: e.activation(fa[:], fa[:], AF.Ln), reads=[b_fa], writes=[b_fa])
                cur, b_cur, oth, b_oth = fa, b_fa, fbuf, b_fbuf
                sh = 1
                while sh < S:
                    Sx.op("dve", lambda e, cur=cur, oth=oth, sh=sh: e.tensor_copy(oth[:, 0:sh], cur[:, 0:sh]),
                          reads=[b_cur], writes=[b_oth])
                    Sx.op("dve", lambda e, cur=cur, oth=oth, sh=sh: e.tensor_tensor(oth[:, sh:S], cur[:, sh:S], cur[:, 0:S - sh], ALU.add),
                          reads=[b_cur], writes=[b_oth])
                    cur, b_cur, oth, b_oth = oth, b_oth, cur, b_cur
                    sh *= 2
                pb, b_pb = banks[4]
                for j in range(NT):
                    Sx.op("pe", lambda e, j=j, cur=cur: e.transpose(pb[:, j * FH:(j + 1) * FH], cur[:, j * 128:(j + 1) * 128], ident[:FH, :FH]),
                          reads=[b_cur, b_ident], writes=[b_pb])
                Sx.op("dve", lambda e: e.tensor_copy(ctok[:], pb[:, :NT * FH].rearrange("p (j h) -> p j h", h=FH)),
                      reads=[b_pb], writes=[b_ctok])
                Sx.dma("sp", sel[:], cst["sel_last"], writes=[b_sel])
                pb2, b_pb2 = banks[5]
                Sx.op("pe", lambda e: e.matmul(pb2[:, :NT * FH], sel[:], ctok[:].rearrange("p j h -> p (j h)"), start=True, stop=True),
                      reads=[b_sel, b_ctok], writes=[b_pb2])
                Sx.op("dve", lambda e: e.tensor_copy(crefb[:], pb2[:, :NT * FH].rearrange("p (j h) -> p j h", h=FH)),
                      reads=[b_pb2], writes=[b_crefb])
                Sx.dma("sp", mlef[:], cst["m_le"], writes=[b_mlef])
                Sx.op("dve", lambda e: e.tensor_copy(mle[:], mlef[:]), reads=[b_mlef], writes=[b_mle])
                Sx.barrier()
                ls2.close()

                qh = RR([lsb([128, S], BF16, "qh") for _ in range(2)])
                kh = RR([lsb([128, S], BF16, "kh") for _ in range(2)])
                vh = RR([lsb([128, NT, 129], BF16, "vh") for _ in range(2)])
                bias_t = RR([lsb([128, NT, NT], F32, "fbias") for _ in range(2)])
                pT = RR([lsb([128, 128], BF16, "pT") for _ in range(4)])
                ot = RR([lsb([128, 128], F32, "ot") for _ in range(3)])
                rc = RR([lsb([128, 1], F32, "rc") for _ in range(3)])
                sbk = RR(banks[0:4])
                obk = RR(banks[6:8])
                hv = hproj.rearrange("(j p) c -> p j c", p=128)
                for h in range(FH):
                    qt, b_qt = qh.next()
                    kt, b_kt = kh.next()
                    vt, b_vt = vh.next()
                    bt, b_bt = bias_t.next()
                    Sx.dma("sp", qt[:], qkT[h * 128:(h + 1) * 128, :], reads=[B["qkT"]], writes=[b_qt])
                    Sx.dma("sp", kt[:], qkT[FOX_W + h * 128:FOX_W + (h + 1) * 128, :], reads=[B["qkT"]], writes=[b_kt])
                    Sx.op("dve", lambda e, vt=vt: e.memset(vt[:, :, 128:129], 1.0), writes=[b_vt])
                    for j0 in range(0, NT, 16):
                        j1 = min(NT, j0 + 16)
                        Sx.dma("pool", vt[:, j0:j1, 0:128], hv[:, j0:j1, h * 128:(h + 1) * 128], reads=[B["hproj"]], writes=[b_vt])
                    for j in range(NT):
                        Sx.op("dve", lambda e, j=j, bt=bt, h=h: e.tensor_scalar(
                            bt[:, j, :], crefb[:, :, h], ctok[:, j, h:h + 1], None, ALU.subtract),
                            reads=[b_crefb, b_ctok], writes=[b_bt])
                    pairs = [(i, j) for i in range(NT) for j in range(i + 1)]
                    pend = []
                    state = {}

                    def emit_qk(i, j, kt=kt, qt=qt, b_kt=b_kt, b_qt=b_qt):
                        sp_, b_sp = sbk.next()
                        Sx.op("pe", lambda e, sp_=sp_, i=i, j=j: e.matmul(
                            sp_[:, :128], kt[:, j * 128:(j + 1) * 128], qt[:, i * 128:(i + 1) * 128], start=True, stop=True),
                            reads=[b_kt, b_qt], writes=[b_sp])
                        return (i, j, sp_, b_sp)

                    def emit_rest(i, j, sp_, b_sp, bt=bt, b_bt=b_bt, vt=vt, b_vt=b_vt, h=h):
                        if j == 0:
                            state["ob"] = obk.next()
                        ob, b_ob = state["ob"]
                        p_, b_p = pT.next()
                        Sx.op("act", lambda e: e.activation(p_[:], sp_[:, :128], AF.Exp, bias=bt[:, j, i:i + 1], scale=1.0),
                              reads=[b_sp, b_bt], writes=[b_p])
                        if j == i:
                            Sx.op("dve", lambda e: e.tensor_tensor(p_[:], p_[:], mle[:], ALU.mult),
                                  reads=[b_p, b_mle], writes=[b_p])
                        Sx.op("pe", lambda e: e.matmul(ob[:, :129], p_[:], vt[:, j, :], start=(j == 0), stop=(j == i)),
                              reads=[b_p, b_vt], writes=[b_ob])
                        if j == i:
                            r_, b_r = rc.next()
                            o_, b_o = ot.next()
                            Sx.op("dve", lambda e: e.reciprocal(r_[:], ob[:, 128:129]), reads=[b_ob], writes=[b_r])
                            Sx.op("dve", lambda e: e.tensor_scalar(o_[:], ob[:, 0:128], r_[:, 0:1], None, ALU.mult),
                                  reads=[b_ob, b_r], writes=[b_o])
                            Sx.dma("sp", mix[i * 128:(i + 1) * 128, h * 128:(h + 1) * 128], o_[:], reads=[b_o], writes=[B["mix"]])
                    for (i, j) in pairs:
                        pend.append(emit_qk(i, j))
                        if len(pend) > 3:
                            emit_rest(*pend.pop(0))
                    while pend:
                        emit_rest(*pend.pop(0))
                Sx.barrier()

        def retention(l):
            C = 128
            NCH = S // C
            oq, ok_, ov, og = (OFF_RT - OFF_FV, OFF_RT - OFF_FV + 640, OFF_RT - OFF_FV + 1280, OFF_RT - OFF_FV + 1920)
            with ExitStack() as ls:
                def lsb(shape, dt, name):
                    _id[0] += 1
                    return ls.enter_context(nc.sbuf_tensor(f"{name}_{_id[0]}", list(shape), dt)), Buf(name)

                def cload(shape, src, name):
                    t, b = lsb(shape, F32, name)
                    Sx.dma("sp", t[:], src, writes=[b])
                    return t, b
                dec, b_dec = cload([128, 640], cst["ret_dec"], "rdec")
                qd, b_qd = cload([128, 640], cst["ret_qd"], "rqd")
                kd, b_kd = cload([128, 5], cst["ret_kd"], "rkd")
                cd, b_cd = cload([128, 5], cst["ret_cd"], "rcd")
                gw, b_gw = cload([128, 640], prm["ret_gn_w"][l].broadcast_to([128, 640]), "rgw")
                gb, b_gb = cload([128, 640], prm["ret_gn_b"][l].broadcast_to([128, 640]), "rgb")
                R, b_R = lsb([128, TH, 128], F32, "R")
                Sx.op("dve", lambda e: e.memset(R[:], 0.0), writes=[b_R])
                hin = RR([lsb([128, 2560], F32, "rh") for _ in range(2)])
                cs = RR([lsb([128, 128], F32, "rcs") for _ in range(2)])
                qr = RR([lsb([128, 640], F32, "rqr") for _ in range(2)])
                kr = RR([lsb([128, 640], F32, "rkr") for _ in range(2)])
                ks = RR([lsb([128, 640], F32, "rks") for _ in range(2)])
                tmp = RR([lsb([128, 640], F32, "rtmp") for _ in range(2)])
                qT_ = RR([lsb([128, 640], F32, "rqT") for _ in range(2)])
                qTd = RR([lsb([128, 640], F32, "rqTd") for _ in range(2)])
                kT_ = RR([lsb([128, 640], F32, "rkT") for _ in range(2)])
                inn = RR([lsb([128, 640], F32, "rinn") for _ in range(2)])
                o_ = RR([lsb([128, 640], F32, "ro") for _ in range(2)])
                st_ = RR([lsb([128, 4, TH], F32, "rst") for _ in range(2)])
                gs = RR([lsb([128, 640], F32, "rgs") for _ in range(2)])
                P0, P1, P2, P3 = PS[0], PS[1], PS[2], PS[3]

                def v3(ap):
                    return ap.rearrange("p (h c) -> p h c", c=128)

                def pv(Pt):
                    return Pt[0]

                for ci in range(NCH):
                    r = slice(ci * C, (ci + 1) * C)
                    ht, b_ht = hin.next()
                    Sx.dma("sp", ht[:], hproj[r, oq:oq + 2560], reads=[B["hproj"]], writes=[b_ht])
                    ct, b_ct = cs.next()
                    Sx.dma("sp", ct[:, 0:64], cst["ret_cos"][r, :], writes=[b_ct])
                    Sx.dma("sp", ct[:, 64:128], cst["ret_sin"][r, :], writes=[b_ct])
                    q_, b_q = qr.next()
                    k_, b_k = kr.next()
                    t_, b_t = tmp.next()
                    cosb = ct[:, 0:64].unsqueeze(1).broadcast_to([128, TH, 64])
                    sinb = ct[:, 64:128].unsqueeze(1).broadcast_to([128, TH, 64])
                    for (src_off, dst, b_dst) in ((0, q_, b_q), (640, k_, b_k)):
                        s3 = v3(ht[:, src_off:src_off + 640])
                        d3 = v3(dst[:])
                        t3 = v3(t_[:])
                        x1_, x2_ = s3[:, :, 0:64], s3[:, :, 64:128]
                        Sx.op("dve", lambda e, d3=d3, x1_=x1_, cosb=cosb: e.tensor_tensor(d3[:, :, 0:64], x1_, cosb, ALU.mult), reads=[b_ht, b_ct], writes=[b_dst])
                        Sx.op("dve", lambda e, t3=t3, x2_=x2_, sinb=sinb: e.tensor_tensor(t3[:, :, 0:64], x2_, sinb, ALU.mult), reads=[b_ht, b_ct], writes=[b_t])
                        Sx.op("dve", lambda e, d3=d3, t3=t3: e.tensor_tensor(d3[:, :, 0:64], d3[:, :, 0:64], t3[:, :, 0:64], ALU.subtract), reads=[b_dst, b_t], writes=[b_dst])
                        Sx.op("dve", lambda e, d3=d3, x1_=x1_, sinb=sinb: e.tensor_tensor(d3[:, :, 64:128], x1_, sinb, ALU.mult), reads=[b_ht, b_ct], writes=[b_dst])
                        Sx.op("dve", lambda e, t3=t3, x2_=x2_, cosb=cosb: e.tensor_tensor(t3[:, :, 64:128], x2_, cosb, ALU.mult), reads=[b_ht, b_ct], writes=[b_t])
                        Sx.op("dve", lambda e, d3=d3, t3=t3: e.tensor_tensor(d3[:, :, 64:128], d3[:, :, 64:128], t3[:, :, 64:128], ALU.add), reads=[b_dst, b_t], writes=[b_dst])
                    ks_, b_ks = ks.next()
                    Sx.op("dve", lambda e, ks_=ks_, k_=k_: e.tensor_tensor(v3(ks_[:]), v3(k_[:]), kd[:].unsqueeze(2).broadcast_to([128, TH, 128]), ALU.mult),
                          reads=[b_k, b_kd], writes=[b_ks])
                    qTt, b_qT = qT_.next()
                    kTt, b_kT = kT_.next()
                    qTdt, b_qTd = qTd.next()
                    for (src, b_src, dstT, b_dstT, Pt, scale) in ((q_, b_q, qTt, b_qT, P0, 1.0), (k_, b_k, kTt, b_kT, P1, 128 ** -0.5)):
                        for hh in range(TH):
                            a, c5 = (0, hh) if hh < 4 else (1, 0)
                            Sx.op("pe", lambda e, Pt=Pt, a=a, c5=c5, src=src, hh=hh: e.transpose(
                                Pt[0][:, a, c5 * 128:(c5 + 1) * 128], src[:, hh * 128:(hh + 1) * 128], ident[:]),
                                reads=[b_src, b_ident], writes=[Pt[1][a]])
                        Sx.op("act", lambda e, Pt=Pt, dstT=dstT, scale=scale: e.activation(dstT[:, 0:512], Pt[0][:, 0, :], AF.Copy, scale=scale),
                              reads=[Pt[1][0]], writes=[b_dstT])
                        Sx.op("act", lambda e, Pt=Pt, dstT=dstT, scale=scale: e.activation(dstT[:, 512:640], Pt[0][:, 1, 0:128], AF.Copy, scale=scale),
                              reads=[Pt[1][1]], writes=[b_dstT])
                    Sx.op("dve", lambda e, qTdt=qTdt, qTt=qTt: e.tensor_tensor(qTdt[:], qTt[:], qd[:], ALU.mult),
                          reads=[b_qT, b_qd], writes=[b_qTd])
                    for hh in range(TH):
                        a, c5 = (0, hh) if hh < 4 else (1, 0)
                        Sx.op("pe", lambda e, a=a, c5=c5, hh=hh, kTt=kTt, qTt=qTt: e.matmul(
                            P2[0][:, a, c5 * 128:(c5 + 1) * 128], kTt[:, hh * 128:(hh + 1) * 128], qTt[:, hh * 128:(hh + 1) * 128],
                            start=True, stop=True), reads=[b_kT, b_qT], writes=[P2[1][a]])
                    in_, b_in = inn.next()
                    Sx.op("dve", lambda e, in_=in_: e.tensor_tensor(in_[:, 0:512], P2[0][:, 0, :], dec[:, 0:512], ALU.mult),
                          reads=[P2[1][0], b_dec], writes=[b_in])
                    Sx.op("dve", lambda e, in_=in_: e.tensor_tensor(in_[:, 512:640], P2[0][:, 1, 0:128], dec[:, 512:640], ALU.mult),
                          reads=[P2[1][1], b_dec], writes=[b_in])
                    for hh in range(TH):
                        a, c5 = (0, hh) if hh < 4 else (1, 0)
                        Sx.op("pe", lambda e, a=a, c5=c5, hh=hh, in_=in_, ht=ht: e.matmul(
                            P3[0][:, a, c5 * 128:(c5 + 1) * 128], in_[:, hh * 128:(hh + 1) * 128], ht[:, 1280 + hh * 128:1280 + (hh + 1) * 128],
                            start=True, stop=False), reads=[b_in, b_ht], writes=[P3[1][a]])
                        Sx.op("pe", lambda e, a=a, c5=c5, hh=hh, qTdt=qTdt: e.matmul(
                            P3[0][:, a, c5 * 128:(c5 + 1) * 128], qTdt[:, hh * 128:(hh + 1) * 128], R[:, hh, :],
                            start=False, stop=True), reads=[b_qTd, b_R], writes=[P3[1][a]])
                    ot, b_ot = o_.next()
                    Sx.op("act", lambda e, ot=ot: e.copy(ot[:, 0:512], P3[0][:, 0, :]), reads=[P3[1][0]], writes=[b_ot])
                    Sx.op("act", lambda e, ot=ot: e.copy(ot[:, 512:640], P3[0][:, 1, 0:128]), reads=[P3[1][1]], writes=[b_ot])
                    for hh in range(TH):
                        a, c5 = (0, hh) if hh < 4 else (1, 0)
                        Sx.op("pe", lambda e, a=a, c5=c5, hh=hh, ks_=ks_, ht=ht: e.matmul(
                            P0[0][:, a, c5 * 128:(c5 + 1) * 128], ks_[:, hh * 128:(hh + 1) * 128], ht[:, 1280 + hh * 128:1280 + (hh + 1) * 128],
                            start=True, stop=True), reads=[b_ks, b_ht, b_R], writes=[P0[1][a]])
                    Sx.op("dve", lambda e: e.tensor_tensor(R[:], R[:], cd[:].unsqueeze(2).broadcast_to([128, TH, 128]), ALU.mult),
                          reads=[b_R, b_cd], writes=[b_R])
                    Sx.op("dve", lambda e: e.tensor_tensor(R[:, 0:4, :], R[:, 0:4, :], P0[0][:, 0, :].rearrange("p (h c) -> p h c", c=128), ALU.add),
                          reads=[b_R, P0[1][0]], writes=[b_R])
                    Sx.op("dve", lambda e: e.tensor_tensor(R[:, 4, :], R[:, 4, :], P0[0][:, 1, 0:128], ALU.add),
                          reads=[b_R, P0[1][1]], writes=[b_R])
                    s4, b_s4 = st_.next()
                    o3 = v3(ot[:])
                    t3 = v3(t_[:])
                    Sx.op("dve", lambda e, s4=s4, o3=o3: e.reduce_sum(s4[:, 0, :], o3, AX.X), reads=[b_ot], writes=[b_s4])
                    Sx.op("dve", lambda e, s4=s4: e.tensor_scalar(s4[:, 0, :], s4[:, 0, :], 1.0 / 128, None, ALU.mult), reads=[b_s4], writes=[b_s4])
                    Sx.op("dve", lambda e, s4=s4, o3=o3: e.tensor_tensor(o3, o3, s4[:, 0, :].unsqueeze(2).broadcast_to([128, TH, 128]), ALU.subtract),
                          reads=[b_ot, b_s4], writes=[b_ot])
                    Sx.op("dve", lambda e, o3=o3, t3=t3: e.tensor_tensor(t3, o3, o3, ALU.mult), reads=[b_ot], writes=[b_t])
                    Sx.op("dve", lambda e, s4=s4, t3=t3: e.reduce_sum(s4[:, 1, :], t3, AX.X), reads=[b_t], writes=[b_s4])
                    Sx.op("dve", lambda e, s4=s4: e.tensor_scalar(s4[:, 1, :], s4[:, 1, :], 1.0 / 128, 1e-5, ALU.mult, ALU.add), reads=[b_s4], writes=[b_s4])
                    Sx.op("act", lambda e, s4=s4: e.activation(s4[:, 2, :], s4[:, 1, :], AF.Sqrt), reads=[b_s4], writes=[b_s4])
                    Sx.op("dve", lambda e, s4=s4: e.reciprocal(s4[:, 3, :], s4[:, 2, :]), reads=[b_s4], writes=[b_s4])
                    Sx.op("dve", lambda e, s4=s4, o3=o3: e.tensor_tensor(o3, o3, s4[:, 3, :].unsqueeze(2).broadcast_to([128, TH, 128]), ALU.mult),
                          reads=[b_ot, b_s4], writes=[b_ot])
                    Sx.op("dve", lambda e, ot=ot: e.tensor_tensor(ot[:], ot[:], gw[:], ALU.mult), reads=[b_ot, b_gw], writes=[b_ot])
                    Sx.op("dve", lambda e, ot=ot: e.tensor_tensor(ot[:], ot[:], gb[:], ALU.add), reads=[b_ot, b_gb], writes=[b_ot])
                    g_, b_g = gs.next()
                    Sx.op("act", lambda e, g_=g_, ht=ht: e.activation(g_[:], ht[:, 1920:2560], AF.Silu), reads=[b_ht], writes=[b_g])
                    Sx.op("dve", lambda e, ot=ot, g_=g_: e.tensor_tensor(ot[:], ot[:], g_[:], ALU.mult), reads=[b_ot, b_g], writes=[b_ot])
                    Sx.dma("sp", mix[r, FOX_W + RWKV_W:D], ot[:], reads=[b_ot], writes=[B["mix"]])
                Sx.barrier()

        def mix_transpose():
            with ExitStack() as ls:
                def lsb(shape, dt, name):
                    _id[0] += 1
                    return ls.enter_context(nc.sbuf_tensor(f"{name}_{_id[0]}", list(shape), dt)), Buf(name)
                xo = RR([lsb([128, D], F32, "mx") for _ in range(4)])
                trb = RR(banks[0:8])
                stg = RR([lsb([128, 4, 128], BF16, "mstg") for _ in range(6)])
                for ti in range(NT):
                    xt, b_xt = xo.next()
                    Sx.dma("sp", xt[:], mix[ti * 128:(ti + 1) * 128, :], reads=[B["mix"]], writes=[b_xt])
                    transpose_store(xt, b_xt, D, mixT, B["mixT"], 0, ti * 128, 128, trb, stg)
                Sx.barrier()

        def evac_to(dst, b_dst):
            def mk(ls_stf):
                def evac(ti, c0, cw, pb, b_pb):
                    sg, b_sg = ls_stf.next()
                    Sx.op("act", lambda e: e.copy(sg[:, :cw], pb[:, :cw]), reads=[b_pb], writes=[b_sg])
                    Sx.dma("sp", dst[ti * 128:(ti + 1) * 128, c0:c0 + cw], sg[:, :cw], reads=[b_sg], writes=[b_dst])
                return evac
            return mk

        def out_proj(l):
            with ExitStack() as ls:
                def lsb(shape, dt, name):
                    _id[0] += 1
                    return ls.enter_context(nc.sbuf_tensor(f"{name}_{_id[0]}", list(shape), dt)), Buf(name)
                stf = RR([lsb([128, 512], F32, "ostf") for _ in range(3)])
                gemm_tok(mixT, B["mixT"], D, prm["w_out"][l], 0, D, evac_to(ytmp, B["ytmp"])(stf))

        def ffn(l):
            with ExitStack() as ls:
                def lsb(shape, dt, name):
                    _id[0] += 1
                    return ls.enter_context(nc.sbuf_tensor(f"{name}_{_id[0]}", list(shape), dt)), Buf(name)
                stg = RR([lsb([128, 512], BF16, "fstg") for _ in range(3)])
                stf = RR([lsb([128, 512], F32, "fstf") for _ in range(3)])

                def evac_up(r0, rw, t0, tw, pb, b_pb):
                    sf, b_sf = stf.next()
                    sg, b_sg = stg.next()
                    Sx.op("act", lambda e: e.activation(sf[:rw, :tw], pb[:rw, :tw], AF.Relu), reads=[b_pb], writes=[b_sf])
                    Sx.op("dve", lambda e: e.tensor_tensor(sg[:rw, :tw], sf[:rw, :tw], sf[:rw, :tw], ALU.mult), reads=[b_sf], writes=[b_sg])
                    Sx.dma("sp", hidT[r0:r0 + rw, t0:t0 + tw], sg[:rw, :tw], reads=[b_sg], writes=[B["hidT"]])
                gemm_feat(x1T, B["x1T"], D, prm["w_up"][l], 0, DFF, evac_up)
                gemm_tok(hidT, B["hidT"], DFF, prm["w_down"][l], 0, D, evac_to(ytmp, B["ytmp"])(stf))

        rwkv = make_rwkv(nc, Sx, S, prm, cst, hproj, mix, B, PS, ident, b_ident, _id)

        cur = x_in
        b_cur = Buf("xin")
        input_transpose(cur, b_cur)
        for l in range(L):
            if "in_proj" not in skip:
                in_proj(l)
            if "fox" not in skip:
                fox(l)
            if "rwkv" not in skip:
                rwkv(l)
            if "ret" not in skip:
                retention(l)
            if "mixT" not in skip:
                mix_transpose()
            if "out_proj" not in skip:
                out_proj(l)
            if "ln" not in skip:
                ln_pass(cur, b_cur, ytmp, B["ytmp"], prm["ln1_g"][l], prm["ln1_b"][l], x1, B["x1"], x1T, B["x1T"])
            if "ffn" not in skip:
                ffn(l)
            last = (l == L - 1)
            nxt = y_out if last else xres[l % 2]
            b_nxt = B["y"] if last else B["xresA" if l % 2 == 0 else "xresB"]
            ln_pass(x1, B["x1"], ytmp, B["ytmp"], prm["ln2_g"][l], prm["ln2_b"][l], nxt, b_nxt,
                    None if last else xT, B["xT"])
            cur, b_cur = nxt, b_nxt
        Sx.barrier()
        if dbg:
            srcs = {"hproj": hproj, "mix": mix, "x1": x1, "fT": fT}
            for name, ap in dbg_out.items():
                Sx.dma("sp", ap, srcs[name])
            Sx.barrier()
        Sx.replay()
        print("instructions:", Sx.ninstr, {e: Sx.cnt[e] for e in Sx.cnt})
    return nc


def make_rwkv(nc, Sx, S, prm, cst, hproj, mix, B, PS, ident, b_ident, _id):
    C = 64
    NCH = S // C
    off = OFF_RW - OFF_FV

    def rwkv(l):
        with ExitStack() as ls:
            def lsb(shape, dt, name):
                _id[0] += 1
                return ls.enter_context(nc.sbuf_tensor(f"{name}_{_id[0]}", list(shape), dt)), Buf(name)

            def cload(shape, src, name):
                t, b = lsb(shape, F32, name)
                Sx.dma("sp", t[:], src, writes=[b])
                return t, b
            bc = lambda name, n: prm[name][l].broadcast_to([C, n])
            mu_b, b_mu = cload([C, RWKV_COLS], bc("rwkv_mu", RWKV_COLS), "wmu")
            w0_b, b_w0 = cload([C, 640], bc("rwkv_w0", 640), "ww0")
            a0_b, b_a0 = cload([C, 640], bc("rwkv_a0", 640), "wa0")
            kk_b, b_kkb = cload([C, 640], bc("rwkv_k_k", 640), "wkk")
            ka_b, b_ka = cload([C, 640], bc("rwkv_k_a", 640), "wka")
            rk_b, b_rk = cload([C, 640], bc("rwkv_r_k", 640), "wrk")
            gw_b, b_gw = cload([C, 640], bc("rwkv_gn_w", 640), "wgw")
            gb_b, b_gb = cload([C, 640], bc("rwkv_gn_b", 640), "wgb")
            wup, b_wup = cload([64, 640], prm["rwkv_w_up"][l], "wup")
            aup, b_aup = cload([64, 640], prm["rwkv_a_up"][l], "aup")
            gup, b_gup = cload([128, 640], prm["rwkv_g_up"][l], "gup")
            m_lt, b_mlt = cload([64, 640], cst["m64_lt"], "mlt")
            m_le, b_mle = cload([64, 640], cst["m64_le"], "mle")
            m_gt, b_mgt = cload([64, 640], cst["m64_gt"], "mgt")
            i64, b_i64 = cload([64, 640], cst["i64"], "i64")
            H, b_H = lsb([64, 640], F32, "H")
            Sx.op("dve", lambda e: e.tensor_scalar(H[:].bitcast(F32R), i64[:], 0.0, None, ALU.mult), reads=[b_i64], writes=[b_H])
            _pool = {}

            N2 = {"g", "rtT", "E1T", "ArbT", "ArkT", "W1T", "W2"}

            def T(name, shape=(64, 640), n=1):
                if name in N2:
                    n = 2
                if name not in _pool:
                    _pool[name] = RR([lsb(list(shape), F32, name) for _ in range(n)])
                return _pool[name].next()
            prr = RR(PS)

            def s2(t):
                return t[:, :].rearrange("p (a c) -> p a c", a=2)

            def p2(P):
                return P[0][:64, :, 0:320]

            def ph(P, h):
                return P[0][:64, h // 5, (h % 5) * 64:(h % 5 + 1) * 64]

            def hs_(t, h):
                return t[:, h * 64:(h + 1) * 64]

            def v10(t):
                return t[:, :].rearrange("p (h c) -> p h c", c=64)

            def mm10(lhs_fn, rhs_fn, reads, acc=None):
                P = prr.next()
                for h in range(10):
                    ls_ = lhs_fn(h)
                    rs_ = rhs_fn(h)
                    if not isinstance(ls_, (list, tuple)):
                        ls_, rs_ = [ls_], [rs_]
                    n = len(ls_)
                    for i in range(n):
                        Sx.op("pe", lambda e, P=P, h=h, a=ls_[i], b=rs_[i], i=i, n=n: e.matmul(
                            ph(P, h), a.bitcast(F32R), b.bitcast(F32R), start=(i == 0), stop=(i == n - 1)), reads=reads, writes=[P[1][h // 5]])
                return P

            def evac(P, name, eng="act", mask=None, b_mask=None, add=None, b_add=None, r32=True):
                t, b_t = T(name)
                o = s2(t).bitcast(F32R) if r32 else s2(t)
                if mask is not None:
                    Sx.op("dve", lambda e: e.tensor_tensor(o, p2(P), s2(mask), ALU.mult), reads=[P[1][0], P[1][1], b_mask], writes=[b_t])
                elif add is not None:
                    Sx.op("dve", lambda e: e.tensor_tensor(o, p2(P), s2(add), ALU.add), reads=[P[1][0], P[1][1], b_add], writes=[b_t])
                elif eng == "act":
                    Sx.op("act", lambda e: e.copy(o, p2(P)), reads=[P[1][0], P[1][1]], writes=[b_t])
                else:
                    Sx.op("dve", lambda e: e.tensor_copy(o, p2(P)), reads=[P[1][0], P[1][1]], writes=[b_t])
                return t, b_t

            def chunk(ci):
                c0 = ci * C
                h_, b_h = T("h", (C, RWKV_COLS), n=2)
                hp, b_hp = T("hp", (C, RWKV_COLS))
                Sx.dma("sp", h_[:], hproj[c0:c0 + C, off:off + RWKV_COLS], reads=[B["hproj"]], writes=[b_h])
                if ci == 0:
                    Sx.op("dve", lambda e, hp=hp: e.memset(hp[:], 0.0), writes=[b_hp])
                    Sx.dma("sp", hp[1:C, :], hproj[0:C - 1, off:off + RWKV_COLS], reads=[B["hproj"]], writes=[b_hp])
                else:
                    Sx.dma("sp", hp[:], hproj[c0 - 1:c0 + C - 1, off:off + RWKV_COLS], reads=[B["hproj"]], writes=[b_hp])
                Sx.op("dve", lambda e, hp=hp, h_=h_: e.tensor_tensor(hp[:], hp[:], h_[:], ALU.subtract), reads=[b_hp, b_h], writes=[b_hp])
                Sx.op("dve", lambda e, hp=hp: e.tensor_tensor(hp[:], hp[:], mu_b[:], ALU.mult), reads=[b_hp, b_mu], writes=[b_hp])
                Sx.op("dve", lambda e, hp=hp, h_=h_: e.tensor_tensor(h_[:].bitcast(F32R), h_[:], hp[:], ALU.add), reads=[b_hp, b_h], writes=[b_h])
                r_ = h_[:, 0:640]
                yield
                k_ = h_[:, 640:1280]
                v_ = h_[:, 1280:1920]
                vh = lambda h, h_=h_: h_[:, 1280 + h * 64:1280 + (h + 1) * 64]
                lor, b_lor = T("lor", (C, 256))
                Sx.op("act", lambda e, lor=lor, h_=h_: e.activation(lor[:, 0:64], h_[:, 1920:1984], AF.Tanh), reads=[b_h], writes=[b_lor])
                Sx.op("act", lambda e, lor=lor, h_=h_: e.activation(lor[:, 128:256], h_[:, 2048:2176], AF.Sigmoid), reads=[b_h], writes=[b_lor])
                Pl = prr.next()
                Sx.op("pe", lambda e, Pl=Pl, lor=lor: e.transpose(Pl[0][:64, 0, 0:64], lor[:, 0:64], ident[:C, :C]), reads=[b_lor, b_ident], writes=[Pl[1][0]])
                Sx.op("pe", lambda e, Pl=Pl, h_=h_: e.transpose(Pl[0][:64, 0, 64:128], h_[:, 1984:2048], ident[:C, :C]), reads=[b_h, b_ident], writes=[Pl[1][0]])
                Sx.op("pe", lambda e, Pl=Pl, lor=lor: e.transpose(Pl[0][:, 0, 128:192], lor[:, 128:256], ident[:C, :C]), reads=[b_lor, b_ident], writes=[Pl[1][0]])
                lT, b_lT = T("lT", (128, 192))
                Sx.op("act", lambda e, lT=lT, Pl=Pl: e.copy(lT[:64, 0:128], Pl[0][:64, 0, 0:128]), reads=[Pl[1][0]], writes=[b_lT])
                Sx.op("act", lambda e, lT=lT, Pl=Pl: e.copy(lT[:, 128:192], Pl[0][:, 0, 128:192]), reads=[Pl[1][0]], writes=[b_lT])

                def lora_mm(lhsT, rhsW, b_w):
                    P = prr.next()
                    for a in range(2):
                        Sx.op("pe", lambda e, P=P, a=a: e.matmul(P[0][:64, a, 0:320], lhsT, rhsW[:, a * 320:(a + 1) * 320], start=True, stop=True),
                              reads=[b_lT, b_w], writes=[P[1][a]])
                    return P
                Pz = lora_mm(lT[:64, 0:64], wup, b_wup)
                sg, b_sg = evac(Pz, "sg", add=w0_b, b_add=b_w0, r32=False)
                yield
                Sx.op("act", lambda e, sg=sg: e.activation(sg[:], sg[:], AF.Sigmoid), reads=[b_sg], writes=[b_sg])
                Pa = lora_mm(lT[:64, 64:128], aup, b_aup)
                ai, b_ai = evac(Pa, "ai", add=a0_b, b_add=b_a0, r32=False)
                yield
                Sx.op("act", lambda e, ai=ai: e.activation(ai[:], ai[:], AF.Sigmoid), reads=[b_ai], writes=[b_ai])
                Pg = lora_mm(lT[:, 128:192], gup, b_gup)
                g_, b_g = evac(Pg, "g", r32=False)
                yield
                kk, b_kk = T("kk")
                sq, b_sq = T("sq")
                st, b_st = T("st", (C, 40))
                Sx.op("dve", lambda e, kk=kk, k_=k_: e.tensor_tensor(kk[:], k_, kk_b[:], ALU.mult), reads=[b_h, b_kkb], writes=[b_kk])
                Sx.op("dve", lambda e, kk=kk, sq=sq: e.tensor_tensor(sq[:], kk[:], kk[:], ALU.mult), reads=[b_kk], writes=[b_sq])
                Sx.op("dve", lambda e, st=st, sq=sq: e.reduce_sum(st[:, 0:10], v10(sq), AX.X), reads=[b_sq], writes=[b_st])
                Sx.op("act", lambda e, st=st: e.activation(st[:, 10:20], st[:, 0:10], AF.Sqrt), reads=[b_st], writes=[b_st])
                Sx.op("dve", lambda e, st=st: e.tensor_scalar(st[:, 10:20], st[:, 10:20], 1e-12, None, ALU.max), reads=[b_st], writes=[b_st])
                Sx.op("dve", lambda e, st=st: e.reciprocal(st[:, 20:30], st[:, 10:20]), reads=[b_st], writes=[b_st])
                Sx.op("dve", lambda e, kk=kk, st=st: e.tensor_tensor(v10(kk), v10(kk), st[:, 20:30].unsqueeze(2).broadcast_to([C, 10, 64]), ALU.mult),
                      reads=[b_kk, b_st], writes=[b_kk])
                k2, b_k2 = T("k2", n=2)
                Sx.op("dve", lambda e, k2=k2, ai=ai: e.scalar_tensor_tensor(k2[:], ai[:], -1.0, ka_b[:], ALU.add, ALU.mult), reads=[b_ai, b_ka], writes=[b_k2])
                Sx.op("dve", lambda e, k2=k2, k_=k_: e.scalar_tensor_tensor(k2[:], k2[:], 1.0, k_, ALU.add, ALU.mult), reads=[b_k2, b_h], writes=[b_k2])
                bv, b_bv = T("bv")
                Sx.op("dve", lambda e, bv=bv, kk=kk, ai=ai: e.tensor_tensor(bv[:], kk[:], ai[:], ALU.mult), reads=[b_kk, b_ai], writes=[b_bv])
                yield
                Pc = prr.next()
                for a in range(2):
                    Sx.op("pe", lambda e, Pc=Pc, a=a, sg=sg: e.matmul(Pc[0][:64, a, 0:320], m_le[:, 0:64], sg[:, a * 320:(a + 1) * 320], start=True, stop=True),
                          reads=[b_mle, b_sg], writes=[Pc[1][a]])
                E1, b_E1 = T("E1")
                E2, b_E2 = T("E2")
                E3, b_E3 = T("E3")
                Sx.op("act", lambda e, E1=E1, Pc=Pc: e.activation(s2(E1), p2(Pc), AF.Exp, scale=-EXPM05), reads=[Pc[1][0], Pc[1][1]], writes=[b_E1])
                Sx.op("act", lambda e, E2=E2, Pc=Pc: e.activation(s2(E2), p2(Pc), AF.Exp, scale=EXPM05), reads=[Pc[1][0], Pc[1][1]], writes=[b_E2])
                Sx.op("dve", lambda e, E3=E3, Pc=Pc, sg=sg: e.tensor_tensor(s2(E3), p2(Pc), s2(sg), ALU.subtract), reads=[Pc[1][0], Pc[1][1], b_sg], writes=[b_E3])
                Sx.op("act", lambda e, E3=E3: e.activation(E3[:], E3[:], AF.Exp, scale=-EXPM05), reads=[b_E3], writes=[b_E3])
                yield
                rt, b_rt = T("rt")
                at, b_at = T("at", n=2)
                bt, b_bt = T("bt", n=2)
                kt, b_kt = T("kt", n=2)
                Sx.op("dve", lambda e, rt=rt, r_=r_, E1=E1: e.tensor_tensor(rt[:], r_, E1[:], ALU.mult), reads=[b_h, b_E1], writes=[b_rt])
                Sx.op("dve", lambda e, at=at, kk=kk, E3=E3: e.scalar_tensor_tensor(at[:].bitcast(F32R), kk[:], -1.0, E3[:], ALU.mult, ALU.mult), reads=[b_kk, b_E3], writes=[b_at])
                Sx.op("dve", lambda e, bt=bt, bv=bv, E2=E2: e.tensor_tensor(bt[:].bitcast(F32R), bv[:], E2[:], ALU.mult), reads=[b_bv, b_E2], writes=[b_bt])
                Sx.op("dve", lambda e, kt=kt, k2=k2, E2=E2: e.tensor_tensor(kt[:].bitcast(F32R), k2[:], E2[:], ALU.mult), reads=[b_k2, b_E2], writes=[b_kt])

                def tr10(src, b_src, name):
                    P = prr.next()
                    for h in range(10):
                        Sx.op("pe", lambda e, P=P, h=h: e.transpose(ph(P, h), hs_(src, h), ident[:C, :C]),
                              reads=[b_src, b_ident], writes=[P[1][h // 5]])
                    return evac(P, name)
                atT, b_atT = tr10(at, b_at, "atT")
                yield
                btT, b_btT = tr10(bt, b_bt, "btT")
                yield
                ktT, b_ktT = tr10(kt, b_kt, "ktT")
                yield
                rtT, b_rtT = tr10(rt, b_rt, "rtT")
                yield
                E1T, b_E1T = tr10(E1, b_E1, "E1T")
                yield
                def score(lt, b_l, rt_, b_r, mask, b_mask, name):
                    P = mm10(lambda h: hs_(lt, h), lambda h: hs_(rt_, h), [b_l, b_r])
                    return evac(P, name, mask=mask, b_mask=b_mask)
                Mt, b_Mt = score(btT, b_btT, atT, b_atT, m_lt, b_mlt, "Nt")
                yield
                M, b_M = score(atT, b_atT, btT, b_btT, m_gt, b_mgt, "Nn")
                yield
                AakT, b_AakT = score(ktT, b_ktT, atT, b_atT, m_lt, b_mlt, "AakT")
                yield
                ArbT, b_ArbT = score(btT, b_btT, rtT, b_rtT, m_le, b_mle, "ArbT")
                yield
                ArkT, b_ArkT = score(ktT, b_ktT, rtT, b_rtT, m_le, b_mle, "ArkT")
                yield
                Tt, b_Tt = T("Tt")
                Sx.op("dve", lambda e, Tt=Tt, Mt=Mt: e.tensor_tensor(Tt[:].bitcast(F32R), Mt[:], i64[:], ALU.add), reads=[b_Mt, b_i64], writes=[b_Tt])
                for j in range(5):
                    P = mm10(lambda h: hs_(Mt, h), lambda h: hs_(M, h), [b_Mt, b_M])
                    M2, b_M2 = evac(P, "M2_%d" % (j % 2), eng="act")
                    yield
                    if j < 4:
                        P = mm10(lambda h: hs_(M, h), lambda h: hs_(Mt, h), [b_Mt, b_M])
                        Mt2, b_Mt2 = evac(P, "Mt2_%d" % (j % 2), eng="dve")
                    P = mm10(lambda h: hs_(M2, h), lambda h: hs_(Tt, h), [b_M2, b_Tt])
                    Sx.op("dve", lambda e, Tt=Tt, P=P: e.tensor_tensor(s2(Tt).bitcast(F32R), p2(P), s2(Tt), ALU.add), reads=[P[1][0], P[1][1], b_Tt], writes=[b_Tt])
                    yield
                    M, b_M = M2, b_M2
                    if j < 4:
                        Mt, b_Mt = Mt2, b_Mt2
                P = mm10(lambda h: hs_(AakT, h), vh, [b_AakT, b_h])
                AakV, b_AakV = evac(P, "AakV")
                yield
                P = mm10(lambda h: hs_(Tt, h), lambda h: hs_(AakV, h), [b_Tt, b_AakV])
                W2, b_W2 = evac(P, "W2", eng="dve", r32=False)
                yield
                P = mm10(lambda h: hs_(at, h), lambda h: hs_(Tt, h), [b_at, b_Tt])
                W1T, b_W1T = evac(P, "W1T")
                yield
                yield "TAIL"
                sq, b_sq = T("sq_t")
                st, b_st = T("st_t", (C, 40))
                P = mm10(lambda h: hs_(W1T, h), lambda h: hs_(H, h), [b_W1T, b_H])
                U, b_U = evac(P, "U", add=W2, b_add=b_W2)
                yield
                Py = mm10(lambda h: [hs_(rtT, h), hs_(ArbT, h), hs_(ArkT, h)], lambda h: [hs_(H, h), hs_(U, h), vh(h)],
                          [b_rtT, b_ArbT, b_ArkT, b_H, b_U, b_h])
                y_, b_y = evac(Py, "y", r32=False)
                yield
                Ph = mm10(lambda h: [hs_(bt, h), hs_(kt, h)], lambda h: [hs_(U, h), vh(h)], [b_bt, b_kt, b_U, b_h, b_H])
                Sx.op("dve", lambda e, Ph=Ph: e.tensor_tensor(s2(H).bitcast(F32R), p2(Ph), s2(H), ALU.add), reads=[Ph[1][0], Ph[1][1], b_H], writes=[b_H])
                Sx.op("dve", lambda e, E1T=E1T: e.tensor_tensor(v10(H).bitcast(F32R), v10(H), v10(E1T)[:, :, 63:64].broadcast_to([64, 10, 64]), ALU.mult),
                      reads=[b_H, b_E1T], writes=[b_H])
                yield
                Sx.op("dve", lambda e, st=st, y_=y_: e.reduce_sum(st[:, 0:10], v10(y_), AX.X), reads=[b_y], writes=[b_st])
                Sx.op("dve", lambda e, st=st: e.tensor_scalar(st[:, 0:10], st[:, 0:10], 1.0 / 64, None, ALU.mult), reads=[b_st], writes=[b_st])
                Sx.op("dve", lambda e, st=st, y_=y_: e.tensor_tensor(v10(y_), v10(y_), st[:, 0:10].unsqueeze(2).broadcast_to([C, 10, 64]), ALU.subtract),
                      reads=[b_y, b_st], writes=[b_y])
                Sx.op("dve", lambda e, sq=sq, y_=y_: e.tensor_tensor(sq[:], y_[:], y_[:], ALU.mult), reads=[b_y], writes=[b_sq])
                Sx.op("dve", lambda e, st=st, sq=sq: e.reduce_sum(st[:, 10:20], v10(sq), AX.X), reads=[b_sq], writes=[b_st])
                Sx.op("dve", lambda e, st=st: e.tensor_scalar(st[:, 10:20], st[:, 10:20], 1.0 / 64, 64e-5, ALU.mult, ALU.add), reads=[b_st], writes=[b_st])
                Sx.op("act", lambda e, st=st: e.activation(st[:, 20:30], st[:, 10:20], AF.Sqrt), reads=[b_st], writes=[b_st])
                Sx.op("dve", lambda e, st=st: e.reciprocal(st[:, 30:40], st[:, 20:30]), reads=[b_st], writes=[b_st])
                yield
                Sx.op("dve", lambda e, st=st, y_=y_: e.tensor_tensor(v10(y_), v10(y_), st[:, 30:40].unsqueeze(2).broadcast_to([C, 10, 64]), ALU.mult),
                      reads=[b_y, b_st], writes=[b_y])
                Sx.op("dve", lambda e, y_=y_: e.tensor_tensor(y_[:], y_[:], gw_b[:], ALU.mult), reads=[b_y, b_gw], writes=[b_y])
                Sx.op("dve", lambda e, y_=y_: e.tensor_tensor(y_[:], y_[:], gb_b[:], ALU.add), reads=[b_y, b_gb], writes=[b_y])
                yield
                Sx.op("dve", lambda e, sq=sq, r_=r_, k2=k2: e.tensor_tensor(sq[:], r_, k2[:], ALU.mult), reads=[b_h, b_k2], writes=[b_sq])
                Sx.op("dve", lambda e, sq=sq: e.tensor_tensor(sq[:], sq[:], rk_b[:], ALU.mult), reads=[b_sq, b_rk], writes=[b_sq])
                Sx.op("dve", lambda e, st=st, sq=sq: e.reduce_sum(st[:, 0:10], v10(sq), AX.X), reads=[b_sq], writes=[b_st])
                Sx.op("dve", lambda e, sq=sq, st=st, v_=v_: e.tensor_tensor(v10(sq), v_.rearrange("p (h c) -> p h c", c=64), st[:, 0:10].unsqueeze(2).broadcast_to([C, 10, 64]), ALU.mult),
                      reads=[b_h, b_st], writes=[b_sq])
                Sx.op("dve", lambda e, y_=y_, sq=sq: e.tensor_tensor(y_[:], y_[:], sq[:], ALU.add), reads=[b_y, b_sq], writes=[b_y])
                Sx.op("dve", lambda e, y_=y_, g_=g_: e.tensor_tensor(y_[:], y_[:], g_[:], ALU.mult), reads=[b_y, b_g], writes=[b_y])
                Sx.dma("sp", mix[c0:c0 + C, FOX_W:FOX_W + RWKV_W], y_[:], reads=[b_y], writes=[B["mix"]])

            def adv(g):
                try:
                    return next(g)
                except StopIteration:
                    return "END"
            cur_g = chunk(0)
            while adv(cur_g) != "TAIL":
                pass
            for ci in range(NCH):
                nxt = chunk(ci + 1) if ci + 1 < NCH else None
                nxt_done = nxt is None
                tail_done = False
                while not (tail_done and nxt_done):
                    for _ in range(4):
                        if not nxt_done and adv(nxt) == "TAIL":
                            nxt_done = True
                    if not tail_done and adv(cur_g) == "END":
                        tail_done = True
                cur_g = nxt
            Sx.barrier()
    return rwkv


_CACHE = {}


def _prep_params(inputs, L):
    out = {}
    for k, shp in PARAM_SHAPES.items():
        a = np.asarray(inputs[k], dtype=np.float32)[:L]
        out[k] = np.ascontiguousarray(a.reshape([L] + shp))
    return out


def kernel(**inputs):
    x = np.asarray(inputs["x"], dtype=np.float32)
    Bsz, S, _ = x.shape
    L = np.asarray(inputs["w_in"]).shape[0]
    key = (S, L)
    if key not in _CACHE:
        _CACHE[key] = build_program(S, L)
    nc = _CACHE[key]
    shared = _prep_params(inputs, L)
    shared.update(host_consts())
    shared.update(ret_consts(S))
    n = 8
    in_maps = []
    for c in range(n):
        m = dict(shared)
        m["x"] = np.ascontiguousarray(x[c % Bsz])
        in_maps.append(m)
    res = run_bass_kernel_spmd(nc, in_maps, core_ids=list(range(n)))
    return np.stack([res.results[b]["y"] for b in range(Bsz)], axis=0).astype(np.float32)
```
